# Optimizing a Trainium2 kernel written in Bass

```python
import math
import jax, jax.numpy as jnp
from jax import lax
import numpy as np

D_MODEL = 2048
BATCH = 16
SEQ = 2048
DEPTH = 4

N_MIXERS = 2
N_HGRN_LAYERS = (DEPTH + 1) // 2
N_MAMBA_LAYERS = DEPTH // 2
NORM_EPS = 1e-5

HGRN_EXPAND = 128
HGRN_HEADS = D_MODEL // HGRN_EXPAND
HGRN_DK = HGRN_EXPAND
HGRN_DV = D_MODEL // HGRN_HEADS
HGRN_FDIM = HGRN_HEADS * HGRN_DK
HGRN_IN_DIM = 2 * HGRN_FDIM + 2 * HGRN_HEADS * HGRN_DV
HGRN_CHUNK = 64

M_EXPAND = 2
M_D_INNER = M_EXPAND * D_MODEL
M_HEADDIM = 64
M_HEADS = M_D_INNER // M_HEADDIM
M_GROUPS = 8
M_HPG = M_HEADS // M_GROUPS
M_D_STATE = 128
M_CONV = 4
M_CONV_DIM = M_D_INNER + 2 * M_GROUPS * M_D_STATE
M_IN_DIM = M_D_INNER + M_CONV_DIM + M_HEADS
M_CHUNK = 128

D_FF = 5632
FFN_CONV = 3

kernel_name = 'hybrid_hgrn2_mamba2_convffn'


def rms_norm(x, w):
    xf = x.astype(jnp.float32)
    y = xf * lax.rsqrt(jnp.mean(xf * xf, axis=-1, keepdims=True) + NORM_EPS)
    return (y * w.astype(jnp.float32)).astype(x.dtype)


def causal_dwconv(x, w, b):
    K, C = w.shape
    y = lax.conv_general_dilated(x, w[:, None, :].astype(x.dtype), window_strides=(1,),
                                 padding=[(K - 1, 0)], dimension_numbers=('NWC', 'WIO', 'NWC'),
                                 feature_group_count=C)
    return y + b.astype(x.dtype)


def masked_exp(mask, logits):
    return jnp.where(mask, jnp.exp(jnp.where(mask, logits, 0.0)), 0.0)


def hgrn2_mixer(u, w_in, lb, gn_w, w_out):
    Bsz, L, _ = u.shape
    n_chunks = L // HGRN_CHUNK
    f32 = jnp.float32
    q, f, v, g = jnp.split(u @ w_in, [HGRN_FDIM, 2 * HGRN_FDIM, 2 * HGRN_FDIM + HGRN_HEADS * HGRN_DV], axis=-1)
    q = jax.nn.silu(q.astype(f32))
    f = f.astype(f32)
    lb = lb.astype(f32)
    log_f = jnp.log(lb + (1.0 - lb) * jax.nn.sigmoid(f))
    k = (1.0 - lb) * jax.nn.sigmoid(-f)
    v = v.astype(f32)

    def to_chunks(t, d):
        return t.reshape(Bsz, n_chunks, HGRN_CHUNK, HGRN_HEADS, d).transpose(1, 0, 3, 2, 4)

    causal = jnp.tril(jnp.ones((HGRN_CHUNK, HGRN_CHUNK), bool))[..., None]

    def chunk_step(S, inp):
        qc, kc, vc, gc = inp
        b = jnp.cumsum(gc, axis=2)
        diff = b[:, :, :, None, :] - b[:, :, None, :, :]
        decay = masked_exp(causal, diff)
        A = jnp.einsum('bhik,bhjk,bhijk->bhij', qc, kc, decay)
        o = jnp.einsum('bhij,bhjv->bhiv', A, vc) + jnp.einsum('bhik,bhkv->bhiv', qc * jnp.exp(b), S)
        b_last = b[:, :, -1:, :]
        S = jnp.exp(b_last[:, :, 0, :, None]) * S + jnp.einsum('bhjk,bhjv->bhkv', kc * jnp.exp(b_last - b), vc)
        return S, o

    S0 = jnp.zeros((Bsz, HGRN_HEADS, HGRN_DK, HGRN_DV), f32)
    _, o = lax.scan(chunk_step, S0, (to_chunks(q, HGRN_DK), to_chunks(k, HGRN_DK),
                                     to_chunks(v, HGRN_DV), to_chunks(log_f, HGRN_DK)))
    o = o.transpose(1, 0, 3, 2, 4).reshape(Bsz, L, HGRN_HEADS, HGRN_DV)
    g = g.astype(f32).reshape(Bsz, L, HGRN_HEADS, HGRN_DV)
    o = o * lax.rsqrt(jnp.mean(o * o, axis=-1, keepdims=True) + NORM_EPS) * gn_w.astype(f32) * jax.nn.silu(g)
    return o.reshape(Bsz, L, HGRN_HEADS * HGRN_DV).astype(u.dtype) @ w_out


def mamba2_mixer(u, w_in, conv_w, conv_b, dt_bias, A_log, D_skip, norm_w, w_out):
    Bsz, L, _ = u.shape
    nc = L // M_CHUNK
    f32 = jnp.float32
    z, xBC, dt = jnp.split(u @ w_in, [M_D_INNER, M_D_INNER + M_CONV_DIM], axis=-1)
    xBC = jax.nn.silu(causal_dwconv(xBC, conv_w, conv_b)).astype(f32)
    xs, Bm, Cm = jnp.split(xBC, [M_D_INNER, M_D_INNER + M_GROUPS * M_D_STATE], axis=-1)
    dt = jax.nn.softplus(dt.astype(f32) + dt_bias.astype(f32))
    A = -jnp.exp(A_log.astype(f32))
    xh = xs.reshape(Bsz, L, M_HEADS, M_HEADDIM)
    X = (xh * dt[..., None]).reshape(Bsz, nc, M_CHUNK, M_GROUPS, M_HPG, M_HEADDIM)
    Ad = (dt * A).reshape(Bsz, nc, M_CHUNK, M_GROUPS, M_HPG).transpose(0, 3, 4, 1, 2)
    Bc = Bm.reshape(Bsz, nc, M_CHUNK, M_GROUPS, M_D_STATE)
    Cc = Cm.reshape(Bsz, nc, M_CHUNK, M_GROUPS, M_D_STATE)
    a_cs = jnp.cumsum(Ad, axis=-1)
    causal = jnp.tril(jnp.ones((M_CHUNK, M_CHUNK), bool))
    Lmat = masked_exp(causal, a_cs[..., :, None] - a_cs[..., None, :])
    CB = jnp.einsum('bclgn,bcsgn->bcgls', Cc, Bc)
    y_diag = jnp.einsum('bcgls,bgjcls,bcsgjp->bclgjp', CB, Lmat, X)
    decay_states = jnp.exp(a_cs[..., -1:] - a_cs)
    states = jnp.einsum('bcsgn,bgjcs,bcsgjp->cbgjpn', Bc, decay_states, X)
    chunk_decay = jnp.exp(a_cs[..., -1]).transpose(3, 0, 1, 2)

    def state_step(h, inp):
        st, dec = inp
        return dec[..., None, None] * h + st, h

    h0 = jnp.zeros((Bsz, M_GROUPS, M_HPG, M_HEADDIM, M_D_STATE), f32)
    _, h_in = lax.scan(state_step, h0, (states, chunk_decay))
    y_off = jnp.einsum('bclgn,cbgjpn,bgjcl->bclgjp', Cc, h_in, jnp.exp(a_cs))
    y = (y_diag + y_off).reshape(Bsz, L, M_HEADS, M_HEADDIM) + xh * D_skip.astype(f32)[:, None]
    y = y.reshape(Bsz, L, M_D_INNER) * jax.nn.silu(z.astype(f32))
    y = y.reshape(Bsz, L, M_GROUPS, M_D_INNER // M_GROUPS)
    y = y * lax.rsqrt(jnp.mean(y * y, axis=-1, keepdims=True) + NORM_EPS)
    y = y.reshape(Bsz, L, M_D_INNER) * norm_w.astype(f32)
    return y.astype(u.dtype) @ w_out


def conv_ffn(u, w_up, conv_w, conv_b, w_down):
    h = causal_dwconv(u @ w_up, conv_w, conv_b)
    g, up = jnp.split(h, 2, axis=-1)
    return (jax.nn.silu(g) * up) @ w_down


def setup_inputs(seed: int = 0) -> dict:
    key = jax.random.key(seed)
    ks = jax.random.split(key, 24)
    f32 = jnp.float32
    nh, nm = N_HGRN_LAYERS, N_MAMBA_LAYERS

    def dense(k, shape):
        return jax.random.normal(k, shape, f32) * shape[-2] ** -0.5

    def gain(k, shape):
        return 1.0 + 0.02 * jax.random.normal(k, shape, f32)

    def small(k, shape):
        return 0.02 * jax.random.normal(k, shape, f32)

    dt0 = jnp.exp(jax.random.uniform(ks[10], (nm, M_HEADS), f32, math.log(1e-3), math.log(1e-1)))
    return {
        'x': jax.random.normal(ks[0], (BATCH, SEQ, D_MODEL), f32),
        'mix_norm': gain(ks[1], (DEPTH, D_MODEL)),
        'ffn_norm': gain(ks[2], (DEPTH, D_MODEL)),
        'final_norm': gain(ks[3], (D_MODEL,)),
        'hgrn_w_in': dense(ks[4], (nh, D_MODEL, HGRN_IN_DIM)),
        'hgrn_lb_logits': 0.5 * jax.random.normal(ks[5], (nh, HGRN_FDIM), f32),
        'hgrn_gnorm': gain(ks[6], (nh, HGRN_DV)),
        'hgrn_w_out': dense(ks[7], (nh, HGRN_HEADS * HGRN_DV, D_MODEL)),
        'm_w_in': dense(ks[8], (nm, D_MODEL, M_IN_DIM)),
        'm_conv_w': dense(ks[9], (nm, M_CONV, M_CONV_DIM)),
        'm_conv_b': small(ks[11], (nm, M_CONV_DIM)),
        'm_dt_bias': dt0 + jnp.log(-jnp.expm1(-dt0)),
        'm_A_log': jnp.log(jax.random.uniform(ks[12], (nm, M_HEADS), f32, 1.0, 16.0)),
        'm_D': gain(ks[13], (nm, M_HEADS)),
        'm_norm': gain(ks[14], (nm, M_D_INNER)),
        'm_w_out': dense(ks[15], (nm, M_D_INNER, D_MODEL)),
        'f_w_up': dense(ks[16], (DEPTH, D_MODEL, 2 * D_FF)),
        'f_conv_w': dense(ks[17], (DEPTH, FFN_CONV, 2 * D_FF)),
        'f_conv_b': small(ks[18], (DEPTH, 2 * D_FF)),
        'f_w_down': dense(ks[19], (DEPTH, D_FF, D_MODEL)),
    }


def reference(x, mix_norm, ffn_norm, final_norm, hgrn_w_in, hgrn_lb_logits, hgrn_gnorm, hgrn_w_out,
              m_w_in, m_conv_w, m_conv_b, m_dt_bias, m_A_log, m_D, m_norm, m_w_out,
              f_w_up, f_conv_w, f_conv_b, f_w_down):
    lb_p = jax.nn.softmax(hgrn_lb_logits.astype(jnp.float32), axis=0)
    lower_bounds = jnp.cumsum(lb_p, axis=0) - lb_p[0]
    h = x
    for i in range(DEPTH):
        u = rms_norm(h, mix_norm[i])
        j = i // N_MIXERS
        if i % N_MIXERS == 0:
            h = h + hgrn2_mixer(u, hgrn_w_in[j], lower_bounds[j], hgrn_gnorm[j], hgrn_w_out[j])
        else:
            h = h + mamba2_mixer(u, m_w_in[j], m_conv_w[j], m_conv_b[j], m_dt_bias[j], m_A_log[j],
                                 m_D[j], m_norm[j], m_w_out[j])
        h = h + conv_ffn(rms_norm(h, ffn_norm[i]), f_w_up[i], f_conv_w[i], f_conv_b[i], f_w_down[i])
    return rms_norm(h, final_norm)
```

```python
import numpy as np
import concourse.bass as bass
import concourse.mybir as mybir
from concourse.bass_utils import run_bass_kernel_spmd

F32 = mybir.dt.float32
BF16 = mybir.dt.bfloat16
ALU = mybir.AluOpType
AF = mybir.ActivationFunctionType

D = 2048
KC = 16
T = 512
NSUB = T // 128
SEQ = 2048
EPS = 1e-5
DFF = 5632
M_DI = 4096
M_CONVD = 6144
M_IN = 10304


class Buf:
    __slots__ = ("name", "w", "r")

    def __init__(self, name):
        self.name = name
        self.w = None
        self.r = {}


class KB:
    SEM_ROLL = 30000

    def __init__(self, nc):
        self.nc = nc
        self.engs = {"pe": nc.tensor, "dve": nc.vector, "act": nc.scalar,
                     "pool": nc.gpsimd, "sp": nc.sync}
        self.sem = {}
        self.cnt = {}
        self.nsem = 0
        self.seen = {e: {} for e in self.engs}
        self.dsem = {}
        self.nwait = 0
        self.nins = {e: 0 for e in self.engs}

    def _newsem(self, name):
        self.nsem += 1
        return self.nc.alloc_semaphore(f"{name}_{self.nsem}")

    def _bump(self, eng):
        if eng not in self.sem or self.cnt[eng] >= self.SEM_ROLL:
            self.sem[eng] = self._newsem("s_" + eng)
            self.cnt[eng] = 0
        self.cnt[eng] += 1
        return (id(self.sem[eng]), self.sem[eng], self.cnt[eng], eng)

    def _wait(self, eng, deps):
        h = self.engs[eng]
        seen = self.seen[eng]
        for tok in deps:
            k, sem, val, src = tok
            if eng == "pe" and src == "pe":
                continue
            if seen.get(k, 0) >= val:
                continue
            h.wait_ge(sem, val)
            self.nwait += 1
            seen[k] = val

    @staticmethod
    def _deps(reads, writes):
        deps = []
        for b in reads:
            if b.w is not None:
                deps.append(b.w)
        for b in writes:
            if b.w is not None:
                deps.append(b.w)
            deps.extend(b.r.values())
        return deps

    def op(self, eng, fn, reads=(), writes=()):
        self._wait(eng, self._deps(reads, writes))
        ins = fn(self.engs[eng])
        self.nins[eng] += 1
        tok = self._bump(eng)
        ins.then_inc(tok[1], 1)
        for b in reads:
            b.r[eng] = tok
        for b in writes:
            b.w = tok
            b.r = {}
        return ins

    def dma(self, q, out, in_, key, reads=(), writes=()):
        self._wait(q, self._deps(reads, writes))
        if key not in self.dsem:
            self.dsem[key] = [self._newsem("d_" + key), 0]
        ent = self.dsem[key]
        ent[1] += 16
        ins = self.engs[q].dma_start(out=out, in_=in_)
        ins.then_inc(ent[0], 16)
        self.nins[q] += 1
        tok = (id(ent[0]), ent[0], ent[1], "dma:" + key)
        for b in reads:
            b.r["dma:" + key] = tok
        for b in writes:
            b.w = tok
            b.r = {}
        return tok

    def wait_all(self, eng, bufs):
        deps = []
        for b in bufs:
            if b.w is not None:
                deps.append(b.w)
            deps.extend(b.r.values())
        self._wait(eng, deps)


def _cols(v):
    v = np.asarray(v, np.float32)
    return np.ascontiguousarray(v.reshape(-1, 128).T)


def _pack(named):
    offs = {}
    parts = []
    o = 0
    for k, a in named:
        a = np.asarray(a, np.float32).reshape(128, -1)
        offs[k] = (o, a.shape[1])
        parts.append(a)
        o += a.shape[1]
    return np.ascontiguousarray(np.concatenate(parts, axis=1)), offs


def pack_params(inp):
    named = []
    for l in range(4):
        named.append((f"mixn{l}", _cols(inp["mix_norm"][l])))
        named.append((f"ffnn{l}", _cols(inp["ffn_norm"][l])))
        fcw = inp["f_conv_w"][l]
        for k in range(3):
            named.append((f"fcw{l}_{k}", _cols(fcw[k])))
        named.append((f"fcb{l}", _cols(inp["f_conv_b"][l])))
    named.append(("finn", _cols(inp["final_norm"])))
    for j in range(2):
        named.append((f"lbl{j}", _cols(inp["hgrn_lb_logits"][j])))
        named.append((f"gn{j}", _cols(inp["hgrn_gnorm"][j])))
        for k in range(4):
            named.append((f"mcw{j}_{k}", _cols(inp["m_conv_w"][j][k])))
        named.append((f"mcb{j}", _cols(inp["m_conv_b"][j])))
        named.append((f"mnw{j}", _cols(inp["m_norm"][j])))
        named.append((f"dtb{j}", np.broadcast_to(np.asarray(inp["m_dt_bias"][j], np.float32)[None, :], (128, 64))))
        named.append((f"alog{j}", np.broadcast_to(np.asarray(inp["m_A_log"][j], np.float32)[None, :], (128, 64))))
        named.append((f"dsk{j}", np.broadcast_to(np.asarray(inp["m_D"][j], np.float32)[None, :], (128, 64))))
    return _pack(named)


def make_consts():
    i = np.arange(128)
    ident = np.eye(128, dtype=np.float32)
    U = (i[:, None] <= i[None, :]).astype(np.float32)
    MS = (i[:, None] > i[None, :]).astype(np.float32)
    ones = np.ones((128, 128), np.float32)
    return _pack([("ident", ident), ("U", U), ("MS", MS), ("ones", ones)])


class Prog:
    def __init__(self, n_seq=2, tiles_per_seq=4, stages=None, final=True, nslots=3):
        self.n_seq = n_seq
        self.tiles_per_seq = tiles_per_seq
        self.final = final
        if stages is None:
            stages = []
            for l in range(4):
                stages.append(("hgrn" if l % 2 == 0 else "mamba", l))
                stages.append(("ffn", l))
        self.stages = stages
        self.nslots = nslots
        nc = bass.Bass("TRN2", target_bir_lowering=False)
        self.nc = nc
        self.kb = KB(nc)
        self._n = 0
        self._dma_toks = []
        self._scr_off = 0
        self.bank_i = 0
        self.bank_p = 0
        self.bank_s = 0

    def T_(self, name, shape, dt):
        t = self.nc.alloc_sbuf_tensor(name, list(shape), dt).ap()
        return t, Buf(name)

    def carve_reset(self):
        self._scr_off = 0

    def carve(self, name, shape, dt):
        esz = 4 if dt == F32 else 2
        n = int(np.prod(shape[1:]))
        nbytes = (n * esz + 31) // 32 * 32
        off = self._scr_off
        self._scr_off += nbytes
        assert self._scr_off <= self.SCR, (name, self._scr_off)
        v = self.scr[0:shape[0], off // 2: off // 2 + n * esz // 2]
        if dt == F32:
            v = v.bitcast(F32)
        if len(shape) == 3:
            v = v.rearrange("p (a b) -> p a b", b=shape[2])
        return v, Buf(name)

    def barrier(self):
        kb = self.kb
        engs = ["pe", "act", "dve", "pool"]
        toks = []
        for e in engs:
            if e in kb.sem:
                toks.append((id(kb.sem[e]), kb.sem[e], kb.cnt[e], e))
        toks.extend(self._dma_toks)
        self._dma_toks = []
        for e in engs:
            h = kb.engs[e]
            for (k, sem, val, src) in toks:
                if src == e or kb.seen[e].get(k, 0) >= val:
                    continue
                h.wait_ge(sem, val)
                kb.nwait += 1
                kb.seen[e][k] = val

    def next_bank(self, pool="A"):
        if pool == "A":
            i = self.bank_i % 8
            self.bank_i += 1
        elif pool == "P":
            i = self.bank_p % 4
            self.bank_p += 1
        else:
            i = 4 + self.bank_s % 4
            self.bank_s += 1
        return self.banks[i], self.bbufs[i]

    def mm(self, bank_buf, pairs, reads, first_start=True, last_stop=True):
        n = len(pairs)

        def fn(e):
            ins = None
            for i, (o, l, r) in enumerate(pairs):
                ins = e.matmul(o, l, r, start=(i == 0 and first_start), stop=(i == n - 1 and last_stop))
            return ins
        self.kb.nins["pe"] += n - 1
        assert bank_buf.w is None or bank_buf.r, bank_buf.name
        return self.kb.op("pe", fn, reads=reads, writes=[bank_buf])

    def mm_multi(self, bank_buf, groups, reads):
        def fn(e):
            ins = None
            for g in groups:
                n = len(g)
                for i, (o, l, r) in enumerate(g):
                    ins = e.matmul(o, l, r, start=(i == 0), stop=(i == n - 1))
            return ins
        self.kb.nins["pe"] += sum(len(g) for g in groups) - 1
        assert bank_buf.w is None or bank_buf.r, bank_buf.name
        return self.kb.op("pe", fn, reads=reads, writes=[bank_buf])

    def tr_multi(self, bank_buf, items, ident, reads):
        def fn(e):
            ins = None
            for (o, i_) in items:
                if i_.dtype == F32:
                    ins = e.matmul(o, i_, ident, start=True, stop=True)
                else:
                    ins = e.transpose(o, i_, ident)
            return ins
        self.kb.nins["pe"] += len(items) - 1
        assert bank_buf.w is None or bank_buf.r, bank_buf.name
        return self.kb.op("pe", fn, reads=reads, writes=[bank_buf])

    def build(self, pc_offs, cc_offs):
        nc, kb = self.nc, self.kb
        self.pco = pc_offs
        self.cco = cc_offs
        npc = max(o + n for o, n in pc_offs.values())
        ncc = max(o + n for o, n in cc_offs.values())
        ntok = self.tiles_per_seq * T
        dr = lambda name, shape: nc.dram_tensor(name, list(shape), F32, kind="ExternalInput").ap()
        self.x = dr("x", [self.n_seq, SEQ, D])
        self.pcols_d = dr("pcols", [128, npc])
        self.consts_d = dr("consts", [128, ncc])
        kinds = {k for k, _ in self.stages}
        wshapes = {"hgrn": {"hgrn_w_in": [2, D, 8192], "hgrn_w_out": [2, D, D]},
                   "mamba": {"m_w_in": [2, D, M_IN], "m_w_out": [2, M_DI, D]},
                   "ffn": {"f_w_up": [4, D, 2 * DFF], "f_w_down": [4, DFF, D]}}
        self.W = {}
        for k in ("hgrn", "mamba", "ffn"):
            if k in kinds:
                for nm, shp in wshapes[k].items():
                    self.W[nm] = dr(nm, shp)
        self.out = nc.dram_tensor("out", [self.n_seq, SEQ, D], F32, kind="ExternalOutput").ap()

        self.pc, self.Bpc = self.T_("pc", [128, npc], F32)
        self.cf, self.Bcf = self.T_("cf", [128, ncc], F32)
        self.cb, self.Bcb = self.T_("cb", [128, ncc], BF16)
        kb.dma("sp", self.pc, self.pcols_d, "pc", writes=[self.Bpc])
        kb.dma("sp", self.cf, self.consts_d, "cf", writes=[self.Bcf])
        kb.dma("pool", self.cb, self.consts_d, "cb", writes=[self.Bcb])

        self.banks = [nc.alloc_psum_tensor(f"bank{i}", [128, 512], F32).ap() for i in range(8)]
        self.bbufs = [Buf(f"bank{i}") for i in range(8)]

        self.hT, _ = self.T_("hT", [128, KC, T], F32)
        self.Bh = [Buf(f"h{m}") for m in range(KC)]
        self.uT, self.Bu = self.T_("uT", [128, KC, T], BF16)
        self.slots = [self.T_(f"wslot{i}", [128, 8192], BF16) for i in range(self.nslots)]
        self.plan = []
        self.issued = 0
        self.consumed = 0
        self.sqring = [self.T_(f"sqr{i}", [128, T], BF16) for i in range(4)]
        self.lnv, self.Blnv = self.T_("lnv", [128, T], F32)
        self.rstd, self.Brstd = self.T_("rstd", [128, T], F32)
        self.state_bufs = []
        self.ftail = []
        for l in range(4):
            t_, b_ = self.T_(f"ftail{l}", [128, 88, 2], F32)
            self.ftail.append((t_, b_))
            self.state_bufs.append((t_, b_))
        self.mtail = []
        for j in range(2):
            t_, b_ = self.T_(f"mtail{j}", [128, 48, 3], F32)
            self.mtail.append((t_, b_))
            self.state_bufs.append((t_, b_))
        self.SCR = 86 * 1024
        self.scr = nc.alloc_sbuf_tensor("scr", [128, self.SCR // 2], BF16).ap()
        self.st_d = {}
        for (kind, l) in self.stages:
            if kind == "hgrn":
                self.st_d[l] = nc.dram_tensor(f"st{l}", [128, 16 * 128], F32).ap()
            if kind == "mamba":
                self.st_d[l] = nc.dram_tensor(f"st{l}", [128, 64 * 64], F32).ap()
        print("sbuf bytes remaining/partition:", nc.sbuf_bytes_remaining)

        self.uniq = {}
        for s in range(self.n_seq):
            for t in range(self.tiles_per_seq):
                for si, (kind, l) in enumerate(self.stages):
                    self.cur_stage = si
                    getattr(self, "plan_" + kind)(l)
        print("weight slabs:", len(self.plan), "unique:", len(self.uniq))
        self.prologue_init()

        self.prep_params()
        ti = 0
        nst = len(self.stages)
        for s in range(self.n_seq):
            self.reset_state()
            for t in range(self.tiles_per_seq):
                self.first_tile = (t == 0)
                self.barrier()
                self.load_tile(s, t, ti)
                if ti == 0:
                    self.prologue_stage(0)
                for si, (kind, l) in enumerate(self.stages):
                    self.rmsnorm(("ffnn%d" if kind == "ffn" else "mixn%d") % l)
                    self.barrier()
                    if ti == 0 and si + 1 < nst:
                        self.prologue_stage(si + 1)
                    getattr(self, "emit_" + kind)(l)
                self.barrier()
                self.store_tile(s, t, ti)
                ti += 1
        assert self.consumed == len(self.plan), (self.consumed, len(self.plan))
        kb._wait("pool", self._dma_toks)
        print("instructions:", kb.nins, "waits:", kb.nwait, "sems:", kb.nsem)
        return nc

    def pcol(self, key, i=None):
        o, n = self.pco[key]
        if i is None:
            return self.pc[:, o:o + n]
        return self.pc[:, o + i:o + i + 1]

    def cst(self, key, bf=False):
        o, n = self.cco[key]
        return (self.cb if bf else self.cf)[:, o:o + n]

    def padd(self, key, nel, pieces):
        if key not in self.uniq:
            self.uniq[key] = (len(self.uniq), nel, pieces, self.cur_stage)
        self.plan.append(key)

    def prologue_init(self):
        nu = len(self.uniq)
        NPT = 96
        wts = [self.nc.dram_tensor(f"wbf{i}", [min(NPT, nu - i * NPT), 128, 8192], BF16).ap() for i in range((nu + NPT - 1) // NPT)]

        class _W:
            def __getitem__(_s, u):
                return wts[u // NPT][u % NPT]
        self.wbf = _W()
        self.wbuf = {}

    def prologue_stage(self, st):
        kb = self.kb
        bufs = []
        for key, (u, nel, pieces, st_) in self.uniq.items():
            if st_ != st:
                continue
            for (dst_fn, src) in pieces:
                kb.dma("pool", dst_fn(self.wbf[u]), src, f"pro{st}")
            b = Buf(f"wbf{u}")
            self.wbuf[key] = b
            bufs.append(b)
        if bufs:
            ent = kb.dsem[f"pro{st}"]
            tok = (id(ent[0]), ent[0], ent[1], f"dma:pro{st}")
            for b in bufs:
                b.w = tok

    def wget(self):
        kb = self.kb
        while self.issued < len(self.plan) and self.issued < self.consumed + self.nslots:
            i = self.issued
            key = self.plan[i]
            u, nel, _, _ = self.uniq[key]
            slot, b = self.slots[i % self.nslots]
            kb.dma("sp", slot[:, 0:nel], self.wbf[u][:, 0:nel], f"ws{i % self.nslots}", reads=[self.wbuf[key]], writes=[b])
            self.issued += 1
        slot, b = self.slots[self.consumed % self.nslots]
        self.consumed += 1
        return slot, b

    @staticmethod
    def wsrc(w2d, c0, n):
        return w2d.rearrange("(kc p) n -> p kc n", p=128)[:, :, c0:c0 + n]

    @staticmethod
    def wdst(off, kc, n):
        return lambda slot: slot[:, off:off + kc * n].rearrange("p (kc n) -> p kc n", n=n)

    def load_tile(self, s, t, ti):
        kb = self.kb
        ident = self.cst("ident")
        self.carve_reset()
        self.xin = [self.carve(f"xin{i}", [128, D], F32) for i in range(2)]
        for sub in range(NSUB):
            xin, bx = self.xin[sub % 2]
            r0 = t * T + sub * 128
            kb.dma("pool", xin, self.x[s, r0:r0 + 128, :], f"xin{sub % 2}", writes=[bx])
            for q in range(4):
                bank, bb = self.next_bank()
                items = [(bank[:, j * 128:(j + 1) * 128], xin[:, (4 * q + j) * 128:(4 * q + j + 1) * 128]) for j in range(4)]
                self.tr_multi(bb, items, ident, reads=[bx, self.Bcf])
                dst = self.hT[:, 4 * q:4 * q + 4, sub * 128:(sub + 1) * 128]
                src = bank.rearrange("p (j t) -> p j t", t=128)
                eng = "act" if q % 2 == 0 else "dve"
                if eng == "act":
                    kb.op("act", lambda e, d=dst, s_=src: e.copy(d, s_), reads=[bb], writes=self.Bh[4 * q:4 * q + 4])
                else:
                    kb.op("dve", lambda e, d=dst, s_=src: e.tensor_copy(d, s_), reads=[bb], writes=self.Bh[4 * q:4 * q + 4])

    def store_tile(self, s, t, ti):
        kb = self.kb
        ident = self.cst("ident")
        self.carve_reset()
        self.xin = [self.carve(f"xout{i}", [128, D], F32) for i in range(2)]
        if self.final:
            self.rms_stats(self.Bh)
            o, _ = self.pco["finn"]
            for m in range(KC):
                kb.op("dve", lambda e, m=m: e.scalar_tensor_tensor(self.hT[:, m, :], self.hT[:, m, :], self.pc[:, o + m:o + m + 1], self.rstd, op0=ALU.mult, op1=ALU.mult),
                      reads=[self.Bh[m], self.Brstd, self.Bpc], writes=[self.Bh[m]])
        for sub in range(NSUB):
            xo, bx = self.xin[sub % 2]
            for q in range(4):
                bank, bb = self.next_bank()
                items = [(bank[:, j * 128:(j + 1) * 128], self.hT[:, 4 * q + j, sub * 128:(sub + 1) * 128]) for j in range(4)]
                self.tr_multi(bb, items, ident, reads=self.Bh[4 * q:4 * q + 4] + [self.Bcf])
                if q % 2 == 0:
                    kb.op("act", lambda e, q=q, bank=bank, xo=xo: e.copy(xo[:, q * 512:(q + 1) * 512], bank), reads=[bb], writes=[bx])
                else:
                    kb.op("dve", lambda e, q=q, bank=bank, xo=xo: e.tensor_copy(xo[:, q * 512:(q + 1) * 512], bank), reads=[bb], writes=[bx])
            r0 = t * T + sub * 128
            self._dma_toks.append(kb.dma("pool", self.out[s, r0:r0 + 128, :], xo, f"xout{sub % 2}", reads=[bx]))

    def rms_stats(self, hbufs):
        kb = self.kb
        ones = self.cst("ones", bf=True)
        bank, bb = self.next_bank()
        for m in range(KC):
            sqt, bsq = self.sqring[m % len(self.sqring)]
            if m % 2 == 0:
                kb.op("act", lambda e, m=m, sqt=sqt: e.activation(sqt, self.hT[:, m, :], AF.Square), reads=[hbufs[m]], writes=[bsq])
            else:
                kb.op("pool", lambda e, m=m, sqt=sqt: e.tensor_tensor(sqt, self.hT[:, m, :], self.hT[:, m, :], ALU.mult), reads=[hbufs[m]], writes=[bsq])
            ins_first = (m == 0)
            self.kb._wait("pe", kb._deps([bsq, self.Bcb], [bb] if m == 0 else []))
            ins = self.nc.tensor.matmul(bank, ones, sqt, start=(m == 0), stop=(m == KC - 1))
            kb.nins["pe"] += 1
            tok = kb._bump("pe")
            ins.then_inc(tok[1], 1)
            bsq.r["pe"] = tok
            if m == KC - 1:
                bb.w = tok
                bb.r = {}
        kb.op("act", lambda e: e.activation(self.lnv, bank, AF.Ln, bias=EPS, scale=1.0 / D), reads=[bb], writes=[self.Blnv])
        kb.op("act", lambda e: e.activation(self.rstd, self.lnv, AF.Exp, scale=-0.5), reads=[self.Blnv], writes=[self.Brstd])

    def rmsnorm(self, wkey):
        kb = self.kb
        self.rms_stats(self.Bh)
        o, _ = self.pco[wkey]
        for m in range(KC):
            kb.op("dve", lambda e, m=m: e.scalar_tensor_tensor(self.uT[:, m, :], self.hT[:, m, :], self.pc[:, o + m:o + m + 1], self.rstd, op0=ALU.mult, op1=ALU.mult),
                  reads=[self.Bh[m], self.Brstd, self.Bpc], writes=[self.Bu])

    def prep_params(self):
        kb = self.kb
        self.lb = []
        l0 = self.pcol("lbl0")
        l1 = self.pcol("lbl1")
        d01, bd01 = self.T_("d01", [128, 16], F32)
        p0, bp0 = self.T_("p0", [128, 16], F32)
        p1, bp1 = self.T_("p1", [128, 16], F32)
        kb.op("dve", lambda e: e.tensor_tensor(d01, l0, l1, ALU.subtract), reads=[self.Bpc], writes=[bd01])
        kb.op("act", lambda e: e.activation(p0, d01, AF.Sigmoid), reads=[bd01], writes=[bp0])
        kb.op("act", lambda e: e.activation(p1, d01, AF.Sigmoid, scale=-1.0), reads=[bd01], writes=[bp1])
        for j in range(2):
            lb, blb = self.T_(f"lb{j}", [128, 16], F32)
            oml, boml = self.T_(f"oml{j}", [128, 16], F32)
            noml, bnoml = self.T_(f"noml{j}", [128, 16], F32)
            if j == 0:
                kb.op("dve", lambda e, lb=lb: e.tensor_tensor(lb, p0, p0, ALU.subtract), reads=[bp0], writes=[blb])
            else:
                kb.op("dve", lambda e, lb=lb: e.tensor_tensor(lb, p0, p1, ALU.add), reads=[bp0, bp1], writes=[blb])
                kb.op("dve", lambda e, lb=lb: e.tensor_tensor(lb, lb, p0, ALU.subtract), reads=[bp0, blb], writes=[blb])
            kb.op("dve", lambda e, lb=lb, noml=noml: e.tensor_scalar(noml, lb, 1.0, None, op0=ALU.subtract), reads=[blb], writes=[bnoml])
            kb.op("dve", lambda e, oml=oml, noml=noml: e.tensor_scalar(oml, noml, -1.0, None, op0=ALU.mult), reads=[bnoml], writes=[boml])
            self.lb.append((lb, blb, oml, boml, noml, bnoml))
        self.Arow = []
        for j in range(2):
            a, ba = self.T_(f"Arow{j}", [128, 64], F32)
            kb.op("act", lambda e, a=a, j=j: e.activation(a, self.pcol(f"alog{j}"), AF.Exp), reads=[self.Bpc], writes=[ba])
            kb.op("dve", lambda e, a=a: e.tensor_scalar(a, a, -1.0, None, op0=ALU.mult), reads=[ba], writes=[ba])
            self.Arow.append((a, ba))

    def reset_state(self):
        kb = self.kb
        for (t, b) in self.state_bufs:
            kb.op("dve", lambda e, t=t: e.memset(t, 0.0), writes=[b])

    def alloc_ffn(self):
        self.carve_reset()
        self.xs = [self.carve(f"xs{i}", [128, T + 4], F32) for i in range(3)]
        self.yc = [self.carve(f"yc{i}", [128, T], F32) for i in range(4)]
        self.gs = [self.carve(f"gs{i}", [128, T], F32) for i in range(2)]
        self.aT = [self.carve(f"aT{i}", [128, 4, T], BF16) for i in range(2)]

    def plan_ffn(self, l):
        wu = self.W["f_w_up"][l]
        wd = self.W["f_w_down"][l]
        for grp in range(11):
            for s2 in range(2):
                s = 2 * grp + s2
                self.padd(("fu", l, s), 8192, [(self.wdst(0, 16, 256), self.wsrc(wu, 256 * s, 256)),
                                               (self.wdst(4096, 16, 256), self.wsrc(wu, DFF + 256 * s, 256))])
            self.padd(("fd", l, grp), 8192, [(lambda slot: slot.rearrange("p (c n) -> p c n", n=D),
                                              wd[512 * grp:512 * grp + 512, :].rearrange("(c p) n -> p c n", p=128))])

    def conv_chunk(self, bank, bb, tail, btail, ci, wkeys, bkey, ntap, ring_i):
        kb = self.kb
        nt = ntap - 1
        xs, bxs = self.xs[ring_i % len(self.xs)]
        yc, byc = self.yc[ring_i % len(self.yc)]
        kb.op("act", lambda e: e.copy(xs[:, nt:nt + T], bank), reads=[bb], writes=[bxs])
        kb.op("dve", lambda e: e.tensor_copy(xs[:, 0:nt], tail[:, ci, 0:nt]), reads=[btail], writes=[bxs])
        wl = self.pcol(wkeys[ntap - 1], ci)
        kb.op("act", lambda e: e.activation(yc, bank, AF.Identity, bias=self.pcol(bkey, ci), scale=wl), reads=[bb, self.Bpc], writes=[byc])
        for k in range(ntap - 1):
            wk = self.pcol(wkeys[k], ci)
            kb.op("dve", lambda e, k=k, wk=wk: e.scalar_tensor_tensor(yc, xs[:, k:k + T], wk, yc, op0=ALU.mult, op1=ALU.add),
                  reads=[bxs, byc, self.Bpc], writes=[byc])
        kb.op("dve", lambda e: e.tensor_copy(tail[:, ci, 0:nt], xs[:, T:T + nt]), reads=[bxs], writes=[btail])
        return yc, byc

    def emit_ffn(self, l):
        kb = self.kb
        self.alloc_ffn()
        tail, btail = self.ftail[l]
        wkeys = [f"fcw{l}_{k}" for k in range(3)]
        ring = 0
        for grp in range(11):
            aT, baT = self.aT[grp % 2]
            for s2 in range(2):
                s = 2 * grp + s2
                slot, bs = self.wget()
                sv = slot.rearrange("p (h kc n) -> p h kc n", h=2, n=256)
                ys = []
                for c in range(4):
                    bank, bb = self.next_bank()
                    self.mm(bb, [(bank, sv[:, c // 2, kc, (c % 2) * 128:(c % 2 + 1) * 128], self.uT[:, kc, :]) for kc in range(KC)], reads=[bs, self.Bu])
                    ci = (2 * s + c) if c < 2 else (44 + 2 * s + (c - 2))
                    ys.append(self.conv_chunk(bank, bb, tail, btail, ci, wkeys, f"fcb{l}", 3, ring))
                    ring += 1
                for c in range(2):
                    gsb, bgs = self.gs[c]
                    (yg, byg), (yu, byu) = ys[c], ys[2 + c]
                    kb.op("act", lambda e, gsb=gsb, yg=yg: e.activation(gsb, yg, AF.Silu), reads=[byg], writes=[bgs])
                    kb.op("dve", lambda e, gsb=gsb, yu=yu, c=c, s2=s2: e.tensor_tensor(aT[:, 2 * s2 + c, :], gsb, yu, ALU.mult), reads=[bgs, byu], writes=[baT])
            slot, bs = self.wget()
            sv = slot.rearrange("p (c n) -> p c n", n=D)
            for m in range(KC):
                bank, bb = self.next_bank()
                self.mm(bb, [(bank, sv[:, c, m * 128:(m + 1) * 128], aT[:, c, :]) for c in range(4)], reads=[bs, baT])
                kb.op("dve", lambda e, m=m, bank=bank: e.tensor_tensor(self.hT[:, m, :], self.hT[:, m, :], bank, ALU.add), reads=[bb, self.Bh[m]], writes=[self.Bh[m]])

    def alloc_hgrn(self):
        self.carve_reset()
        c = self.carve
        self.hS = c("hS", [128, 16, 128], F32)
        self.h_in = []
        for i in range(4):
            self.h_in.append({"q": c(f"h_q{i}", [128, T], F32), "sg": c(f"h_sg{i}", [128, T], F32),
                              "gs": c(f"h_gs{i}", [128, T], BF16), "vT": c(f"h_vT{i}", [128, T], BF16)})
        self.h_ones = c("h_ones", [128, T], F32)
        self.h_pb = []
        for i in range(2):
            d = {}
            d["k"] = c(f"h_k{i}", [128, T], F32)
            d["bp"] = c(f"h_bp{i}", [128, T + 8], F32)
            d["d1"] = c(f"h_d1{i}", [128, T], F32)
            d["d2"] = c(f"h_d2{i}", [128, T], F32)
            d["d3"] = c(f"h_d3{i}", [128, T], F32)
            for nm in ("qm", "km", "qa", "kl"):
                d[nm] = c(f"h_{nm}{i}", [128, T], BF16)
            d["vtm"] = c(f"h_vtm{i}", [64, 8, 128], BF16)
            d["kltm"] = c(f"h_kltm{i}", [64, 8, 128], BF16)
            d["A"] = c(f"h_A{i}", [64, 8, 64], BF16)
            d["Sbf"] = c(f"h_Sbf{i}", [128, 8, 128], BF16)
            d["dd"] = c(f"h_dd{i}", [128, 8], F32)
            d["dec"] = c(f"h_dec{i}", [128, 8], F32)
            self.h_pb.append(d)
        self.h_oT = [c(f"h_oT{i}", [128, 4, T], BF16) for i in range(2)]

    @staticmethod
    def hgrn_seq():
        seq = [("A", 0), ("A", 1)]
        for p in range(8):
            if 2 * p + 2 < 16:
                seq.append(("A", 2 * p + 2))
                seq.append(("A", 2 * p + 3))
            if p % 2 == 0 and p >= 2:
                seq.append(("O", (p - 2) // 2))
        seq.append(("O", 3))
        return seq

    def plan_hgrn(self, l):
        j = l // 2
        wi = self.W["hgrn_w_in"][j]
        wo = self.W["hgrn_w_out"][j]
        for (kind, i) in self.hgrn_seq():
            if kind == "A":
                h = i
                self.padd(("hi", l, h), 8192, [(self.wdst(2048 * k, 16, 128), self.wsrc(wi, 2048 * k + 128 * h, 128)) for k in range(4)])
            elif kind == "O":
                grp = i
                self.padd(("ho", l, grp), 8192, [(lambda slot: slot.rearrange("p (c n) -> p c n", n=D),
                                                  wo[512 * grp:512 * grp + 512, :].rearrange("(c p) n -> p c n", p=128))])

    def emit_hgrn(self, l):
        kb = self.kb
        j = l // 2
        self.alloc_hgrn()
        lb, blb, oml, boml, noml, bnoml = self.lb[j]
        S, _bS = self.hS
        bSh = [Buf(f"hS{i}") for i in range(16)]
        ones_f, bones = self.h_ones
        identb = self.cst("ident", bf=True)
        onesb = self.cst("ones", bf=True)
        U64 = self.cst("U")[0:64, 0:64].rearrange("p (o t) -> p o t", o=1).to_broadcast([64, 8, 64])
        gn = self.pcol(f"gn{j}")

        if self.first_tile:
            kb.op("dve", lambda e: e.memset(S, 0.0), writes=bSh)
        else:
            kb.dma("pool", S.rearrange("p a b -> p (a b)"), self.st_d[l], f"stin{l}", writes=bSh)
        kb.op("dve", lambda e: e.memset(ones_f, 1.0), writes=[bones])
        for i in range(2):
            bp_, bbp_ = self.h_pb[i]["bp"]
            kb.op("dve", lambda e, bp_=bp_: e.memset(bp_[:, 0:1], 0.0), writes=[bbp_])

        v3 = lambda ap: ap.rearrange("p (c t) -> p c t", t=64)
        bc = lambda ap: ap.to_broadcast([128, 8, 64])
        pbanks = {}

        def gen_A(h):
            slot, bs = self.wget()
            sv = slot.rearrange("p (c kc n) -> p c kc n", c=4, n=128)
            pb = []
            for c in range(4):
                bank, bb = self.next_bank("P")
                self.mm(bb, [(bank, sv[:, c, kc, :], self.uT[:, kc, :]) for kc in range(KC)], reads=[bs, self.Bu])
                pb.append((bank, bb))
                if c < 3:
                    yield
            pbanks[h] = pb
            stage_B1(h)
            yield

        def stage_B1(h):
            (q_ps, bq_ps), (f_ps, bf_ps), (v_ps, bv_ps), (g_ps, bg_ps) = pbanks.pop(h)
            I = self.h_in[h % 4]
            q, bq = I["q"]; sg, bsg = I["sg"]; gs, bgs = I["gs"]; vT, bvT = I["vT"]
            kb.op("act", lambda e: e.activation(q, q_ps, AF.Silu), reads=[bq_ps], writes=[bq])
            kb.op("act", lambda e: e.activation(gs, g_ps, AF.Silu), reads=[bg_ps], writes=[bgs])
            kb.op("act", lambda e: e.activation(sg, f_ps, AF.Sigmoid), reads=[bf_ps], writes=[bsg])
            kb.op("act", lambda e: e.copy(vT, v_ps), reads=[bv_ps], writes=[bvT])

        def stage_B2(h):
            par = h % 2
            hh = h % 4
            oT, boT = self.h_oT[(h // 4) % 2]
            I = self.h_in[h % 4]
            q, bq = I["q"]; sg, bsg = I["sg"]; gs, bgs = I["gs"]; vT, bvT = I["vT"]
            Pb = self.h_pb[par]
            k_, bk = Pb["k"]; bp, bbp = Pb["bp"]; d1, bd1 = Pb["d1"]; d2, bd2 = Pb["d2"]; d3, bd3 = Pb["d3"]
            qm, bqm = Pb["qm"]; km, bkm = Pb["km"]; qa, bqa = Pb["qa"]; kl, bkl = Pb["kl"]
            vtm, bvtm = Pb["vtm"]; kltm, bkltm = Pb["kltm"]; A, bA = Pb["A"]; Sbf, bSbf = Pb["Sbf"]
            dd, bdd = Pb["dd"]; dec, bdec = Pb["dec"]
            bS = bSh[h]
            lf, blf = d3, bd3
            sbk = [4 + 2 * par, 5 + 2 * par]
            nb = [0]

            def nbank():
                i = sbk[nb[0] % 2]
                nb[0] += 1
                return self.banks[i], self.bbufs[i]
            b3 = bp[:, 1:T + 1].rearrange("p (c t) -> p c t", t=64)
            bst = bp[:, 0:T].rearrange("p (c t) -> p c t", t=64)[:, :, 0:1]
            bmid = b3[:, :, 31:32]
            blast = b3[:, :, 63:64]
            kb.op("act", lambda e: e.activation(lf, sg, AF.Ln, bias=lb[:, h:h + 1], scale=oml[:, h:h + 1]), reads=[bsg, blb, boml], writes=[blf])
            yield
            kb.op("dve", lambda e: e.tensor_scalar(k_, sg, noml[:, h:h + 1], oml[:, h:h + 1], op0=ALU.mult, op1=ALU.add), reads=[bsg, bnoml, boml], writes=[bk])
            kb.op("dve", lambda e: e.tensor_tensor_scan(bp[:, 1:T + 1], ones_f, lf, 0.0, ALU.mult, ALU.add), reads=[bones, blf], writes=[bbp])
            kb.op("dve", lambda e: e.tensor_tensor(v3(d1), b3, bc(bmid), ALU.subtract), reads=[bbp], writes=[bd1])
            yield
            kb.op("act", lambda e: e.activation(d2, d1, AF.Exp), reads=[bd1], writes=[bd2])
            kb.op("act", lambda e: e.activation(d3, d1, AF.Exp, scale=-1.0), reads=[bd1], writes=[bd3])
            yield
            kb.op("dve", lambda e: e.tensor_tensor(qm, q, d2, ALU.mult), reads=[bq, bd2], writes=[bqm])
            kb.op("dve", lambda e: e.tensor_tensor(km, k_, d3, ALU.mult), reads=[bk, bd3], writes=[bkm])
            kb.op("dve", lambda e: e.tensor_tensor(v3(d1), b3, bc(bst), ALU.subtract), reads=[bbp], writes=[bd1])
            yield
            kb.op("act", lambda e: e.activation(d2, d1, AF.Exp), reads=[bd1], writes=[bd2])
            yield
            kb.op("dve", lambda e: e.tensor_tensor(qa, q, d2, ALU.mult), reads=[bq, bd2], writes=[bqa])
            kb.op("dve", lambda e: e.tensor_tensor(v3(d1), b3, bc(blast), ALU.subtract), reads=[bbp], writes=[bd1])
            kb.op("dve", lambda e: e.tensor_tensor(dd.rearrange("p (c o) -> p c o", o=1), blast, bst, ALU.subtract), reads=[bbp], writes=[bdd])
            yield
            kb.op("act", lambda e: e.activation(d3, d1, AF.Exp, scale=-1.0), reads=[bd1], writes=[bd3])
            kb.op("act", lambda e: e.activation(dec, dd, AF.Exp), reads=[bdd], writes=[bdec])
            yield
            kb.op("dve", lambda e: e.tensor_tensor(kl, k_, d3, ALU.mult), reads=[bk, bd3], writes=[bkl])
            yield
            bank, bb = nbank()
            bkb = bank.bitcast(BF16)
            self.tr_multi(bb, [(bkb[0:64, c * 128:(c + 1) * 128], vT[:, c * 64:(c + 1) * 64]) for c in range(8)], identb, reads=[bvT, self.Bcb])
            bank2, bb2 = nbank()
            bkb2 = bank2.bitcast(BF16)
            self.tr_multi(bb2, [(bkb2[0:64, c * 128:(c + 1) * 128], kl[:, c * 64:(c + 1) * 64]) for c in range(8)], identb, reads=[bkl, self.Bcb])
            yield
            kb.op("act", lambda e: e.copy(vtm.rearrange("p c v -> p (c v)"), bkb[0:64, :]), reads=[bb], writes=[bvtm])
            kb.op("dve", lambda e: e.tensor_copy(kltm.rearrange("p c v -> p (c v)"), bkb2[0:64, :]), reads=[bb2], writes=[bkltm])
            yield
            bankA, bbA = nbank()
            self.mm_multi(bbA, [[(bankA[0:64, c * 64:(c + 1) * 64], km[:, c * 64:(c + 1) * 64], qm[:, c * 64:(c + 1) * 64])] for c in range(8)], reads=[bkm, bqm])
            dS = []
            bankd, bbd = nbank()
            self.mm_multi(bbd, [[(bankd[:, cc * 128:(cc + 1) * 128], kltm[:, cc, :], vtm[:, cc, :])] for cc in range(4)], reads=[bkltm, bvtm])
            dS.append((bankd, bbd))
            yield
            kb.op("dve", lambda e: e.tensor_tensor(A, bankA[0:64, :].rearrange("p (c t) -> p c t", t=64), U64, ALU.mult), reads=[bbA, self.Bcf], writes=[bA])
            yield
            bankd, bbd = nbank()
            self.mm_multi(bbd, [[(bankd[:, cc * 128:(cc + 1) * 128], kltm[:, 4 + cc, :], vtm[:, 4 + cc, :])] for cc in range(4)], reads=[bkltm, bvtm])
            dS.append((bankd, bbd))
            yield
            for c in range(8):
                bank, bb = dS[c // 4]
                kb.op("act", lambda e, c=c: e.copy(Sbf[:, c, :], S[:, h, :]), reads=[bS], writes=[bSbf])
                kb.op("dve", lambda e, c=c, bank=bank: e.scalar_tensor_tensor(S[:, h, :], S[:, h, :], dec[:, c:c + 1], bank[:, (c % 4) * 128:(c % 4 + 1) * 128], op0=ALU.mult, op1=ALU.add),
                      reads=[bS, bdec, bb], writes=[bS])
                yield
            o_ps, bo_ps = nbank()
            self.mm_multi(bo_ps, [[(o_ps[:, c * 64:(c + 1) * 64], vtm[:, c, :], A[:, c, :]),
                                   (o_ps[:, c * 64:(c + 1) * 64], Sbf[:, c, :], qa[:, c * 64:(c + 1) * 64])] for c in range(8)],
                          reads=[bvtm, bA, bSbf, bqa])
            yield
            sqt, bsq = self.sqring[par]
            kb.op("act", lambda e: e.activation(sqt, o_ps, AF.Square), reads=[bo_ps], writes=[bsq])
            yield
            bankn, bbn = nbank()
            self.mm(bbn, [(bankn, onesb, sqt)], reads=[bsq, self.Bcb])
            yield
            kb.op("act", lambda e: e.activation(d1, bankn, AF.Ln, bias=EPS, scale=1.0 / 128), reads=[bbn], writes=[bd1])
            kb.op("act", lambda e: e.activation(d2, d1, AF.Exp, scale=-0.5), reads=[bd1], writes=[bd2])
            yield
            kb.op("dve", lambda e: e.scalar_tensor_tensor(d3, o_ps, gn[:, 0:1], d2, op0=ALU.mult, op1=ALU.mult), reads=[bo_ps, bd2, self.Bpc], writes=[bd3])
            kb.op("dve", lambda e: e.tensor_tensor(oT[:, hh, :], d3, gs, ALU.mult), reads=[bd3, bgs], writes=[boT])
            yield

        def gen_O(grp):
            oT, boT = self.h_oT[grp % 2]
            slot, bs = self.wget()
            sv = slot.rearrange("p (c n) -> p c n", n=D)
            for m in range(KC):
                bank, bb = self.next_bank("P")
                self.mm(bb, [(bank, sv[:, c, m * 128:(m + 1) * 128], oT[:, c, :]) for c in range(4)], reads=[bs, boT])
                kb.op("dve", lambda e, m=m, bank=bank: e.tensor_tensor(self.hT[:, m, :], self.hT[:, m, :], bank, ALU.add), reads=[bb, self.Bh[m]], writes=[self.Bh[m]])
                yield

        def chain(gens):
            for g_ in gens:
                yield from g_

        def drain(gen):
            for _ in gen:
                pass

        def rr2(mains, bg):
            mains = list(mains)
            while mains:
                for gen in list(mains):
                    try:
                        next(gen)
                    except StopIteration:
                        mains.remove(gen)
                try:
                    next(bg)
                except StopIteration:
                    pass

        drain(gen_A(0))
        drain(gen_A(1))
        for p in range(8):
            items = []
            if 2 * p + 2 < 16:
                items += [gen_A(2 * p + 2), gen_A(2 * p + 3)]
            if p % 2 == 0 and p >= 2:
                items.append(gen_O((p - 2) // 2))
            bg = chain(items)
            rr2([stage_B2(2 * p), stage_B2(2 * p + 1)], bg)
            drain(bg)
        drain(gen_O(3))
        self._dma_toks.append(kb.dma("pool", self.st_d[l], S.rearrange("p a b -> p (a b)"), f"stout{l}", reads=bSh))

    def alloc_mamba(self):
        self.carve_reset()
        c = self.carve
        self.m_h2 = [c(f"m_h{i}", [128, T], F32) for i in range(2)]
        self.m_dt = c("m_dt", [128, 4, 64], F32)
        self.m_e = c("m_e", [128, 4, 64], F32)
        self.m_dtA = c("m_dtA", [128, 4, 64], F32)
        self.m_acs = c("m_acs", [128, 4, 64], F32)
        self.m_eacs = c("m_eacs", [128, 4, 64], F32)
        self.m_dst = c("m_dst", [128, 4, 64], F32)
        self.m_cdec = c("m_cdec", [128, 4, 64], F32)
        self.m_xc2 = [c(f"m_xc{i}", [128, 4, T], BF16) for i in range(2)]
        self.m_BT2 = [c(f"m_BT{i}", [128, T], BF16) for i in range(2)]
        self.m_CT2 = [c(f"m_CT{i}", [128, T], BF16) for i in range(2)]
        self.m_zs2 = [c(f"m_zs{i}", [128, 4, T], BF16) for i in range(2)]
        self.xs = [c(f"mxs{i}", [128, T + 4], F32) for i in range(2)]
        self.yc = [c(f"myc{i}", [128, T], F32) for i in range(2)]
        self.m_cb = []
        for i in range(2):
            d = {}
            d["xtm"] = c(f"m_xtm{i}", [128, T], F32)
            d["X"] = c(f"m_X{i}", [128, 8, 64], BF16)
            d["Xh"] = c(f"m_Xh{i}", [128, 8, 64], BF16)
            d["Btm"] = c(f"m_Btm{i}", [128, 128], BF16)
            d["CBm"] = c(f"m_CBm{i}", [128, 128], BF16)
            d["Lh"] = c(f"m_Lh{i}", [128, 8, 128], F32)
            d["E"] = c(f"m_E{i}", [128, 8, 128], BF16)
            d["M"] = c(f"m_M{i}", [128, 8, 128], BF16)
            d["t1"] = c(f"m_t1{i}", [128, T], F32)
            d["t2"] = c(f"m_t2{i}", [128, T], F32)
            d["ygn"] = c(f"m_ygn{i}", [128, T], BF16)
            d["ss"] = c(f"m_ss{i}", [128, 4], F32)
            self.m_cb.append(d)
        self.m_hbf = c("m_hbf", [128, T], BF16)
        self.m_yT = [c(f"m_yT{i}", [128, 4, T], BF16) for i in range(2)]

    @staticmethod
    def mamba_seq():
        seq = [("Z", 0), ("X", 0), ("BC", 0)]
        for g in range(8):
            if g < 7:
                seq += [("Z", g + 1), ("X", g + 1), ("BC", g + 1)]
            if g >= 1:
                seq.append(("O", g - 1))
        seq.append(("O", 7))
        return seq

    def plan_mamba(self, l):
        j = l // 2
        wi = self.W["m_w_in"][j]
        wo = self.W["m_w_out"][j]
        self.padd(("mdt", l), 1024, [(self.wdst(0, 16, 64), self.wsrc(wi, 10240, 64))])
        for (kind, g) in self.mamba_seq():
            if kind == "Z":
                self.padd(("mz", l, g), 8192, [(self.wdst(0, 16, 512), self.wsrc(wi, 512 * g, 512))])
            elif kind == "X":
                self.padd(("mx", l, g), 8192, [(self.wdst(0, 16, 512), self.wsrc(wi, 4096 + 512 * g, 512))])
            elif kind == "BC":
                self.padd(("mbc", l, g), 4096, [(self.wdst(0, 16, 128), self.wsrc(wi, 8192 + 128 * g, 128)),
                                                (self.wdst(2048, 16, 128), self.wsrc(wi, 9216 + 128 * g, 128))])
            elif kind == "O":
                self.padd(("mo", l, g), 8192, [(lambda slot: slot.rearrange("p (c n) -> p c n", n=D),
                                                wo[512 * g:512 * g + 512, :].rearrange("(c p) n -> p c n", p=128))])

    def emit_mamba(self, l):
        kb = self.kb
        j = l // 2
        self.alloc_mamba()
        dt, bdt = self.m_dt
        ee, bee = self.m_e
        dtA, bdtA = self.m_dtA
        acs, bacs = self.m_acs
        eacs, beacs = self.m_eacs
        dst, bdst = self.m_dst
        cdec, bcdec = self.m_cdec
        hbf, bhbf = self.m_hbf
        tail, btail = self.mtail[j]
        identb = self.cst("ident", bf=True)
        Uf = self.cst("U")
        onesf = self.cst("ones")
        MS = self.cst("MS")
        Arow, bArow = self.Arow[j]
        dtb = self.pcol(f"dtb{j}")
        dsk = self.pcol(f"dsk{j}")
        wkeys = [f"mcw{j}_{k}" for k in range(4)]

        first_tile = self.first_tile

        slot, bs = self.wget()
        sv = slot[:, 0:16 * 64].rearrange("p (kc n) -> p kc n", n=64)
        bank, bb = self.next_bank()
        self.mm_multi(bb, [[(bank[:, sub * 64:(sub + 1) * 64], self.uT[:, kc, sub * 128:(sub + 1) * 128], sv[:, kc, :]) for kc in range(KC)] for sub in range(4)], reads=[bs, self.Bu])
        b4 = lambda ap: ap.rearrange("p (s h) -> p s h", h=64)
        kb.op("dve", lambda e, bank=bank: e.tensor_tensor(dt, b4(bank[:, 0:256]), dtb.rearrange("p (o h) -> p o h", o=1).to_broadcast([128, 4, 64]), ALU.add), reads=[bb, self.Bpc], writes=[bdt])
        kb.op("act", lambda e: e.activation(ee, dt, AF.Exp), reads=[bdt], writes=[bee])
        kb.op("act", lambda e: e.activation(dt, ee, AF.Ln, bias=1.0), reads=[bee], writes=[bdt])
        kb.op("dve", lambda e: e.tensor_tensor(dtA, dt, Arow.rearrange("p (o h) -> p o h", o=1).to_broadcast([128, 4, 64]), ALU.mult), reads=[bdt, bArow], writes=[bdtA])
        bank, bb = self.next_bank()
        self.mm_multi(bb, [[(bank[:, sub * 64:(sub + 1) * 64], Uf, dtA[:, sub, :])] for sub in range(4)], reads=[bdtA, self.Bcf])
        kb.op("act", lambda e, bank=bank: e.copy(acs, b4(bank[:, 0:256])), reads=[bb], writes=[bacs])
        kb.op("act", lambda e, bank=bank: e.activation(eacs, b4(bank[:, 0:256]), AF.Exp), reads=[bb], writes=[beacs])
        bank, bb = self.next_bank()
        self.mm_multi(bb, [[(bank[:, sub * 64:(sub + 1) * 64], onesf, dtA[:, sub, :])] for sub in range(4)], reads=[bdtA, self.Bcf])
        kb.op("dve", lambda e, bank=bank: e.tensor_tensor(dst, b4(bank[:, 0:256]), acs, ALU.subtract), reads=[bb, bacs], writes=[bdst])
        kb.op("act", lambda e: e.activation(dst, dst, AF.Exp), reads=[bdst], writes=[bdst])
        kb.op("act", lambda e, bank=bank: e.activation(cdec, b4(bank[:, 0:256]), AF.Exp), reads=[bb], writes=[bcdec])

        ringc = [0]
        hb8 = lambda ap, g: ap[:, 8 * g:8 * g + 8].rearrange("p (h o) -> p h o", o=1).to_broadcast([128, 8, 64])

        def gen_Z(g):
            zs, bzs = self.m_zs2[g % 2]
            hS, bhS = self.m_h2[g % 2]
            if first_tile:
                kb.op("dve", lambda e: e.memset(hS, 0.0), writes=[bhS])
            else:
                kb.dma("pool", hS, self.st_d[l][:, 512 * g:512 * g + 512], f"stin{l}_{g % 2}", writes=[bhS])
            slot, bs = self.wget()
            sv = slot.rearrange("p (kc n) -> p kc n", n=512)
            for sub in range(4):
                bank, bb = self.next_bank("P")
                self.mm(bb, [(bank, self.uT[:, kc, sub * 128:(sub + 1) * 128], sv[:, kc, :]) for kc in range(KC)], reads=[bs, self.Bu])
                kb.op("act", lambda e, bank=bank, sub=sub: e.activation(zs[:, sub, :], bank, AF.Silu), reads=[bb], writes=[bzs])
                yield

        def gen_X(g):
            xc, bxc = self.m_xc2[g % 2]
            ring = ringc[0]
            slot, bs = self.wget()
            sv = slot.rearrange("p (kc n) -> p kc n", n=512)
            for c in range(4):
                ring = ringc[0]
                bank, bb = self.next_bank("P")
                self.mm(bb, [(bank, sv[:, kc, c * 128:(c + 1) * 128], self.uT[:, kc, :]) for kc in range(KC)], reads=[bs, self.Bu])
                y_, by_ = self.conv_chunk(bank, bb, tail, btail, 4 * g + c, wkeys, f"mcb{j}", 4, ring)
                ring += 1
                kb.op("act", lambda e, y_=y_, c=c: e.activation(xc[:, c, :], y_, AF.Silu), reads=[by_], writes=[bxc])
                ringc[0] = ring
                yield
            ringc[0] = ring

        def gen_BC(g):
            BT, bBT = self.m_BT2[g % 2]
            CT, bCT = self.m_CT2[g % 2]
            ring = ringc[0]
            slot, bs = self.wget()
            sv = slot[:, 0:4096].rearrange("p (c kc n) -> p c kc n", c=2, n=128)
            for c, (dstT, bdstT, ci) in enumerate([(BT, bBT, 32 + g), (CT, bCT, 40 + g)]):
                ring = ringc[0]
                bank, bb = self.next_bank("P")
                self.mm(bb, [(bank, sv[:, c, kc, :], self.uT[:, kc, :]) for kc in range(KC)], reads=[bs, self.Bu])
                y_, by_ = self.conv_chunk(bank, bb, tail, btail, ci, wkeys, f"mcb{j}", 4, ring)
                ring += 1
                kb.op("act", lambda e, y_=y_, dstT=dstT: e.activation(dstT, y_, AF.Silu), reads=[by_], writes=[bdstT])
                ringc[0] = ring
                yield
            ringc[0] = ring

        def c_front(g, s_, B):
            xc, bxc = self.m_xc2[g % 2]
            BT, bBT = self.m_BT2[g % 2]
            CT, bCT = self.m_CT2[g % 2]
            xtm, bxtm = B["xtm"]; X, bX = B["X"]; Xh, bXh = B["Xh"]; Btm, bBtm = B["Btm"]
            CBm, bCBm = B["CBm"]; Lh, bLh = B["Lh"]; E, bE = B["E"]; M, bM = B["M"]
            tc = slice(128 * s_, 128 * s_ + 128)
            bank, bb = self.next_bank("S")
            bkb = bank.bitcast(BF16)
            self.tr_multi(bb, [(bkb[:, c * 128:(c + 1) * 128], xc[:, c, tc]) for c in range(4)] + [(bkb[:, 512:640], BT[:, tc])], identb, reads=[bxc, bBT, self.Bcb])
            yield
            kb.op("act", lambda e: e.copy(xtm, bkb[:, 0:512]), reads=[bb], writes=[bxtm])
            kb.op("act", lambda e: e.copy(Btm, bkb[:, 512:640]), reads=[bb], writes=[bBtm])
            yield
            x3 = xtm.rearrange("p (h q) -> p h q", q=64)
            kb.op("dve", lambda e: e.tensor_tensor(X, x3, hb8(dt[:, s_, :], g), ALU.mult), reads=[bxtm, bdt], writes=[bX])
            kb.op("dve", lambda e: e.tensor_tensor(Xh, X, hb8(dst[:, s_, :], g), ALU.mult), reads=[bX, bdst], writes=[bXh])
            kb.op("dve", lambda e: e.tensor_tensor(Lh, MS.rearrange("p (o t) -> p o t", o=1).to_broadcast([128, 8, 128]),
                                                     dtA[:, s_, 8 * g:8 * g + 8].rearrange("p (h o) -> p h o", o=1).to_broadcast([128, 8, 128]), ALU.mult),
                  reads=[bdtA, self.Bcf], writes=[bLh])
            yield
            bankc, bbc = self.next_bank("S")
            self.mm(bbc, [(bankc[:, 0:128], BT[:, tc], CT[:, tc])], reads=[bBT, bCT])
            yield
            kb.op("dve", lambda e: e.tensor_tensor(CBm, bankc[:, 0:128], Uf, ALU.mult), reads=[bbc, self.Bcf], writes=[bCBm])
            yield
            dbanks = []
            for half in range(2):
                bank2, bb2 = self.next_bank("S")
                self.mm_multi(bb2, [[(bank2[:, q * 128:(q + 1) * 128], Lh[:, 4 * half + q, :], Uf)] for q in range(4)], reads=[bLh, self.Bcf])
                dbanks.append((bank2, bb2))
            yield
            for half, (bank2, bb2) in enumerate(dbanks):
                kb.op("act", lambda e, bank2=bank2, half=half: e.activation(E[:, 4 * half:4 * half + 4, :], bank2.rearrange("p (h t) -> p h t", t=128), AF.Exp), reads=[bb2], writes=[bE])
            yield
            kb.op("dve", lambda e: e.tensor_tensor(M, E, CBm.rearrange("p (o t) -> p o t", o=1).to_broadcast([128, 8, 128]), ALU.mult), reads=[bE, bCBm], writes=[bM])
            yield

        def c_mid(g, s_, B):
            CT, bCT = self.m_CT2[g % 2]
            hS, bhS = self.m_h2[g % 2]
            X, bX = B["X"]; Xh, bXh = B["Xh"]; Btm, bBtm = B["Btm"]; M, bM = B["M"]; t1, bt1 = B["t1"]
            tc = slice(128 * s_, 128 * s_ + 128)
            bankd, bbd = self.next_bank("S")
            self.mm_multi(bbd, [[(bankd[:, q * 64:(q + 1) * 64], M[:, q, :], X[:, q, :])] for q in range(8)], reads=[bM, bX])
            banko, bbo = self.next_bank("S")
            self.mm(bbo, [(banko, CT[:, tc], hbf)], reads=[bCT, bhbf])
            yield
            t13 = t1.rearrange("p (h q) -> p h q", q=64)
            kb.op("dve", lambda e: e.tensor_tensor(t13, banko.rearrange("p (h q) -> p h q", q=64), hb8(eacs[:, s_, :], g), ALU.mult), reads=[bbo, beacs], writes=[bt1])
            kb.op("dve", lambda e: e.tensor_tensor(t1, t1, bankd, ALU.add), reads=[bt1, bbd], writes=[bt1])
            yield
            banks_, bbs = self.next_bank("S")
            self.mm(bbs, [(banks_, Btm, Xh.rearrange("p h q -> p (h q)"))], reads=[bBtm, bXh])
            yield
            h3 = hS.rearrange("p (h q) -> p h q", q=64)
            kb.op("dve", lambda e: e.tensor_tensor(h3, h3, hb8(cdec[:, s_, :], g), ALU.mult), reads=[bhS, bcdec], writes=[bhS])
            kb.op("dve", lambda e: e.tensor_tensor(hS, hS, banks_, ALU.add), reads=[bhS, bbs], writes=[bhS])
            if s_ < 3:
                kb.op("act", lambda e: e.copy(hbf, hS), reads=[bhS], writes=[bhbf])
            yield

        def c_tail(g, s_, B):
            yT, byT = self.m_yT[g % 2]
            zs, bzs = self.m_zs2[g % 2]
            xtm, bxtm = B["xtm"]; t1, bt1 = B["t1"]; t2, bt2 = B["t2"]; ygn, bygn = B["ygn"]; ss, bss = B["ss"]
            tc = slice(128 * s_, 128 * s_ + 128)
            x3 = xtm.rearrange("p (h q) -> p h q", q=64)
            t23 = t2.rearrange("p (h q) -> p h q", q=64)
            kb.op("dve", lambda e: e.tensor_tensor(t23, x3, hb8(dsk, g), ALU.mult), reads=[bxtm, self.Bpc], writes=[bt2])
            kb.op("dve", lambda e: e.tensor_tensor(t1, t1, t2, ALU.add), reads=[bt1, bt2], writes=[bt1])
            kb.op("dve", lambda e: e.tensor_tensor(t1, t1, zs[:, s_, :], ALU.mult), reads=[bt1, bzs], writes=[bt1])
            yield
            kb.op("act", lambda e: e.activation(t2, t1, AF.Square, accum_out=ss[:, 0:1]), reads=[bt1], writes=[bt2, bss])
            kb.op("act", lambda e: e.activation(ss[:, 1:2], ss[:, 0:1], AF.Ln, bias=EPS, scale=1.0 / 512), reads=[bss], writes=[bss])
            kb.op("act", lambda e: e.activation(ss[:, 2:3], ss[:, 1:2], AF.Exp, scale=-0.5), reads=[bss], writes=[bss])
            yield
            kb.op("dve", lambda e: e.tensor_scalar(ygn, t1, ss[:, 2:3], None, op0=ALU.mult), reads=[bt1, bss], writes=[bygn])
            yield
            bank, bb = self.next_bank("S")
            bkb = bank.bitcast(BF16)
            self.tr_multi(bb, [(bkb[:, c * 128:(c + 1) * 128], ygn[:, c * 128:(c + 1) * 128]) for c in range(4)], identb, reads=[bygn, self.Bcb])
            yield
            o_, _n = self.pco[f"mnw{j}"]
            kb.op("dve", lambda e: e.tensor_tensor(yT[:, :, tc], bkb[:, 0:512].rearrange("p (c t) -> p c t", t=128),
                                                     self.pc[:, o_ + 4 * g:o_ + 4 * g + 4].rearrange("p (c o) -> p c o", o=1).to_broadcast([128, 4, 128]), ALU.mult),
                  reads=[bb, self.Bpc], writes=[byT])
            yield

        def chain(gens):
            for g_ in gens:
                yield from g_

        def drain(gen):
            for _ in gen:
                pass

        def rr2(mains, bg):
            mains = list(mains)
            while mains:
                for gen in list(mains):
                    try:
                        next(gen)
                    except StopIteration:
                        mains.remove(gen)
                try:
                    next(bg)
                except StopIteration:
                    pass

        def do_pair(g, pair, bg):
            hS, bhS = self.m_h2[g % 2]
            sa, sb = 2 * pair, 2 * pair + 1
            Ba, Bb = self.m_cb[0], self.m_cb[1]
            if pair == 0:
                kb.op("act", lambda e: e.copy(hbf, hS), reads=[bhS], writes=[bhbf])
            rr2([c_front(g, sa, Ba), c_front(g, sb, Bb)], bg)
            rr2([chain([c_mid(g, sa, Ba), c_mid(g, sb, Bb)])], bg)
            rr2([c_tail(g, sa, Ba), c_tail(g, sb, Bb)], bg)
            if pair == 1:
                self._dma_toks.append(kb.dma("pool", self.st_d[l][:, 512 * g:512 * g + 512], hS, f"stout{l}_{g % 2}", reads=[bhS]))

        def gen_O(g):
            yT, byT = self.m_yT[g % 2]
            slot, bs = self.wget()
            sv = slot.rearrange("p (c n) -> p c n", n=D)
            for m in range(KC):
                bank, bb = self.next_bank("P")
                self.mm(bb, [(bank, sv[:, c, m * 128:(m + 1) * 128], yT[:, c, :]) for c in range(4)], reads=[bs, byT])
                kb.op("dve", lambda e, m=m, bank=bank: e.tensor_tensor(self.hT[:, m, :], self.hT[:, m, :], bank, ALU.add), reads=[bb, self.Bh[m]], writes=[self.Bh[m]])
                yield

        drain(gen_Z(0))
        drain(gen_X(0))
        drain(gen_BC(0))
        for g in range(8):
            bgA = chain([gen_Z(g + 1), gen_X(g + 1)] if g < 7 else [])
            do_pair(g, 0, bgA)
            drain(bgA)
            bgB = chain(([gen_BC(g + 1)] if g < 7 else []) + ([gen_O(g - 1)] if g >= 1 else []))
            do_pair(g, 1, bgB)
            drain(bgB)
        drain(gen_O(7))


_CACHE = {}


def build_nc(inputs_offs, **kw):
    pc_offs, cc_offs = inputs_offs
    p = Prog(**kw)
    nc = p.build(pc_offs, cc_offs)
    return nc


def kernel(**inputs):
    inp = {k: np.asarray(v) for k, v in inputs.items()}
    pcols, pc_offs = pack_params(inp)
    consts, cc_offs = make_consts()
    nc = build_nc((pc_offs, cc_offs))
    x = np.ascontiguousarray(inp["x"], dtype=np.float32)
    in_maps = []
    for c in range(8):
        m = {"x": x[2 * c:2 * c + 2], "pcols": pcols, "consts": consts}
        for k in ("hgrn_w_in", "hgrn_w_out", "m_w_in", "m_w_out", "f_w_up", "f_w_down"):
            m[k] = np.ascontiguousarray(inp[k], dtype=np.float32)
        in_maps.append(m)
    res = run_bass_kernel_spmd(nc, in_maps, core_ids=list(range(8)))
    return np.concatenate([np.asarray(r["out"]) for r in res.results], axis=0).astype(np.float32)
```

```python
import numpy as np
import concourse.bass as bass
import concourse.mybir as mybir
from concourse.bass_utils import run_bass_kernel_spmd

F32 = mybir.dt.float32
BF16 = mybir.dt.bfloat16
ALU = mybir.AluOpType
AF = mybir.ActivationFunctionType

D = 2048
KC = 16
T = 512
NSUB = T // 128
SEQ = 2048
EPS = 1e-5
DFF = 5632
M_DI = 4096
M_CONVD = 6144
M_IN = 10304


class Buf:
    __slots__ = ("name", "w", "r")

    def __init__(self, name):
        self.name = name
        self.w = None
        self.r = {}


class KB:
    SEM_ROLL = 30000

    def __init__(self, nc):
        self.nc = nc
        self.engs = {"pe": nc.tensor, "dve": nc.vector, "act": nc.scalar,
                     "pool": nc.gpsimd, "sp": nc.sync}
        self.sem = {}
        self.cnt = {}
        self.nsem = 0
        self.seen = {e: {} for e in self.engs}
        self.dsem = {}
        self.nwait = 0
        self.nins = {e: 0 for e in self.engs}

    def _newsem(self, name):
        self.nsem += 1
        return self.nc.alloc_semaphore(f"{name}_{self.nsem}")

    def _bump(self, eng):
        if eng not in self.sem or self.cnt[eng] >= self.SEM_ROLL:
            self.sem[eng] = self._newsem("s_" + eng)
            self.cnt[eng] = 0
        self.cnt[eng] += 1
        return (id(self.sem[eng]), self.sem[eng], self.cnt[eng], eng)

    def _wait(self, eng, deps):
        h = self.engs[eng]
        seen = self.seen[eng]
        for tok in deps:
            k, sem, val, src = tok
            if eng == "pe" and src == "pe":
                continue
            if seen.get(k, 0) >= val:
                continue
            h.wait_ge(sem, val)
            self.nwait += 1
            seen[k] = val

    @staticmethod
    def _deps(reads, writes):
        deps = []
        for b in reads:
            if b.w is not None:
                deps.append(b.w)
        for b in writes:
            if b.w is not None:
                deps.append(b.w)
            deps.extend(b.r.values())
        return deps

    def op(self, eng, fn, reads=(), writes=()):
        self._wait(eng, self._deps(reads, writes))
        ins = fn(self.engs[eng])
        self.nins[eng] += 1
        tok = self._bump(eng)
        ins.then_inc(tok[1], 1)
        for b in reads:
            b.r[eng] = tok
        for b in writes:
            b.w = tok
            b.r = {}
        return ins

    def dma(self, q, out, in_, key, reads=(), writes=()):
        self._wait(q, self._deps(reads, writes))
        if key not in self.dsem:
            self.dsem[key] = [self._newsem("d_" + key), 0]
        ent = self.dsem[key]
        ent[1] += 16
        ins = self.engs[q].dma_start(out=out, in_=in_)
        ins.then_inc(ent[0], 16)
        self.nins[q] += 1
        tok = (id(ent[0]), ent[0], ent[1], "dma:" + key)
        for b in reads:
            b.r["dma:" + key] = tok
        for b in writes:
            b.w = tok
            b.r = {}
        return tok

    def wait_all(self, eng, bufs):
        deps = []
        for b in bufs:
            if b.w is not None:
                deps.append(b.w)
            deps.extend(b.r.values())
        self._wait(eng, deps)


def _cols(v):
    v = np.asarray(v, np.float32)
    return np.ascontiguousarray(v.reshape(-1, 128).T)


def _pack(named):
    offs = {}
    parts = []
    o = 0
    for k, a in named:
        a = np.asarray(a, np.float32).reshape(128, -1)
        offs[k] = (o, a.shape[1])
        parts.append(a)
        o += a.shape[1]
    return np.ascontiguousarray(np.concatenate(parts, axis=1)), offs


def pack_params(inp):
    named = []
    for l in range(4):
        named.append((f"mixn{l}", _cols(inp["mix_norm"][l])))
        named.append((f"ffnn{l}", _cols(inp["ffn_norm"][l])))
        fcw = inp["f_conv_w"][l]
        for k in range(3):
            named.append((f"fcw{l}_{k}", _cols(fcw[k])))
        named.append((f"fcb{l}", _cols(inp["f_conv_b"][l])))
    named.append(("finn", _cols(inp["final_norm"])))
    for j in range(2):
        named.append((f"lbl{j}", _cols(inp["hgrn_lb_logits"][j])))
        named.append((f"gn{j}", _cols(inp["hgrn_gnorm"][j])))
        for k in range(4):
            named.append((f"mcw{j}_{k}", _cols(inp["m_conv_w"][j][k])))
        named.append((f"mcb{j}", _cols(inp["m_conv_b"][j])))
        named.append((f"mnw{j}", _cols(inp["m_norm"][j])))
        named.append((f"dtb{j}", np.broadcast_to(np.asarray(inp["m_dt_bias"][j], np.float32)[None, :], (128, 64))))
        named.append((f"alog{j}", np.broadcast_to(np.asarray(inp["m_A_log"][j], np.float32)[None, :], (128, 64))))
        named.append((f"dsk{j}", np.broadcast_to(np.asarray(inp["m_D"][j], np.float32)[None, :], (128, 64))))
    return _pack(named)


def make_consts():
    i = np.arange(128)
    ident = np.eye(128, dtype=np.float32)
    U = (i[:, None] <= i[None, :]).astype(np.float32)
    MS = (i[:, None] > i[None, :]).astype(np.float32)
    ones = np.ones((128, 128), np.float32)
    return _pack([("ident", ident), ("U", U), ("MS", MS), ("ones", ones)])


class Prog:
    def __init__(self, n_seq=2, tiles_per_seq=4, stages=None, final=True, nslots=3):
        self.n_seq = n_seq
        self.tiles_per_seq = tiles_per_seq
        self.final = final
        if stages is None:
            stages = []
            for l in range(4):
                stages.append(("hgrn" if l % 2 == 0 else "mamba", l))
                stages.append(("ffn", l))
        self.stages = stages
        self.nslots = nslots
        nc = bass.Bass("TRN2", target_bir_lowering=False)
        self.nc = nc
        self.kb = KB(nc)
        self._n = 0
        self._dma_toks = []
        self._scr_off = 0
        self.bank_i = 0
        self.bank_p = 0
        self.bank_s = 0

    def T_(self, name, shape, dt):
        t = self.nc.alloc_sbuf_tensor(name, list(shape), dt).ap()
        return t, Buf(name)

    def carve_reset(self):
        self._scr_off = 0

    def carve(self, name, shape, dt):
        esz = 4 if dt == F32 else 2
        n = int(np.prod(shape[1:]))
        nbytes = (n * esz + 31) // 32 * 32
        off = self._scr_off
        self._scr_off += nbytes
        assert self._scr_off <= self.SCR, (name, self._scr_off)
        v = self.scr[0:shape[0], off // 2: off // 2 + n * esz // 2]
        if dt == F32:
            v = v.bitcast(F32)
        if len(shape) == 3:
            v = v.rearrange("p (a b) -> p a b", b=shape[2])
        return v, Buf(name)

    def barrier(self):
        kb = self.kb
        engs = ["pe", "act", "dve", "pool"]
        toks = []
        for e in engs:
            if e in kb.sem:
                toks.append((id(kb.sem[e]), kb.sem[e], kb.cnt[e], e))
        toks.extend(self._dma_toks)
        self._dma_toks = []
        for e in engs:
            h = kb.engs[e]
            for (k, sem, val, src) in toks:
                if src == e or kb.seen[e].get(k, 0) >= val:
                    continue
                h.wait_ge(sem, val)
                kb.nwait += 1
                kb.seen[e][k] = val

    def next_bank(self, pool="A"):
        if pool == "A":
            i = self.bank_i % 8
            self.bank_i += 1
        elif pool == "P":
            i = self.bank_p % 4
            self.bank_p += 1
        else:
            i = 4 + self.bank_s % 4
            self.bank_s += 1
        return self.banks[i], self.bbufs[i]

    def mm(self, bank_buf, pairs, reads, first_start=True, last_stop=True):
        n = len(pairs)

        def fn(e):
            ins = None
            for i, (o, l, r) in enumerate(pairs):
                ins = e.matmul(o, l, r, start=(i == 0 and first_start), stop=(i == n - 1 and last_stop))
            return ins
        self.kb.nins["pe"] += n - 1
        assert bank_buf.w is None or bank_buf.r, bank_buf.name
        return self.kb.op("pe", fn, reads=reads, writes=[bank_buf])

    def mm_multi(self, bank_buf, groups, reads):
        def fn(e):
            ins = None
            for g in groups:
                n = len(g)
                for i, (o, l, r) in enumerate(g):
                    ins = e.matmul(o, l, r, start=(i == 0), stop=(i == n - 1))
            return ins
        self.kb.nins["pe"] += sum(len(g) for g in groups) - 1
        assert bank_buf.w is None or bank_buf.r, bank_buf.name
        return self.kb.op("pe", fn, reads=reads, writes=[bank_buf])

    def tr_multi(self, bank_buf, items, ident, reads):
        def fn(e):
            ins = None
            for (o, i_) in items:
                if i_.dtype == F32:
                    ins = e.matmul(o, i_, ident, start=True, stop=True)
                else:
                    ins = e.transpose(o, i_, ident)
            return ins
        self.kb.nins["pe"] += len(items) - 1
        assert bank_buf.w is None or bank_buf.r, bank_buf.name
        return self.kb.op("pe", fn, reads=reads, writes=[bank_buf])

    def build(self, pc_offs, cc_offs):
        nc, kb = self.nc, self.kb
        self.pco = pc_offs
        self.cco = cc_offs
        npc = max(o + n for o, n in pc_offs.values())
        ncc = max(o + n for o, n in cc_offs.values())
        ntok = self.tiles_per_seq * T
        dr = lambda name, shape: nc.dram_tensor(name, list(shape), F32, kind="ExternalInput").ap()
        self.x = dr("x", [self.n_seq, SEQ, D])
        self.pcols_d = dr("pcols", [128, npc])
        self.consts_d = dr("consts", [128, ncc])
        kinds = {k for k, _ in self.stages}
        wshapes = {"hgrn": {"hgrn_w_in": [2, D, 8192], "hgrn_w_out": [2, D, D]},
                   "mamba": {"m_w_in": [2, D, M_IN], "m_w_out": [2, M_DI, D]},
                   "ffn": {"f_w_up": [4, D, 2 * DFF], "f_w_down": [4, DFF, D]}}
        self.W = {}
        for k in ("hgrn", "mamba", "ffn"):
            if k in kinds:
                for nm, shp in wshapes[k].items():
                    self.W[nm] = dr(nm, shp)
        self.out = nc.dram_tensor("out", [self.n_seq, SEQ, D], F32, kind="ExternalOutput").ap()

        self.pc, self.Bpc = self.T_("pc", [128, npc], F32)
        self.cf, self.Bcf = self.T_("cf", [128, ncc], F32)
        self.cb, self.Bcb = self.T_("cb", [128, ncc], BF16)
        kb.dma("sp", self.pc, self.pcols_d, "pc", writes=[self.Bpc])
        kb.dma("sp", self.cf, self.consts_d, "cf", writes=[self.Bcf])
        kb.dma("pool", self.cb, self.consts_d, "cb", writes=[self.Bcb])

        self.banks = [nc.alloc_psum_tensor(f"bank{i}", [128, 512], F32).ap() for i in range(8)]
        self.bbufs = [Buf(f"bank{i}") for i in range(8)]

        self.hT, _ = self.T_("hT", [128, KC, T], F32)
        self.Bh = [Buf(f"h{m}") for m in range(KC)]
        self.uT, self.Bu = self.T_("uT", [128, KC, T], BF16)
        self.slots = [self.T_(f"wslot{i}", [128, 8192], BF16) for i in range(self.nslots)]
        self.plan = []
        self.issued = 0
        self.consumed = 0
        self.sqring = [self.T_(f"sqr{i}", [128, T], BF16) for i in range(4)]
        self.lnv, self.Blnv = self.T_("lnv", [128, T], F32)
        self.rstd, self.Brstd = self.T_("rstd", [128, T], F32)
        self.state_bufs = []
        self.ftail = []
        for l in range(4):
            t_, b_ = self.T_(f"ftail{l}", [128, 88, 2], F32)
            self.ftail.append((t_, b_))
            self.state_bufs.append((t_, b_))
        self.mtail = []
        for j in range(2):
            t_, b_ = self.T_(f"mtail{j}", [128, 48, 3], F32)
            self.mtail.append((t_, b_))
            self.state_bufs.append((t_, b_))
        self.SCR = 86 * 1024
        self.scr = nc.alloc_sbuf_tensor("scr", [128, self.SCR // 2], BF16).ap()
        self.st_d = {}
        for (kind, l) in self.stages:
            if kind == "hgrn":
                self.st_d[l] = nc.dram_tensor(f"st{l}", [128, 16 * 128], F32).ap()
            if kind == "mamba":
                self.st_d[l] = nc.dram_tensor(f"st{l}", [128, 64 * 64], F32).ap()
        print("sbuf bytes remaining/partition:", nc.sbuf_bytes_remaining)

        self.uniq = {}
        for s in range(self.n_seq):
            for t in range(self.tiles_per_seq):
                for si, (kind, l) in enumerate(self.stages):
                    self.cur_stage = si
                    getattr(self, "plan_" + kind)(l)
        print("weight slabs:", len(self.plan), "unique:", len(self.uniq))
        self.prologue_init()

        self.prep_params()
        ti = 0
        nst = len(self.stages)
        for s in range(self.n_seq):
            self.reset_state()
            for t in range(self.tiles_per_seq):
                self.first_tile = (t == 0)
                self.barrier()
                self.load_tile(s, t, ti)
                if ti == 0:
                    self.prologue_stage(0)
                for si, (kind, l) in enumerate(self.stages):
                    self.rmsnorm(("ffnn%d" if kind == "ffn" else "mixn%d") % l)
                    self.barrier()
                    if ti == 0 and si + 1 < nst:
                        self.prologue_stage(si + 1)
                    getattr(self, "emit_" + kind)(l)
                self.barrier()
                self.store_tile(s, t, ti)
                ti += 1
        assert self.consumed == len(self.plan), (self.consumed, len(self.plan))
        kb._wait("pool", self._dma_toks)
        print("instructions:", kb.nins, "waits:", kb.nwait, "sems:", kb.nsem)
        return nc

    def pcol(self, key, i=None):
        o, n = self.pco[key]
        if i is None:
            return self.pc[:, o:o + n]
        return self.pc[:, o + i:o + i + 1]

    def cst(self, key, bf=False):
        o, n = self.cco[key]
        return (self.cb if bf else self.cf)[:, o:o + n]

    def padd(self, key, nel, pieces):
        if key not in self.uniq:
            self.uniq[key] = (len(self.uniq), nel, pieces, self.cur_stage)
        self.plan.append(key)

    def prologue_init(self):
        nu = len(self.uniq)
        NPT = 96
        wts = [self.nc.dram_tensor(f"wbf{i}", [min(NPT, nu - i * NPT), 128, 8192], BF16).ap() for i in range((nu + NPT - 1) // NPT)]

        class _W:
            def __getitem__(_s, u):
                return wts[u // NPT][u % NPT]
        self.wbf = _W()
        self.wbuf = {}

    def prologue_stage(self, st):
        kb = self.kb
        bufs = []
        for key, (u, nel, pieces, st_) in self.uniq.items():
            if st_ != st:
                continue
            for (dst_fn, src) in pieces:
                kb.dma("pool", dst_fn(self.wbf[u]), src, f"pro{st}")
            b = Buf(f"wbf{u}")
            self.wbuf[key] = b
            bufs.append(b)
        if bufs:
            ent = kb.dsem[f"pro{st}"]
            tok = (id(ent[0]), ent[0], ent[1], f"dma:pro{st}")
            for b in bufs:
                b.w = tok

    def wget(self):
        kb = self.kb
        while self.issued < len(self.plan) and self.issued < self.consumed + self.nslots:
            i = self.issued
            key = self.plan[i]
            u, nel, _, _ = self.uniq[key]
            slot, b = self.slots[i % self.nslots]
            kb.dma("sp", slot[:, 0:nel], self.wbf[u][:, 0:nel], f"ws{i % self.nslots}", reads=[self.wbuf[key]], writes=[b])
            self.issued += 1
        slot, b = self.slots[self.consumed % self.nslots]
        self.consumed += 1
        return slot, b

    @staticmethod
    def wsrc(w2d, c0, n):
        return w2d.rearrange("(kc p) n -> p kc n", p=128)[:, :, c0:c0 + n]

    @staticmethod
    def wdst(off, kc, n):
        return lambda slot: slot[:, off:off + kc * n].rearrange("p (kc n) -> p kc n", n=n)

    def load_tile(self, s, t, ti):
        kb = self.kb
        ident = self.cst("ident")
        self.carve_reset()
        self.xin = [self.carve(f"xin{i}", [128, D], F32) for i in range(2)]
        for sub in range(NSUB):
            xin, bx = self.xin[sub % 2]
            r0 = t * T + sub * 128
            kb.dma("pool", xin, self.x[s, r0:r0 + 128, :], f"xin{sub % 2}", writes=[bx])
            for q in range(4):
                bank, bb = self.next_bank()
                items = [(bank[:, j * 128:(j + 1) * 128], xin[:, (4 * q + j) * 128:(4 * q + j + 1) * 128]) for j in range(4)]
                self.tr_multi(bb, items, ident, reads=[bx, self.Bcf])
                dst = self.hT[:, 4 * q:4 * q + 4, sub * 128:(sub + 1) * 128]
                src = bank.rearrange("p (j t) -> p j t", t=128)
                eng = "act" if q % 2 == 0 else "dve"
                if eng == "act":
                    kb.op("act", lambda e, d=dst, s_=src: e.copy(d, s_), reads=[bb], writes=self.Bh[4 * q:4 * q + 4])
                else:
                    kb.op("dve", lambda e, d=dst, s_=src: e.tensor_copy(d, s_), reads=[bb], writes=self.Bh[4 * q:4 * q + 4])

    def store_tile(self, s, t, ti):
        kb = self.kb
        ident = self.cst("ident")
        self.carve_reset()
        self.xin = [self.carve(f"xout{i}", [128, D], F32) for i in range(2)]
        if self.final:
            self.rms_stats(self.Bh)
            o, _ = self.pco["finn"]
            for m in range(KC):
                kb.op("dve", lambda e, m=m: e.scalar_tensor_tensor(self.hT[:, m, :], self.hT[:, m, :], self.pc[:, o + m:o + m + 1], self.rstd, op0=ALU.mult, op1=ALU.mult),
                      reads=[self.Bh[m], self.Brstd, self.Bpc], writes=[self.Bh[m]])
        for sub in range(NSUB):
            xo, bx = self.xin[sub % 2]
            for q in range(4):
                bank, bb = self.next_bank()
                items = [(bank[:, j * 128:(j + 1) * 128], self.hT[:, 4 * q + j, sub * 128:(sub + 1) * 128]) for j in range(4)]
                self.tr_multi(bb, items, ident, reads=self.Bh[4 * q:4 * q + 4] + [self.Bcf])
                if q % 2 == 0:
                    kb.op("act", lambda e, q=q, bank=bank, xo=xo: e.copy(xo[:, q * 512:(q + 1) * 512], bank), reads=[bb], writes=[bx])
                else:
                    kb.op("dve", lambda e, q=q, bank=bank, xo=xo: e.tensor_copy(xo[:, q * 512:(q + 1) * 512], bank), reads=[bb], writes=[bx])
            r0 = t * T + sub * 128
            self._dma_toks.append(kb.dma("pool", self.out[s, r0:r0 + 128, :], xo, f"xout{sub % 2}", reads=[bx]))

    def rms_stats(self, hbufs):
        kb = self.kb
        ones = self.cst("ones", bf=True)
        bank, bb = self.next_bank()
        for m in range(KC):
            sqt, bsq = self.sqring[m % len(self.sqring)]
            if m % 3 == 0:
                kb.op("act", lambda e, m=m, sqt=sqt: e.activation(sqt, self.hT[:, m, :], AF.Square), reads=[hbufs[m]], writes=[bsq])
            else:
                kb.op("pool" if m % 3 == 1 else "dve", lambda e, m=m, sqt=sqt: e.tensor_tensor(sqt, self.hT[:, m, :], self.hT[:, m, :], ALU.mult), reads=[hbufs[m]], writes=[bsq])
            ins_first = (m == 0)
            self.kb._wait("pe", kb._deps([bsq, self.Bcb], [bb] if m == 0 else []))
            ins = self.nc.tensor.matmul(bank, ones, sqt, start=(m == 0), stop=(m == KC - 1))
            kb.nins["pe"] += 1
            tok = kb._bump("pe")
            ins.then_inc(tok[1], 1)
            bsq.r["pe"] = tok
            if m == KC - 1:
                bb.w = tok
                bb.r = {}
        kb.op("act", lambda e: e.activation(self.lnv, bank, AF.Ln, bias=EPS, scale=1.0 / D), reads=[bb], writes=[self.Blnv])
        kb.op("act", lambda e: e.activation(self.rstd, self.lnv, AF.Exp, scale=-0.5), reads=[self.Blnv], writes=[self.Brstd])

    def rmsnorm(self, wkey):
        kb = self.kb
        self.rms_stats(self.Bh)
        o, _ = self.pco[wkey]
        for m in range(KC):
            kb.op("dve", lambda e, m=m: e.scalar_tensor_tensor(self.uT[:, m, :], self.hT[:, m, :], self.pc[:, o + m:o + m + 1], self.rstd, op0=ALU.mult, op1=ALU.mult),
                  reads=[self.Bh[m], self.Brstd, self.Bpc], writes=[self.Bu])

    def prep_params(self):
        kb = self.kb
        self.lb = []
        l0 = self.pcol("lbl0")
        l1 = self.pcol("lbl1")
        d01, bd01 = self.T_("d01", [128, 16], F32)
        p0, bp0 = self.T_("p0", [128, 16], F32)
        p1, bp1 = self.T_("p1", [128, 16], F32)
        kb.op("dve", lambda e: e.tensor_tensor(d01, l0, l1, ALU.subtract), reads=[self.Bpc], writes=[bd01])
        kb.op("act", lambda e: e.activation(p0, d01, AF.Sigmoid), reads=[bd01], writes=[bp0])
        kb.op("act", lambda e: e.activation(p1, d01, AF.Sigmoid, scale=-1.0), reads=[bd01], writes=[bp1])
        for j in range(2):
            lb, blb = self.T_(f"lb{j}", [128, 16], F32)
            oml, boml = self.T_(f"oml{j}", [128, 16], F32)
            noml, bnoml = self.T_(f"noml{j}", [128, 16], F32)
            if j == 0:
                kb.op("dve", lambda e, lb=lb: e.tensor_tensor(lb, p0, p0, ALU.subtract), reads=[bp0], writes=[blb])
            else:
                kb.op("dve", lambda e, lb=lb: e.tensor_tensor(lb, p0, p1, ALU.add), reads=[bp0, bp1], writes=[blb])
                kb.op("dve", lambda e, lb=lb: e.tensor_tensor(lb, lb, p0, ALU.subtract), reads=[bp0, blb], writes=[blb])
            kb.op("dve", lambda e, lb=lb, noml=noml: e.tensor_scalar(noml, lb, 1.0, None, op0=ALU.subtract), reads=[blb], writes=[bnoml])
            kb.op("dve", lambda e, oml=oml, noml=noml: e.tensor_scalar(oml, noml, -1.0, None, op0=ALU.mult), reads=[bnoml], writes=[boml])
            self.lb.append((lb, blb, oml, boml, noml, bnoml))
        self.Arow = []
        for j in range(2):
            a, ba = self.T_(f"Arow{j}", [128, 64], F32)
            kb.op("act", lambda e, a=a, j=j: e.activation(a, self.pcol(f"alog{j}"), AF.Exp), reads=[self.Bpc], writes=[ba])
            kb.op("dve", lambda e, a=a: e.tensor_scalar(a, a, -1.0, None, op0=ALU.mult), reads=[ba], writes=[ba])
            self.Arow.append((a, ba))

    def reset_state(self):
        kb = self.kb
        for (t, b) in self.state_bufs:
            kb.op("dve", lambda e, t=t: e.memset(t, 0.0), writes=[b])

    def alloc_ffn(self):
        self.carve_reset()
        self.xs = [self.carve(f"xs{i}", [128, T + 4], F32) for i in range(3)]
        self.yc = [self.carve(f"yc{i}", [128, T], F32) for i in range(4)]
        self.gs = [self.carve(f"gs{i}", [128, T], F32) for i in range(2)]
        self.aT = [self.carve(f"aT{i}", [128, 4, T], BF16) for i in range(2)]

    def plan_ffn(self, l):
        wu = self.W["f_w_up"][l]
        wd = self.W["f_w_down"][l]
        for grp in range(11):
            for s2 in range(2):
                s = 2 * grp + s2
                self.padd(("fu", l, s), 8192, [(self.wdst(0, 16, 256), self.wsrc(wu, 256 * s, 256)),
                                               (self.wdst(4096, 16, 256), self.wsrc(wu, DFF + 256 * s, 256))])
            self.padd(("fd", l, grp), 8192, [(lambda slot: slot.rearrange("p (c n) -> p c n", n=D),
                                              wd[512 * grp:512 * grp + 512, :].rearrange("(c p) n -> p c n", p=128))])

    def conv_chunk(self, bank, bb, tail, btail, ci, wkeys, bkey, ntap, ring_i):
        kb = self.kb
        nt = ntap - 1
        xs, bxs = self.xs[ring_i % len(self.xs)]
        yc, byc = self.yc[ring_i % len(self.yc)]
        kb.op("act", lambda e: e.copy(xs[:, nt:nt + T], bank), reads=[bb], writes=[bxs])
        kb.op("dve", lambda e: e.tensor_copy(xs[:, 0:nt], tail[:, ci, 0:nt]), reads=[btail], writes=[bxs])
        wl = self.pcol(wkeys[ntap - 1], ci)
        kb.op("act", lambda e: e.activation(yc, bank, AF.Identity, bias=self.pcol(bkey, ci), scale=wl), reads=[bb, self.Bpc], writes=[byc])
        for k in range(ntap - 1):
            wk = self.pcol(wkeys[k], ci)
            kb.op("dve", lambda e, k=k, wk=wk: e.scalar_tensor_tensor(yc, xs[:, k:k + T], wk, yc, op0=ALU.mult, op1=ALU.add),
                  reads=[bxs, byc, self.Bpc], writes=[byc])
        kb.op("dve", lambda e: e.tensor_copy(tail[:, ci, 0:nt], xs[:, T:T + nt]), reads=[bxs], writes=[btail])
        return yc, byc

    def emit_ffn(self, l):
        kb = self.kb
        self.alloc_ffn()
        tail, btail = self.ftail[l]
        wkeys = [f"fcw{l}_{k}" for k in range(3)]
        ring = 0
        for grp in range(11):
            aT, baT = self.aT[grp % 2]
            for s2 in range(2):
                s = 2 * grp + s2
                slot, bs = self.wget()
                sv = slot.rearrange("p (h kc n) -> p h kc n", h=2, n=256)
                ys = []
                for c in range(4):
                    bank, bb = self.next_bank()
                    self.mm(bb, [(bank, sv[:, c // 2, kc, (c % 2) * 128:(c % 2 + 1) * 128], self.uT[:, kc, :]) for kc in range(KC)], reads=[bs, self.Bu])
                    ci = (2 * s + c) if c < 2 else (44 + 2 * s + (c - 2))
                    ys.append(self.conv_chunk(bank, bb, tail, btail, ci, wkeys, f"fcb{l}", 3, ring))
                    ring += 1
                for c in range(2):
                    gsb, bgs = self.gs[c]
                    (yg, byg), (yu, byu) = ys[c], ys[2 + c]
                    kb.op("act", lambda e, gsb=gsb, yg=yg: e.activation(gsb, yg, AF.Silu), reads=[byg], writes=[bgs])
                    kb.op("dve", lambda e, gsb=gsb, yu=yu, c=c, s2=s2: e.tensor_tensor(aT[:, 2 * s2 + c, :], gsb, yu, ALU.mult), reads=[bgs, byu], writes=[baT])
            slot, bs = self.wget()
            sv = slot.rearrange("p (c n) -> p c n", n=D)
            for m in range(KC):
                bank, bb = self.next_bank()
                self.mm(bb, [(bank, sv[:, c, m * 128:(m + 1) * 128], aT[:, c, :]) for c in range(4)], reads=[bs, baT])
                kb.op("dve", lambda e, m=m, bank=bank: e.tensor_tensor(self.hT[:, m, :], self.hT[:, m, :], bank, ALU.add), reads=[bb, self.Bh[m]], writes=[self.Bh[m]])

    def alloc_hgrn(self):
        self.carve_reset()
        c = self.carve
        self.hS = c("hS", [128, 16, 128], F32)
        self.h_in = []
        for i in range(4):
            self.h_in.append({"q": c(f"h_q{i}", [128, T], F32), "sg": c(f"h_sg{i}", [128, T], F32),
                              "gs": c(f"h_gs{i}", [128, T], BF16), "vT": c(f"h_vT{i}", [128, T], BF16)})
        self.h_ones = c("h_ones", [128, T], F32)
        self.h_pb = []
        for i in range(2):
            d = {}
            d["k"] = c(f"h_k{i}", [128, T], F32)
            d["bp"] = c(f"h_bp{i}", [128, T + 8], F32)
            d["d1"] = c(f"h_d1{i}", [128, T], F32)
            d["d2"] = c(f"h_d2{i}", [128, T], F32)
            d["d3"] = c(f"h_d3{i}", [128, T], F32)
            for nm in ("qm", "km", "qa", "kl"):
                d[nm] = c(f"h_{nm}{i}", [128, T], BF16)
            d["vtm"] = c(f"h_vtm{i}", [64, 8, 128], BF16)
            d["kltm"] = c(f"h_kltm{i}", [64, 8, 128], BF16)
            d["A"] = c(f"h_A{i}", [64, 8, 64], BF16)
            d["Sbf"] = c(f"h_Sbf{i}", [128, 8, 128], BF16)
            d["dd"] = c(f"h_dd{i}", [128, 8], F32)
            d["dec"] = c(f"h_dec{i}", [128, 8], F32)
            self.h_pb.append(d)
        self.h_oT = [c(f"h_oT{i}", [128, 4, T], BF16) for i in range(2)]

    @staticmethod
    def hgrn_seq():
        seq = [("A", 0), ("A", 1)]
        for p in range(8):
            if 2 * p + 2 < 16:
                seq.append(("A", 2 * p + 2))
                seq.append(("A", 2 * p + 3))
            if p % 2 == 0 and p >= 2:
                seq.append(("O", (p - 2) // 2))
        seq.append(("O", 3))
        return seq

    def plan_hgrn(self, l):
        j = l // 2
        wi = self.W["hgrn_w_in"][j]
        wo = self.W["hgrn_w_out"][j]
        for (kind, i) in self.hgrn_seq():
            if kind == "A":
                h = i
                self.padd(("hi", l, h), 8192, [(self.wdst(2048 * k, 16, 128), self.wsrc(wi, 2048 * k + 128 * h, 128)) for k in range(4)])
            elif kind == "O":
                grp = i
                self.padd(("ho", l, grp), 8192, [(lambda slot: slot.rearrange("p (c n) -> p c n", n=D),
                                                  wo[512 * grp:512 * grp + 512, :].rearrange("(c p) n -> p c n", p=128))])

    def emit_hgrn(self, l):
        kb = self.kb
        j = l // 2
        self.alloc_hgrn()
        lb, blb, oml, boml, noml, bnoml = self.lb[j]
        S, _bS = self.hS
        bSh = [Buf(f"hS{i}") for i in range(16)]
        ones_f, bones = self.h_ones
        identb = self.cst("ident", bf=True)
        onesb = self.cst("ones", bf=True)
        U64 = self.cst("U")[0:64, 0:64].rearrange("p (o t) -> p o t", o=1).to_broadcast([64, 8, 64])
        gn = self.pcol(f"gn{j}")

        if self.first_tile:
            kb.op("dve", lambda e: e.memset(S, 0.0), writes=bSh)
        else:
            kb.dma("pool", S.rearrange("p a b -> p (a b)"), self.st_d[l], f"stin{l}", writes=bSh)
        kb.op("dve", lambda e: e.memset(ones_f, 1.0), writes=[bones])
        for i in range(2):
            bp_, bbp_ = self.h_pb[i]["bp"]
            kb.op("dve", lambda e, bp_=bp_: e.memset(bp_[:, 0:1], 0.0), writes=[bbp_])

        v3 = lambda ap: ap.rearrange("p (c t) -> p c t", t=64)
        bc = lambda ap: ap.to_broadcast([128, 8, 64])
        pbanks = {}

        def gen_A(h):
            slot, bs = self.wget()
            sv = slot.rearrange("p (c kc n) -> p c kc n", c=4, n=128)
            pb = []
            for c in range(4):
                bank, bb = self.next_bank("P")
                self.mm(bb, [(bank, sv[:, c, kc, :], self.uT[:, kc, :]) for kc in range(KC)], reads=[bs, self.Bu])
                pb.append((bank, bb))
                if c < 3:
                    yield
            pbanks[h] = pb
            stage_B1(h)
            yield

        def stage_B1(h):
            (q_ps, bq_ps), (f_ps, bf_ps), (v_ps, bv_ps), (g_ps, bg_ps) = pbanks.pop(h)
            I = self.h_in[h % 4]
            q, bq = I["q"]; sg, bsg = I["sg"]; gs, bgs = I["gs"]; vT, bvT = I["vT"]
            kb.op("act", lambda e: e.activation(q, q_ps, AF.Silu), reads=[bq_ps], writes=[bq])
            kb.op("act", lambda e: e.activation(gs, g_ps, AF.Silu), reads=[bg_ps], writes=[bgs])
            kb.op("act", lambda e: e.activation(sg, f_ps, AF.Sigmoid), reads=[bf_ps], writes=[bsg])
            kb.op("act", lambda e: e.copy(vT, v_ps), reads=[bv_ps], writes=[bvT])

        def stage_B2(h):
            par = h % 2
            hh = h % 4
            oT, boT = self.h_oT[(h // 4) % 2]
            I = self.h_in[h % 4]
            q, bq = I["q"]; sg, bsg = I["sg"]; gs, bgs = I["gs"]; vT, bvT = I["vT"]
            Pb = self.h_pb[par]
            k_, bk = Pb["k"]; bp, bbp = Pb["bp"]; d1, bd1 = Pb["d1"]; d2, bd2 = Pb["d2"]; d3, bd3 = Pb["d3"]
            qm, bqm = Pb["qm"]; km, bkm = Pb["km"]; qa, bqa = Pb["qa"]; kl, bkl = Pb["kl"]
            vtm, bvtm = Pb["vtm"]; kltm, bkltm = Pb["kltm"]; A, bA = Pb["A"]; Sbf, bSbf = Pb["Sbf"]
            dd, bdd = Pb["dd"]; dec, bdec = Pb["dec"]
            bS = bSh[h]
            lf, blf = d3, bd3
            sbk = [4 + 2 * par, 5 + 2 * par]
            nb = [0]

            def nbank():
                i = sbk[nb[0] % 2]
                nb[0] += 1
                return self.banks[i], self.bbufs[i]
            b3 = bp[:, 1:T + 1].rearrange("p (c t) -> p c t", t=64)
            bst = bp[:, 0:T].rearrange("p (c t) -> p c t", t=64)[:, :, 0:1]
            bmid = b3[:, :, 31:32]
            blast = b3[:, :, 63:64]
            kb.op("act", lambda e: e.activation(lf, sg, AF.Ln, bias=lb[:, h:h + 1], scale=oml[:, h:h + 1]), reads=[bsg, blb, boml], writes=[blf])
            yield
            kb.op("dve", lambda e: e.tensor_scalar(k_, sg, noml[:, h:h + 1], oml[:, h:h + 1], op0=ALU.mult, op1=ALU.add), reads=[bsg, bnoml, boml], writes=[bk])
            kb.op("dve", lambda e: e.tensor_tensor_scan(bp[:, 1:T + 1], ones_f, lf, 0.0, ALU.mult, ALU.add), reads=[bones, blf], writes=[bbp])
            kb.op("dve", lambda e: e.tensor_tensor(v3(d1), b3, bc(bmid), ALU.subtract), reads=[bbp], writes=[bd1])
            yield
            kb.op("act", lambda e: e.activation(d2, d1, AF.Exp), reads=[bd1], writes=[bd2])
            kb.op("act", lambda e: e.activation(d3, d1, AF.Exp, scale=-1.0), reads=[bd1], writes=[bd3])
            yield
            kb.op("dve", lambda e: e.tensor_tensor(qm, q, d2, ALU.mult), reads=[bq, bd2], writes=[bqm])
            kb.op("dve", lambda e: e.tensor_tensor(km, k_, d3, ALU.mult), reads=[bk, bd3], writes=[bkm])
            kb.op("dve", lambda e: e.tensor_tensor(v3(d1), b3, bc(bst), ALU.subtract), reads=[bbp], writes=[bd1])
            yield
            kb.op("act", lambda e: e.activation(d2, d1, AF.Exp), reads=[bd1], writes=[bd2])
            yield
            kb.op("dve", lambda e: e.tensor_tensor(qa, q, d2, ALU.mult), reads=[bq, bd2], writes=[bqa])
            kb.op("dve", lambda e: e.tensor_tensor(v3(d1), b3, bc(blast), ALU.subtract), reads=[bbp], writes=[bd1])
            kb.op("dve", lambda e: e.tensor_tensor(dd.rearrange("p (c o) -> p c o", o=1), blast, bst, ALU.subtract), reads=[bbp], writes=[bdd])
            yield
            kb.op("act", lambda e: e.activation(d3, d1, AF.Exp, scale=-1.0), reads=[bd1], writes=[bd3])
            kb.op("act", lambda e: e.activation(dec, dd, AF.Exp), reads=[bdd], writes=[bdec])
            yield
            kb.op("dve", lambda e: e.tensor_tensor(kl, k_, d3, ALU.mult), reads=[bk, bd3], writes=[bkl])
            yield
            bank, bb = nbank()
            bkb = bank.bitcast(BF16)
            self.tr_multi(bb, [(bkb[0:64, c * 128:(c + 1) * 128], vT[:, c * 64:(c + 1) * 64]) for c in range(8)], identb, reads=[bvT, self.Bcb])
            bank2, bb2 = nbank()
            bkb2 = bank2.bitcast(BF16)
            self.tr_multi(bb2, [(bkb2[0:64, c * 128:(c + 1) * 128], kl[:, c * 64:(c + 1) * 64]) for c in range(8)], identb, reads=[bkl, self.Bcb])
            yield
            kb.op("act", lambda e: e.copy(vtm.rearrange("p c v -> p (c v)"), bkb[0:64, :]), reads=[bb], writes=[bvtm])
            kb.op("dve", lambda e: e.tensor_copy(kltm.rearrange("p c v -> p (c v)"), bkb2[0:64, :]), reads=[bb2], writes=[bkltm])
            yield
            bankA, bbA = nbank()
            self.mm_multi(bbA, [[(bankA[0:64, c * 64:(c + 1) * 64], km[:, c * 64:(c + 1) * 64], qm[:, c * 64:(c + 1) * 64])] for c in range(8)], reads=[bkm, bqm])
            dS = []
            bankd, bbd = nbank()
            self.mm_multi(bbd, [[(bankd[:, cc * 128:(cc + 1) * 128], kltm[:, cc, :], vtm[:, cc, :])] for cc in range(4)], reads=[bkltm, bvtm])
            dS.append((bankd, bbd))
            yield
            kb.op("dve", lambda e: e.tensor_tensor(A, bankA[0:64, :].rearrange("p (c t) -> p c t", t=64), U64, ALU.mult), reads=[bbA, self.Bcf], writes=[bA])
            yield
            bankd, bbd = nbank()
            self.mm_multi(bbd, [[(bankd[:, cc * 128:(cc + 1) * 128], kltm[:, 4 + cc, :], vtm[:, 4 + cc, :])] for cc in range(4)], reads=[bkltm, bvtm])
            dS.append((bankd, bbd))
            yield
            for c in range(8):
                bank, bb = dS[c // 4]
                kb.op("act", lambda e, c=c: e.copy(Sbf[:, c, :], S[:, h, :]), reads=[bS], writes=[bSbf])
                kb.op("dve", lambda e, c=c, bank=bank: e.scalar_tensor_tensor(S[:, h, :], S[:, h, :], dec[:, c:c + 1], bank[:, (c % 4) * 128:(c % 4 + 1) * 128], op0=ALU.mult, op1=ALU.add),
                      reads=[bS, bdec, bb], writes=[bS])
                yield
            o_ps, bo_ps = nbank()
            self.mm_multi(bo_ps, [[(o_ps[:, c * 64:(c + 1) * 64], vtm[:, c, :], A[:, c, :]),
                                   (o_ps[:, c * 64:(c + 1) * 64], Sbf[:, c, :], qa[:, c * 64:(c + 1) * 64])] for c in range(8)],
                          reads=[bvtm, bA, bSbf, bqa])
            yield
            sqt, bsq = self.sqring[par]
            kb.op("act", lambda e: e.activation(sqt, o_ps, AF.Square), reads=[bo_ps], writes=[bsq])
            yield
            bankn, bbn = nbank()
            self.mm(bbn, [(bankn, onesb, sqt)], reads=[bsq, self.Bcb])
            yield
            kb.op("act", lambda e: e.activation(d1, bankn, AF.Ln, bias=EPS, scale=1.0 / 128), reads=[bbn], writes=[bd1])
            kb.op("act", lambda e: e.activation(d2, d1, AF.Exp, scale=-0.5), reads=[bd1], writes=[bd2])
            yield
            kb.op("dve", lambda e: e.scalar_tensor_tensor(d3, o_ps, gn[:, 0:1], d2, op0=ALU.mult, op1=ALU.mult), reads=[bo_ps, bd2, self.Bpc], writes=[bd3])
            kb.op("dve", lambda e: e.tensor_tensor(oT[:, hh, :], d3, gs, ALU.mult), reads=[bd3, bgs], writes=[boT])
            yield

        def gen_O(grp):
            oT, boT = self.h_oT[grp % 2]
            slot, bs = self.wget()
            sv = slot.rearrange("p (c n) -> p c n", n=D)
            for m in range(KC):
                bank, bb = self.next_bank("P")
                self.mm(bb, [(bank, sv[:, c, m * 128:(m + 1) * 128], oT[:, c, :]) for c in range(4)], reads=[bs, boT])
                kb.op("dve", lambda e, m=m, bank=bank: e.tensor_tensor(self.hT[:, m, :], self.hT[:, m, :], bank, ALU.add), reads=[bb, self.Bh[m]], writes=[self.Bh[m]])
                if m % 4 == 3:
                    yield

        def chain(gens):
            for g_ in gens:
                yield from g_

        def drain(gen):
            for _ in gen:
                pass

        acc = [0.0]

        def rr2(mains, bg, rate=1.0):
            mains = list(mains)
            while mains:
                for gen in list(mains):
                    try:
                        next(gen)
                    except StopIteration:
                        mains.remove(gen)
                acc[0] += rate
                while acc[0] >= 1.0:
                    acc[0] -= 1.0
                    try:
                        next(bg)
                    except StopIteration:
                        pass

        drain(gen_A(0))
        drain(gen_A(1))
        for p in range(8):
            items = []
            kinds = []
            if 2 * p + 2 < 16:
                items += [gen_A(2 * p + 2), gen_A(2 * p + 3)]
                kinds += ["A", "A"]
            if p % 2 == 0 and p >= 2:
                items.append(gen_O((p - 2) // 2))
                kinds.append("O")
            npieces = sum(5 if it_[0] == "A" else 4 for it_ in kinds)
            bg = chain(items)
            rr2([stage_B2(2 * p), stage_B2(2 * p + 1)], bg, rate=npieces / 25.0)
            drain(bg)
        drain(gen_O(3))
        self._dma_toks.append(kb.dma("pool", self.st_d[l], S.rearrange("p a b -> p (a b)"), f"stout{l}", reads=bSh))

    def alloc_mamba(self):
        self.carve_reset()
        c = self.carve
        self.m_h2 = [c(f"m_h{i}", [128, T], F32) for i in range(2)]
        self.m_dt = c("m_dt", [128, 4, 64], F32)
        self.m_e = c("m_e", [128, 4, 64], F32)
        self.m_dtA = c("m_dtA", [128, 4, 64], F32)
        self.m_acs = c("m_acs", [128, 4, 64], F32)
        self.m_eacs = c("m_eacs", [128, 4, 64], F32)
        self.m_dst = c("m_dst", [128, 4, 64], F32)
        self.m_cdec = c("m_cdec", [128, 4, 64], F32)
        self.m_xc2 = [c(f"m_xc{i}", [128, 4, T], BF16) for i in range(2)]
        self.m_BT2 = [c(f"m_BT{i}", [128, T], BF16) for i in range(2)]
        self.m_CT2 = [c(f"m_CT{i}", [128, T], BF16) for i in range(2)]
        self.m_zs2 = [c(f"m_zs{i}", [128, 4, T], BF16) for i in range(2)]
        self.xs = [c(f"mxs{i}", [128, T + 4], F32) for i in range(2)]
        self.yc = [c(f"myc{i}", [128, T], F32) for i in range(2)]
        self.m_cb = []
        for i in range(2):
            d = {}
            d["xtm"] = c(f"m_xtm{i}", [128, T], F32)
            d["X"] = c(f"m_X{i}", [128, 8, 64], BF16)
            d["Xh"] = c(f"m_Xh{i}", [128, 8, 64], BF16)
            d["Btm"] = c(f"m_Btm{i}", [128, 128], BF16)
            d["CBm"] = c(f"m_CBm{i}", [128, 128], BF16)
            d["Lh"] = c(f"m_Lh{i}", [128, 8, 128], F32)
            d["E"] = c(f"m_E{i}", [128, 8, 128], BF16)
            d["M"] = c(f"m_M{i}", [128, 8, 128], BF16)
            d["t1"] = c(f"m_t1{i}", [128, T], F32)
            d["t2"] = c(f"m_t2{i}", [128, T], F32)
            d["ygn"] = c(f"m_ygn{i}", [128, T], BF16)
            d["ss"] = c(f"m_ss{i}", [128, 4], F32)
            self.m_cb.append(d)
        self.m_hbf = c("m_hbf", [128, T], BF16)
        self.m_yT = [c(f"m_yT{i}", [128, 4, T], BF16) for i in range(2)]

    @staticmethod
    def mamba_seq():
        seq = [("Z", 0), ("X", 0), ("BC", 0)]
        for g in range(8):
            if g < 7:
                seq += [("Z", g + 1), ("X", g + 1), ("BC", g + 1)]
            if g >= 1:
                seq.append(("O", g - 1))
        seq.append(("O", 7))
        return seq

    def plan_mamba(self, l):
        j = l // 2
        wi = self.W["m_w_in"][j]
        wo = self.W["m_w_out"][j]
        self.padd(("mdt", l), 1024, [(self.wdst(0, 16, 64), self.wsrc(wi, 10240, 64))])
        for (kind, g) in self.mamba_seq():
            if kind == "Z":
                self.padd(("mz", l, g), 8192, [(self.wdst(0, 16, 512), self.wsrc(wi, 512 * g, 512))])
            elif kind == "X":
                self.padd(("mx", l, g), 8192, [(self.wdst(0, 16, 512), self.wsrc(wi, 4096 + 512 * g, 512))])
            elif kind == "BC":
                self.padd(("mbc", l, g), 4096, [(self.wdst(0, 16, 128), self.wsrc(wi, 8192 + 128 * g, 128)),
                                                (self.wdst(2048, 16, 128), self.wsrc(wi, 9216 + 128 * g, 128))])
            elif kind == "O":
                self.padd(("mo", l, g), 8192, [(lambda slot: slot.rearrange("p (c n) -> p c n", n=D),
                                                wo[512 * g:512 * g + 512, :].rearrange("(c p) n -> p c n", p=128))])

    def emit_mamba(self, l):
        kb = self.kb
        j = l // 2
        self.alloc_mamba()
        dt, bdt = self.m_dt
        ee, bee = self.m_e
        dtA, bdtA = self.m_dtA
        acs, bacs = self.m_acs
        eacs, beacs = self.m_eacs
        dst, bdst = self.m_dst
        cdec, bcdec = self.m_cdec
        hbf, bhbf = self.m_hbf
        tail, btail = self.mtail[j]
        identb = self.cst("ident", bf=True)
        Uf = self.cst("U")
        onesf = self.cst("ones")
        MS = self.cst("MS")
        Arow, bArow = self.Arow[j]
        dtb = self.pcol(f"dtb{j}")
        dsk = self.pcol(f"dsk{j}")
        wkeys = [f"mcw{j}_{k}" for k in range(4)]

        first_tile = self.first_tile

        slot, bs = self.wget()
        sv = slot[:, 0:16 * 64].rearrange("p (kc n) -> p kc n", n=64)
        bank, bb = self.next_bank()
        self.mm_multi(bb, [[(bank[:, sub * 64:(sub + 1) * 64], self.uT[:, kc, sub * 128:(sub + 1) * 128], sv[:, kc, :]) for kc in range(KC)] for sub in range(4)], reads=[bs, self.Bu])
        b4 = lambda ap: ap.rearrange("p (s h) -> p s h", h=64)
        kb.op("dve", lambda e, bank=bank: e.tensor_tensor(dt, b4(bank[:, 0:256]), dtb.rearrange("p (o h) -> p o h", o=1).to_broadcast([128, 4, 64]), ALU.add), reads=[bb, self.Bpc], writes=[bdt])
        kb.op("act", lambda e: e.activation(ee, dt, AF.Exp), reads=[bdt], writes=[bee])
        kb.op("act", lambda e: e.activation(dt, ee, AF.Ln, bias=1.0), reads=[bee], writes=[bdt])
        kb.op("dve", lambda e: e.tensor_tensor(dtA, dt, Arow.rearrange("p (o h) -> p o h", o=1).to_broadcast([128, 4, 64]), ALU.mult), reads=[bdt, bArow], writes=[bdtA])
        bank, bb = self.next_bank()
        self.mm_multi(bb, [[(bank[:, sub * 64:(sub + 1) * 64], Uf, dtA[:, sub, :])] for sub in range(4)], reads=[bdtA, self.Bcf])
        kb.op("act", lambda e, bank=bank: e.copy(acs, b4(bank[:, 0:256])), reads=[bb], writes=[bacs])
        kb.op("act", lambda e, bank=bank: e.activation(eacs, b4(bank[:, 0:256]), AF.Exp), reads=[bb], writes=[beacs])
        bank, bb = self.next_bank()
        self.mm_multi(bb, [[(bank[:, sub * 64:(sub + 1) * 64], onesf, dtA[:, sub, :])] for sub in range(4)], reads=[bdtA, self.Bcf])
        kb.op("dve", lambda e, bank=bank: e.tensor_tensor(dst, b4(bank[:, 0:256]), acs, ALU.subtract), reads=[bb, bacs], writes=[bdst])
        kb.op("act", lambda e: e.activation(dst, dst, AF.Exp), reads=[bdst], writes=[bdst])
        kb.op("act", lambda e, bank=bank: e.activation(cdec, b4(bank[:, 0:256]), AF.Exp), reads=[bb], writes=[bcdec])

        ringc = [0]
        hb8 = lambda ap, g: ap[:, 8 * g:8 * g + 8].rearrange("p (h o) -> p h o", o=1).to_broadcast([128, 8, 64])

        def gen_Z(g):
            zs, bzs = self.m_zs2[g % 2]
            hS, bhS = self.m_h2[g % 2]
            if first_tile:
                kb.op("dve", lambda e: e.memset(hS, 0.0), writes=[bhS])
            else:
                kb.dma("pool", hS, self.st_d[l][:, 512 * g:512 * g + 512], f"stin{l}_{g % 2}", writes=[bhS])
            slot, bs = self.wget()
            sv = slot.rearrange("p (kc n) -> p kc n", n=512)
            for sub in range(4):
                bank, bb = self.next_bank("P")
                self.mm(bb, [(bank, self.uT[:, kc, sub * 128:(sub + 1) * 128], sv[:, kc, :]) for kc in range(KC)], reads=[bs, self.Bu])
                kb.op("act", lambda e, bank=bank, sub=sub: e.activation(zs[:, sub, :], bank, AF.Silu), reads=[bb], writes=[bzs])
                yield

        def gen_X(g):
            xc, bxc = self.m_xc2[g % 2]
            ring = ringc[0]
            slot, bs = self.wget()
            sv = slot.rearrange("p (kc n) -> p kc n", n=512)
            for c in range(4):
                ring = ringc[0]
                bank, bb = self.next_bank("P")
                self.mm(bb, [(bank, sv[:, kc, c * 128:(c + 1) * 128], self.uT[:, kc, :]) for kc in range(KC)], reads=[bs, self.Bu])
                y_, by_ = self.conv_chunk(bank, bb, tail, btail, 4 * g + c, wkeys, f"mcb{j}", 4, ring)
                ring += 1
                kb.op("act", lambda e, y_=y_, c=c: e.activation(xc[:, c, :], y_, AF.Silu), reads=[by_], writes=[bxc])
                ringc[0] = ring
                yield
            ringc[0] = ring

        def gen_BC(g):
            BT, bBT = self.m_BT2[g % 2]
            CT, bCT = self.m_CT2[g % 2]
            ring = ringc[0]
            slot, bs = self.wget()
            sv = slot[:, 0:4096].rearrange("p (c kc n) -> p c kc n", c=2, n=128)
            for c, (dstT, bdstT, ci) in enumerate([(BT, bBT, 32 + g), (CT, bCT, 40 + g)]):
                ring = ringc[0]
                bank, bb = self.next_bank("P")
                self.mm(bb, [(bank, sv[:, c, kc, :], self.uT[:, kc, :]) for kc in range(KC)], reads=[bs, self.Bu])
                y_, by_ = self.conv_chunk(bank, bb, tail, btail, ci, wkeys, f"mcb{j}", 4, ring)
                ring += 1
                kb.op("act", lambda e, y_=y_, dstT=dstT: e.activation(dstT, y_, AF.Silu), reads=[by_], writes=[bdstT])
                ringc[0] = ring
                yield
            ringc[0] = ring

        def c_front(g, s_, B):
            xc, bxc = self.m_xc2[g % 2]
            BT, bBT = self.m_BT2[g % 2]
            CT, bCT = self.m_CT2[g % 2]
            xtm, bxtm = B["xtm"]; X, bX = B["X"]; Xh, bXh = B["Xh"]; Btm, bBtm = B["Btm"]
            CBm, bCBm = B["CBm"]; Lh, bLh = B["Lh"]; E, bE = B["E"]; M, bM = B["M"]
            tc = slice(128 * s_, 128 * s_ + 128)
            bank, bb = self.next_bank("S")
            bkb = bank.bitcast(BF16)
            self.tr_multi(bb, [(bkb[:, c * 128:(c + 1) * 128], xc[:, c, tc]) for c in range(4)] + [(bkb[:, 512:640], BT[:, tc])], identb, reads=[bxc, bBT, self.Bcb])
            yield
            kb.op("act", lambda e: e.copy(xtm, bkb[:, 0:512]), reads=[bb], writes=[bxtm])
            kb.op("act", lambda e: e.copy(Btm, bkb[:, 512:640]), reads=[bb], writes=[bBtm])
            yield
            x3 = xtm.rearrange("p (h q) -> p h q", q=64)
            kb.op("dve", lambda e: e.tensor_tensor(X, x3, hb8(dt[:, s_, :], g), ALU.mult), reads=[bxtm, bdt], writes=[bX])
            kb.op("dve", lambda e: e.tensor_tensor(Xh, X, hb8(dst[:, s_, :], g), ALU.mult), reads=[bX, bdst], writes=[bXh])
            kb.op("dve", lambda e: e.tensor_tensor(Lh, MS.rearrange("p (o t) -> p o t", o=1).to_broadcast([128, 8, 128]),
                                                     dtA[:, s_, 8 * g:8 * g + 8].rearrange("p (h o) -> p h o", o=1).to_broadcast([128, 8, 128]), ALU.mult),
                  reads=[bdtA, self.Bcf], writes=[bLh])
            yield
            bankc, bbc = self.next_bank("S")
            self.mm(bbc, [(bankc[:, 0:128], BT[:, tc], CT[:, tc])], reads=[bBT, bCT])
            yield
            kb.op("dve", lambda e: e.tensor_tensor(CBm, bankc[:, 0:128], Uf, ALU.mult), reads=[bbc, self.Bcf], writes=[bCBm])
            yield
            dbanks = []
            for half in range(2):
                bank2, bb2 = self.next_bank("S")
                self.mm_multi(bb2, [[(bank2[:, q * 128:(q + 1) * 128], Lh[:, 4 * half + q, :], Uf)] for q in range(4)], reads=[bLh, self.Bcf])
                dbanks.append((bank2, bb2))
            yield
            for half, (bank2, bb2) in enumerate(dbanks):
                kb.op("act", lambda e, bank2=bank2, half=half: e.activation(E[:, 4 * half:4 * half + 4, :], bank2.rearrange("p (h t) -> p h t", t=128), AF.Exp), reads=[bb2], writes=[bE])
            yield
            kb.op("dve", lambda e: e.tensor_tensor(M, E, CBm.rearrange("p (o t) -> p o t", o=1).to_broadcast([128, 8, 128]), ALU.mult), reads=[bE, bCBm], writes=[bM])
            yield

        def c_mid(g, s_, B):
            CT, bCT = self.m_CT2[g % 2]
            hS, bhS = self.m_h2[g % 2]
            X, bX = B["X"]; Xh, bXh = B["Xh"]; Btm, bBtm = B["Btm"]; M, bM = B["M"]; t1, bt1 = B["t1"]
            tc = slice(128 * s_, 128 * s_ + 128)
            bankd, bbd = self.next_bank("S")
            self.mm_multi(bbd, [[(bankd[:, q * 64:(q + 1) * 64], M[:, q, :], X[:, q, :])] for q in range(8)], reads=[bM, bX])
            banko, bbo = self.next_bank("S")
            self.mm(bbo, [(banko, CT[:, tc], hbf)], reads=[bCT, bhbf])
            yield
            t13 = t1.rearrange("p (h q) -> p h q", q=64)
            kb.op("dve", lambda e: e.tensor_tensor(t13, banko.rearrange("p (h q) -> p h q", q=64), hb8(eacs[:, s_, :], g), ALU.mult), reads=[bbo, beacs], writes=[bt1])
            kb.op("dve", lambda e: e.tensor_tensor(t1, t1, bankd, ALU.add), reads=[bt1, bbd], writes=[bt1])
            yield
            banks_, bbs = self.next_bank("S")
            self.mm(bbs, [(banks_, Btm, Xh.rearrange("p h q -> p (h q)"))], reads=[bBtm, bXh])
            yield
            h3 = hS.rearrange("p (h q) -> p h q", q=64)
            kb.op("dve", lambda e: e.tensor_tensor(h3, h3, hb8(cdec[:, s_, :], g), ALU.mult), reads=[bhS, bcdec], writes=[bhS])
            kb.op("dve", lambda e: e.tensor_tensor(hS, hS, banks_, ALU.add), reads=[bhS, bbs], writes=[bhS])
            if s_ < 3:
                kb.op("act", lambda e: e.copy(hbf, hS), reads=[bhS], writes=[bhbf])
            yield

        def c_tail(g, s_, B):
            yT, byT = self.m_yT[g % 2]
            zs, bzs = self.m_zs2[g % 2]
            xtm, bxtm = B["xtm"]; t1, bt1 = B["t1"]; t2, bt2 = B["t2"]; ygn, bygn = B["ygn"]; ss, bss = B["ss"]
            tc = slice(128 * s_, 128 * s_ + 128)
            x3 = xtm.rearrange("p (h q) -> p h q", q=64)
            t23 = t2.rearrange("p (h q) -> p h q", q=64)
            kb.op("dve", lambda e: e.tensor_tensor(t23, x3, hb8(dsk, g), ALU.mult), reads=[bxtm, self.Bpc], writes=[bt2])
            kb.op("dve", lambda e: e.tensor_tensor(t1, t1, t2, ALU.add), reads=[bt1, bt2], writes=[bt1])
            kb.op("dve", lambda e: e.tensor_tensor(t1, t1, zs[:, s_, :], ALU.mult), reads=[bt1, bzs], writes=[bt1])
            yield
            kb.op("act", lambda e: e.activation(t2, t1, AF.Square, accum_out=ss[:, 0:1]), reads=[bt1], writes=[bt2, bss])
            kb.op("act", lambda e: e.activation(ss[:, 1:2], ss[:, 0:1], AF.Ln, bias=EPS, scale=1.0 / 512), reads=[bss], writes=[bss])
            kb.op("act", lambda e: e.activation(ss[:, 2:3], ss[:, 1:2], AF.Exp, scale=-0.5), reads=[bss], writes=[bss])
            yield
            kb.op("dve", lambda e: e.tensor_scalar(ygn, t1, ss[:, 2:3], None, op0=ALU.mult), reads=[bt1, bss], writes=[bygn])
            yield
            bank, bb = self.next_bank("S")
            bkb = bank.bitcast(BF16)
            self.tr_multi(bb, [(bkb[:, c * 128:(c + 1) * 128], ygn[:, c * 128:(c + 1) * 128]) for c in range(4)], identb, reads=[bygn, self.Bcb])
            yield
            o_, _n = self.pco[f"mnw{j}"]
            kb.op("dve", lambda e: e.tensor_tensor(yT[:, :, tc], bkb[:, 0:512].rearrange("p (c t) -> p c t", t=128),
                                                     self.pc[:, o_ + 4 * g:o_ + 4 * g + 4].rearrange("p (c o) -> p c o", o=1).to_broadcast([128, 4, 128]), ALU.mult),
                  reads=[bb, self.Bpc], writes=[byT])
            yield

        def chain(gens):
            for g_ in gens:
                yield from g_

        def drain(gen):
            for _ in gen:
                pass

        acc = [0.0]

        def rr2(mains, bg, rate=1.0):
            mains = list(mains)
            while mains:
                for gen in list(mains):
                    try:
                        next(gen)
                    except StopIteration:
                        mains.remove(gen)
                acc[0] += rate
                while acc[0] >= 1.0:
                    acc[0] -= 1.0
                    try:
                        next(bg)
                    except StopIteration:
                        pass

        def do_pair(g, pair, bg, rate):
            hS, bhS = self.m_h2[g % 2]
            sa, sb = 2 * pair, 2 * pair + 1
            Ba, Bb = self.m_cb[0], self.m_cb[1]
            if pair == 0:
                kb.op("act", lambda e: e.copy(hbf, hS), reads=[bhS], writes=[bhbf])
            rr2([c_front(g, sa, Ba), c_front(g, sb, Bb)], bg, rate)
            rr2([chain([c_mid(g, sa, Ba), c_mid(g, sb, Bb)])], bg, rate)
            rr2([c_tail(g, sa, Ba), c_tail(g, sb, Bb)], bg, rate)
            if pair == 1:
                self._dma_toks.append(kb.dma("pool", self.st_d[l][:, 512 * g:512 * g + 512], hS, f"stout{l}_{g % 2}", reads=[bhS]))

        def gen_O(g):
            yT, byT = self.m_yT[g % 2]
            slot, bs = self.wget()
            sv = slot.rearrange("p (c n) -> p c n", n=D)
            for m in range(KC):
                bank, bb = self.next_bank("P")
                self.mm(bb, [(bank, sv[:, c, m * 128:(m + 1) * 128], yT[:, c, :]) for c in range(4)], reads=[bs, byT])
                kb.op("dve", lambda e, m=m, bank=bank: e.tensor_tensor(self.hT[:, m, :], self.hT[:, m, :], bank, ALU.add), reads=[bb, self.Bh[m]], writes=[self.Bh[m]])
                if m % 4 == 3:
                    yield

        drain(gen_Z(0))
        drain(gen_X(0))
        drain(gen_BC(0))
        for g in range(8):
            items = ([gen_Z(g + 1), gen_X(g + 1), gen_BC(g + 1)] if g < 7 else []) + ([gen_O(g - 1)] if g >= 1 else [])
            npieces = (10 if g < 7 else 0) + (4 if g >= 1 else 0)
            bg = chain(items)
            rate = npieces / 40.0
            do_pair(g, 0, bg, rate)
            do_pair(g, 1, bg, rate)
            drain(bg)
        drain(gen_O(7))


_CACHE = {}


def build_nc(inputs_offs, **kw):
    pc_offs, cc_offs = inputs_offs
    p = Prog(**kw)
    nc = p.build(pc_offs, cc_offs)
    return nc


def kernel(**inputs):
    inp = {k: np.asarray(v) for k, v in inputs.items()}
    pcols, pc_offs = pack_params(inp)
    consts, cc_offs = make_consts()
    nc = build_nc((pc_offs, cc_offs))
    x = np.ascontiguousarray(inp["x"], dtype=np.float32)
    in_maps = []
    for c in range(8):
        m = {"x": x[2 * c:2 * c + 2], "pcols": pcols, "consts": consts}
        for k in ("hgrn_w_in", "hgrn_w_out", "m_w_in", "m_w_out", "f_w_up", "f_w_down"):
            m[k] = np.ascontiguousarray(inp[k], dtype=np.float32)
        in_maps.append(m)
    res = run_bass_kernel_spmd(nc, in_maps, core_ids=list(range(8)))
    return np.concatenate([np.asarray(r["out"]) for r in res.results], axis=0).astype(np.float32)
```

```python
import numpy as np
import concourse.bass as bass
import concourse.mybir as mybir
from concourse.bass_utils import run_bass_kernel_spmd

F32 = mybir.dt.float32
BF16 = mybir.dt.bfloat16
ALU = mybir.AluOpType
AF = mybir.ActivationFunctionType

D = 2048
KC = 16
T = 512
NSUB = T // 128
SEQ = 2048
EPS = 1e-5
DFF = 5632
M_DI = 4096
M_CONVD = 6144
M_IN = 10304


class Buf:
    __slots__ = ("name", "w", "r")

    def __init__(self, name):
        self.name = name
        self.w = None
        self.r = {}


class KB:
    SEM_ROLL = 30000

    def __init__(self, nc):
        self.nc = nc
        self.engs = {"pe": nc.tensor, "dve": nc.vector, "act": nc.scalar,
                     "pool": nc.gpsimd, "sp": nc.sync}
        self.sem = {}
        self.cnt = {}
        self.nsem = 0
        self.seen = {e: {} for e in self.engs}
        self.dsem = {}
        self.nwait = 0
        self.nins = {e: 0 for e in self.engs}

    def _newsem(self, name):
        self.nsem += 1
        return self.nc.alloc_semaphore(f"{name}_{self.nsem}")

    def _bump(self, eng):
        if eng not in self.sem or self.cnt[eng] >= self.SEM_ROLL:
            self.sem[eng] = self._newsem("s_" + eng)
            self.cnt[eng] = 0
        self.cnt[eng] += 1
        return (id(self.sem[eng]), self.sem[eng], self.cnt[eng], eng)

    def _wait(self, eng, deps):
        h = self.engs[eng]
        seen = self.seen[eng]
        for tok in deps:
            k, sem, val, src = tok
            if eng == "pe" and src == "pe":
                continue
            if seen.get(k, 0) >= val:
                continue
            h.wait_ge(sem, val)
            self.nwait += 1
            seen[k] = val

    @staticmethod
    def _deps(reads, writes):
        deps = []
        for b in reads:
            if b.w is not None:
                deps.append(b.w)
        for b in writes:
            if b.w is not None:
                deps.append(b.w)
            deps.extend(b.r.values())
        return deps

    def op(self, eng, fn, reads=(), writes=()):
        self._wait(eng, self._deps(reads, writes))
        ins = fn(self.engs[eng])
        self.nins[eng] += 1
        tok = self._bump(eng)
        ins.then_inc(tok[1], 1)
        for b in reads:
            b.r[eng] = tok
        for b in writes:
            b.w = tok
            b.r = {}
        return ins

    def dma(self, q, out, in_, key, reads=(), writes=()):
        self._wait(q, self._deps(reads, writes))
        if key not in self.dsem:
            self.dsem[key] = [self._newsem("d_" + key), 0]
        ent = self.dsem[key]
        ent[1] += 16
        ins = self.engs[q].dma_start(out=out, in_=in_)
        ins.then_inc(ent[0], 16)
        self.nins[q] += 1
        tok = (id(ent[0]), ent[0], ent[1], "dma:" + key)
        for b in reads:
            b.r["dma:" + key] = tok
        for b in writes:
            b.w = tok
            b.r = {}
        return tok

    def wait_all(self, eng, bufs):
        deps = []
        for b in bufs:
            if b.w is not None:
                deps.append(b.w)
            deps.extend(b.r.values())
        self._wait(eng, deps)


def _cols(v):
    v = np.asarray(v, np.float32)
    return np.ascontiguousarray(v.reshape(-1, 128).T)


def _pack(named):
    offs = {}
    parts = []
    o = 0
    for k, a in named:
        a = np.asarray(a, np.float32).reshape(128, -1)
        offs[k] = (o, a.shape[1])
        parts.append(a)
        o += a.shape[1]
    return np.ascontiguousarray(np.concatenate(parts, axis=1)), offs


def pack_params(inp):
    named = []
    for l in range(4):
        named.append((f"mixn{l}", _cols(inp["mix_norm"][l])))
        named.append((f"ffnn{l}", _cols(inp["ffn_norm"][l])))
        fcw = inp["f_conv_w"][l]
        for k in range(3):
            named.append((f"fcw{l}_{k}", _cols(fcw[k])))
        named.append((f"fcb{l}", _cols(inp["f_conv_b"][l])))
    named.append(("finn", _cols(inp["final_norm"])))
    for j in range(2):
        named.append((f"lbl{j}", _cols(inp["hgrn_lb_logits"][j])))
        named.append((f"gn{j}", _cols(inp["hgrn_gnorm"][j])))
        for k in range(4):
            named.append((f"mcw{j}_{k}", _cols(inp["m_conv_w"][j][k])))
        named.append((f"mcb{j}", _cols(inp["m_conv_b"][j])))
        named.append((f"mnw{j}", _cols(inp["m_norm"][j])))
        named.append((f"dtb{j}", np.broadcast_to(np.asarray(inp["m_dt_bias"][j], np.float32)[None, :], (128, 64))))
        named.append((f"alog{j}", np.broadcast_to(np.asarray(inp["m_A_log"][j], np.float32)[None, :], (128, 64))))
        named.append((f"dsk{j}", np.broadcast_to(np.asarray(inp["m_D"][j], np.float32)[None, :], (128, 64))))
    return _pack(named)


def make_consts():
    i = np.arange(128)
    ident = np.eye(128, dtype=np.float32)
    U = (i[:, None] <= i[None, :]).astype(np.float32)
    MS = (i[:, None] > i[None, :]).astype(np.float32)
    ones = np.ones((128, 128), np.float32)
    return _pack([("ident", ident), ("U", U), ("MS", MS), ("ones", ones)])


class Prog:
    def __init__(self, n_seq=2, tiles_per_seq=4, stages=None, final=True, nslots=3):
        self.n_seq = n_seq
        self.tiles_per_seq = tiles_per_seq
        self.final = final
        if stages is None:
            stages = []
            for l in range(4):
                stages.append(("hgrn" if l % 2 == 0 else "mamba", l))
                stages.append(("ffn", l))
        self.stages = stages
        self.nslots = nslots
        nc = bass.Bass("TRN2", target_bir_lowering=False)
        self.nc = nc
        self.kb = KB(nc)
        self._n = 0
        self._dma_toks = []
        self._scr_off = 0
        self.bank_i = 0
        self.bank_p = 0
        self.bank_s = 0

    def T_(self, name, shape, dt):
        t = self.nc.alloc_sbuf_tensor(name, list(shape), dt).ap()
        return t, Buf(name)

    def carve_reset(self):
        self._scr_off = 0

    def carve(self, name, shape, dt):
        esz = 4 if dt == F32 else 2
        n = int(np.prod(shape[1:]))
        nbytes = (n * esz + 31) // 32 * 32
        off = self._scr_off
        self._scr_off += nbytes
        assert self._scr_off <= self.SCR, (name, self._scr_off)
        v = self.scr[0:shape[0], off // 2: off // 2 + n * esz // 2]
        if dt == F32:
            v = v.bitcast(F32)
        if len(shape) == 3:
            v = v.rearrange("p (a b) -> p a b", b=shape[2])
        return v, Buf(name)

    def barrier(self):
        kb = self.kb
        engs = ["pe", "act", "dve", "pool"]
        toks = []
        for e in engs:
            if e in kb.sem:
                toks.append((id(kb.sem[e]), kb.sem[e], kb.cnt[e], e))
        toks.extend(self._dma_toks)
        self._dma_toks = []
        for e in engs:
            h = kb.engs[e]
            for (k, sem, val, src) in toks:
                if src == e or kb.seen[e].get(k, 0) >= val:
                    continue
                h.wait_ge(sem, val)
                kb.nwait += 1
                kb.seen[e][k] = val

    def next_bank(self, pool="A"):
        if pool == "A":
            i = self.bank_i % 8
            self.bank_i += 1
        elif pool == "P":
            i = self.bank_p % 4
            self.bank_p += 1
        else:
            i = 4 + self.bank_s % 4
            self.bank_s += 1
        return self.banks[i], self.bbufs[i]

    def mm(self, bank_buf, pairs, reads, first_start=True, last_stop=True):
        n = len(pairs)

        def fn(e):
            ins = None
            for i, (o, l, r) in enumerate(pairs):
                ins = e.matmul(o, l, r, start=(i == 0 and first_start), stop=(i == n - 1 and last_stop))
            return ins
        self.kb.nins["pe"] += n - 1
        assert bank_buf.w is None or bank_buf.r, bank_buf.name
        return self.kb.op("pe", fn, reads=reads, writes=[bank_buf])

    def mm_multi(self, bank_buf, groups, reads):
        def fn(e):
            ins = None
            for g in groups:
                n = len(g)
                for i, (o, l, r) in enumerate(g):
                    ins = e.matmul(o, l, r, start=(i == 0), stop=(i == n - 1))
            return ins
        self.kb.nins["pe"] += sum(len(g) for g in groups) - 1
        assert bank_buf.w is None or bank_buf.r, bank_buf.name
        return self.kb.op("pe", fn, reads=reads, writes=[bank_buf])

    def tr_multi(self, bank_buf, items, ident, reads):
        def fn(e):
            ins = None
            for (o, i_) in items:
                if i_.dtype == F32:
                    ins = e.matmul(o, i_, ident, start=True, stop=True)
                else:
                    ins = e.transpose(o, i_, ident)
            return ins
        self.kb.nins["pe"] += len(items) - 1
        assert bank_buf.w is None or bank_buf.r, bank_buf.name
        return self.kb.op("pe", fn, reads=reads, writes=[bank_buf])

    def build(self, pc_offs, cc_offs):
        nc, kb = self.nc, self.kb
        self.pco = pc_offs
        self.cco = cc_offs
        npc = max(o + n for o, n in pc_offs.values())
        ncc = max(o + n for o, n in cc_offs.values())
        ntok = self.tiles_per_seq * T
        dr = lambda name, shape: nc.dram_tensor(name, list(shape), F32, kind="ExternalInput").ap()
        self.x = dr("x", [self.n_seq, SEQ, D])
        self.pcols_d = dr("pcols", [128, npc])
        self.consts_d = dr("consts", [128, ncc])
        kinds = {k for k, _ in self.stages}
        wshapes = {"hgrn": {"hgrn_w_in": [2, D, 8192], "hgrn_w_out": [2, D, D]},
                   "mamba": {"m_w_in": [2, D, M_IN], "m_w_out": [2, M_DI, D]},
                   "ffn": {"f_w_up": [4, D, 2 * DFF], "f_w_down": [4, DFF, D]}}
        self.W = {}
        for k in ("hgrn", "mamba", "ffn"):
            if k in kinds:
                for nm, shp in wshapes[k].items():
                    self.W[nm] = dr(nm, shp)
        self.out = nc.dram_tensor("out", [self.n_seq, SEQ, D], F32, kind="ExternalOutput").ap()

        self.pc, self.Bpc = self.T_("pc", [128, npc], F32)
        self.cf, self.Bcf = self.T_("cf", [128, ncc], F32)
        self.cb, self.Bcb = self.T_("cb", [128, ncc], BF16)
        kb.dma("sp", self.pc, self.pcols_d, "pc", writes=[self.Bpc])
        kb.dma("sp", self.cf, self.consts_d, "cf", writes=[self.Bcf])
        kb.dma("pool", self.cb, self.consts_d, "cb", writes=[self.Bcb])

        self.banks = [nc.alloc_psum_tensor(f"bank{i}", [128, 512], F32).ap() for i in range(8)]
        self.bbufs = [Buf(f"bank{i}") for i in range(8)]

        self.hT, _ = self.T_("hT", [128, KC, T], F32)
        self.Bh = [Buf(f"h{m}") for m in range(KC)]
        self.uT, self.Bu = self.T_("uT", [128, KC, T], BF16)
        self.slots = [self.T_(f"wslot{i}", [128, 8192], BF16) for i in range(self.nslots)]
        self.plan = []
        self.issued = 0
        self.consumed = 0
        self.sqring = [self.T_(f"sqr{i}", [128, T], BF16) for i in range(4)]
        self.lnv, self.Blnv = self.T_("lnv", [128, T], F32)
        self.rstd, self.Brstd = self.T_("rstd", [128, T], F32)
        self.state_bufs = []
        self.ftail = []
        for l in range(4):
            t_, b_ = self.T_(f"ftail{l}", [128, 88, 2], F32)
            self.ftail.append((t_, b_))
            self.state_bufs.append((t_, b_))
        self.mtail = []
        for j in range(2):
            t_, b_ = self.T_(f"mtail{j}", [128, 48, 3], F32)
            self.mtail.append((t_, b_))
            self.state_bufs.append((t_, b_))
        self.SCR = 86 * 1024
        self.scr = nc.alloc_sbuf_tensor("scr", [128, self.SCR // 2], BF16).ap()
        self.st_d = {}
        for (kind, l) in self.stages:
            if kind == "hgrn":
                self.st_d[l] = nc.dram_tensor(f"st{l}", [128, 16 * 128], F32).ap()
            if kind == "mamba":
                self.st_d[l] = nc.dram_tensor(f"st{l}", [128, 64 * 64], F32).ap()
        print("sbuf bytes remaining/partition:", nc.sbuf_bytes_remaining)

        self.uniq = {}
        for s in range(self.n_seq):
            for t in range(self.tiles_per_seq):
                for si, (kind, l) in enumerate(self.stages):
                    self.cur_stage = si
                    getattr(self, "plan_" + kind)(l)
        print("weight slabs:", len(self.plan), "unique:", len(self.uniq))
        self.prologue_init()

        self.prep_params()
        ti = 0
        nst = len(self.stages)
        for s in range(self.n_seq):
            self.reset_state()
            for t in range(self.tiles_per_seq):
                self.first_tile = (t == 0)
                self.barrier()
                self.load_tile(s, t, ti)
                if ti == 0:
                    self.prologue_stage(0)
                for si, (kind, l) in enumerate(self.stages):
                    self.rmsnorm(("ffnn%d" if kind == "ffn" else "mixn%d") % l)
                    self.barrier()
                    if ti == 0 and si + 1 < nst:
                        self.prologue_stage(si + 1)
                    getattr(self, "emit_" + kind)(l)
                self.barrier()
                self.store_tile(s, t, ti)
                ti += 1
        assert self.consumed == len(self.plan), (self.consumed, len(self.plan))
        kb._wait("pool", self._dma_toks)
        print("instructions:", kb.nins, "waits:", kb.nwait, "sems:", kb.nsem)
        return nc

    def pcol(self, key, i=None):
        o, n = self.pco[key]
        if i is None:
            return self.pc[:, o:o + n]
        return self.pc[:, o + i:o + i + 1]

    def cst(self, key, bf=False):
        o, n = self.cco[key]
        return (self.cb if bf else self.cf)[:, o:o + n]

    def padd(self, key, nel, pieces):
        if key not in self.uniq:
            self.uniq[key] = (len(self.uniq), nel, pieces, self.cur_stage)
        self.plan.append(key)

    def prologue_init(self):
        nu = len(self.uniq)
        NPT = 96
        wts = [self.nc.dram_tensor(f"wbf{i}", [min(NPT, nu - i * NPT), 128, 8192], BF16).ap() for i in range((nu + NPT - 1) // NPT)]

        class _W:
            def __getitem__(_s, u):
                return wts[u // NPT][u % NPT]
        self.wbf = _W()
        self.wbuf = {}

    def prologue_stage(self, st):
        kb = self.kb
        bufs = []
        for key, (u, nel, pieces, st_) in self.uniq.items():
            if st_ != st:
                continue
            for (dst_fn, src) in pieces:
                kb.dma("pool", dst_fn(self.wbf[u]), src, f"pro{st}")
            b = Buf(f"wbf{u}")
            self.wbuf[key] = b
            bufs.append(b)
        if bufs:
            ent = kb.dsem[f"pro{st}"]
            tok = (id(ent[0]), ent[0], ent[1], f"dma:pro{st}")
            for b in bufs:
                b.w = tok

    def wget(self, hold_prev=False):
        kb = self.kb
        lim = self.consumed + self.nslots - (1 if hold_prev else 0)
        while self.issued < len(self.plan) and self.issued < lim:
            i = self.issued
            key = self.plan[i]
            u, nel, _, _ = self.uniq[key]
            slot, b = self.slots[i % self.nslots]
            kb.dma("sp", slot[:, 0:nel], self.wbf[u][:, 0:nel], f"ws{i % self.nslots}", reads=[self.wbuf[key]], writes=[b])
            self.issued += 1
        slot, b = self.slots[self.consumed % self.nslots]
        self.consumed += 1
        return slot, b

    @staticmethod
    def wsrc(w2d, c0, n):
        return w2d.rearrange("(kc p) n -> p kc n", p=128)[:, :, c0:c0 + n]

    @staticmethod
    def wdst(off, kc, n):
        return lambda slot: slot[:, off:off + kc * n].rearrange("p (kc n) -> p kc n", n=n)

    def load_tile(self, s, t, ti):
        kb = self.kb
        ident = self.cst("ident")
        self.carve_reset()
        self.xin = [self.carve(f"xin{i}", [128, D], F32) for i in range(2)]
        for sub in range(NSUB):
            xin, bx = self.xin[sub % 2]
            r0 = t * T + sub * 128
            kb.dma("pool", xin, self.x[s, r0:r0 + 128, :], f"xin{sub % 2}", writes=[bx])
            for q in range(4):
                bank, bb = self.next_bank()
                items = [(bank[:, j * 128:(j + 1) * 128], xin[:, (4 * q + j) * 128:(4 * q + j + 1) * 128]) for j in range(4)]
                self.tr_multi(bb, items, ident, reads=[bx, self.Bcf])
                dst = self.hT[:, 4 * q:4 * q + 4, sub * 128:(sub + 1) * 128]
                src = bank.rearrange("p (j t) -> p j t", t=128)
                eng = "act" if q % 2 == 0 else "dve"
                if eng == "act":
                    kb.op("act", lambda e, d=dst, s_=src: e.copy(d, s_), reads=[bb], writes=self.Bh[4 * q:4 * q + 4])
                else:
                    kb.op("dve", lambda e, d=dst, s_=src: e.tensor_copy(d, s_), reads=[bb], writes=self.Bh[4 * q:4 * q + 4])

    def store_tile(self, s, t, ti):
        kb = self.kb
        ident = self.cst("ident")
        self.carve_reset()
        self.xin = [self.carve(f"xout{i}", [128, D], F32) for i in range(2)]
        if self.final:
            self.rms_stats(self.Bh)
            o, _ = self.pco["finn"]
            for m in range(KC):
                kb.op("dve", lambda e, m=m: e.scalar_tensor_tensor(self.hT[:, m, :], self.hT[:, m, :], self.pc[:, o + m:o + m + 1], self.rstd, op0=ALU.mult, op1=ALU.mult),
                      reads=[self.Bh[m], self.Brstd, self.Bpc], writes=[self.Bh[m]])
        for sub in range(NSUB):
            xo, bx = self.xin[sub % 2]
            for q in range(4):
                bank, bb = self.next_bank()
                items = [(bank[:, j * 128:(j + 1) * 128], self.hT[:, 4 * q + j, sub * 128:(sub + 1) * 128]) for j in range(4)]
                self.tr_multi(bb, items, ident, reads=self.Bh[4 * q:4 * q + 4] + [self.Bcf])
                if q % 2 == 0:
                    kb.op("act", lambda e, q=q, bank=bank, xo=xo: e.copy(xo[:, q * 512:(q + 1) * 512], bank), reads=[bb], writes=[bx])
                else:
                    kb.op("dve", lambda e, q=q, bank=bank, xo=xo: e.tensor_copy(xo[:, q * 512:(q + 1) * 512], bank), reads=[bb], writes=[bx])
            r0 = t * T + sub * 128
            self._dma_toks.append(kb.dma("pool", self.out[s, r0:r0 + 128, :], xo, f"xout{sub % 2}", reads=[bx]))

    def rms_stats(self, hbufs):
        kb = self.kb
        ones = self.cst("ones", bf=True)
        bank, bb = self.next_bank()
        for m in range(KC):
            sqt, bsq = self.sqring[m % len(self.sqring)]
            if m % 3 == 0:
                kb.op("act", lambda e, m=m, sqt=sqt: e.activation(sqt, self.hT[:, m, :], AF.Square), reads=[hbufs[m]], writes=[bsq])
            else:
                kb.op("pool" if m % 3 == 1 else "dve", lambda e, m=m, sqt=sqt: e.tensor_tensor(sqt, self.hT[:, m, :], self.hT[:, m, :], ALU.mult), reads=[hbufs[m]], writes=[bsq])
            ins_first = (m == 0)
            self.kb._wait("pe", kb._deps([bsq, self.Bcb], [bb] if m == 0 else []))
            ins = self.nc.tensor.matmul(bank, ones, sqt, start=(m == 0), stop=(m == KC - 1))
            kb.nins["pe"] += 1
            tok = kb._bump("pe")
            ins.then_inc(tok[1], 1)
            bsq.r["pe"] = tok
            if m == KC - 1:
                bb.w = tok
                bb.r = {}
        kb.op("act", lambda e: e.activation(self.lnv, bank, AF.Ln, bias=EPS, scale=1.0 / D), reads=[bb], writes=[self.Blnv])
        kb.op("act", lambda e: e.activation(self.rstd, self.lnv, AF.Exp, scale=-0.5), reads=[self.Blnv], writes=[self.Brstd])

    def rmsnorm(self, wkey):
        kb = self.kb
        self.rms_stats(self.Bh)
        o, _ = self.pco[wkey]
        for m in range(KC):
            kb.op("dve", lambda e, m=m: e.scalar_tensor_tensor(self.uT[:, m, :], self.hT[:, m, :], self.pc[:, o + m:o + m + 1], self.rstd, op0=ALU.mult, op1=ALU.mult),
                  reads=[self.Bh[m], self.Brstd, self.Bpc], writes=[self.Bu])

    def prep_params(self):
        kb = self.kb
        self.lb = []
        l0 = self.pcol("lbl0")
        l1 = self.pcol("lbl1")
        d01, bd01 = self.T_("d01", [128, 16], F32)
        p0, bp0 = self.T_("p0", [128, 16], F32)
        p1, bp1 = self.T_("p1", [128, 16], F32)
        kb.op("dve", lambda e: e.tensor_tensor(d01, l0, l1, ALU.subtract), reads=[self.Bpc], writes=[bd01])
        kb.op("act", lambda e: e.activation(p0, d01, AF.Sigmoid), reads=[bd01], writes=[bp0])
        kb.op("act", lambda e: e.activation(p1, d01, AF.Sigmoid, scale=-1.0), reads=[bd01], writes=[bp1])
        for j in range(2):
            lb, blb = self.T_(f"lb{j}", [128, 16], F32)
            oml, boml = self.T_(f"oml{j}", [128, 16], F32)
            noml, bnoml = self.T_(f"noml{j}", [128, 16], F32)
            if j == 0:
                kb.op("dve", lambda e, lb=lb: e.tensor_tensor(lb, p0, p0, ALU.subtract), reads=[bp0], writes=[blb])
            else:
                kb.op("dve", lambda e, lb=lb: e.tensor_tensor(lb, p0, p1, ALU.add), reads=[bp0, bp1], writes=[blb])
                kb.op("dve", lambda e, lb=lb: e.tensor_tensor(lb, lb, p0, ALU.subtract), reads=[bp0, blb], writes=[blb])
            kb.op("dve", lambda e, lb=lb, noml=noml: e.tensor_scalar(noml, lb, 1.0, None, op0=ALU.subtract), reads=[blb], writes=[bnoml])
            kb.op("dve", lambda e, oml=oml, noml=noml: e.tensor_scalar(oml, noml, -1.0, None, op0=ALU.mult), reads=[bnoml], writes=[boml])
            self.lb.append((lb, blb, oml, boml, noml, bnoml))
        self.Arow = []
        for j in range(2):
            a, ba = self.T_(f"Arow{j}", [128, 64], F32)
            kb.op("act", lambda e, a=a, j=j: e.activation(a, self.pcol(f"alog{j}"), AF.Exp), reads=[self.Bpc], writes=[ba])
            kb.op("dve", lambda e, a=a: e.tensor_scalar(a, a, -1.0, None, op0=ALU.mult), reads=[ba], writes=[ba])
            self.Arow.append((a, ba))

    def reset_state(self):
        kb = self.kb
        for (t, b) in self.state_bufs:
            kb.op("dve", lambda e, t=t: e.memset(t, 0.0), writes=[b])

    def alloc_ffn(self):
        self.carve_reset()
        self.xs = [self.carve(f"xs{i}", [128, T + 4], F32) for i in range(3)]
        self.yc = [self.carve(f"yc{i}", [128, T], F32) for i in range(4)]
        self.gs = [self.carve(f"gs{i}", [128, T], F32) for i in range(2)]
        self.aT = [self.carve(f"aT{i}", [128, 4, T], BF16) for i in range(2)]

    def plan_ffn(self, l):
        wu = self.W["f_w_up"][l]
        wd = self.W["f_w_down"][l]

        def fd(grp):
            self.padd(("fd", l, grp), 8192, [(lambda slot: slot.rearrange("p (c n) -> p c n", n=D),
                                              wd[512 * grp:512 * grp + 512, :].rearrange("(c p) n -> p c n", p=128))])
        for grp in range(11):
            for s2 in range(2):
                s = 2 * grp + s2
                self.padd(("fu", l, s), 8192, [(self.wdst(0, 16, 256), self.wsrc(wu, 256 * s, 256)),
                                               (self.wdst(4096, 16, 256), self.wsrc(wu, DFF + 256 * s, 256))])
            if grp >= 1:
                fd(grp - 1)
        fd(10)

    def conv_chunk(self, bank, bb, tail, btail, ci, wkeys, bkey, ntap, ring_i):
        kb = self.kb
        nt = ntap - 1
        xs, bxs = self.xs[ring_i % len(self.xs)]
        yc, byc = self.yc[ring_i % len(self.yc)]
        kb.op("act", lambda e: e.copy(xs[:, nt:nt + T], bank), reads=[bb], writes=[bxs])
        kb.op("dve", lambda e: e.tensor_copy(xs[:, 0:nt], tail[:, ci, 0:nt]), reads=[btail], writes=[bxs])
        wl = self.pcol(wkeys[ntap - 1], ci)
        kb.op("act", lambda e: e.activation(yc, bank, AF.Identity, bias=self.pcol(bkey, ci), scale=wl), reads=[bb, self.Bpc], writes=[byc])
        for k in range(ntap - 1):
            wk = self.pcol(wkeys[k], ci)
            kb.op("dve", lambda e, k=k, wk=wk: e.scalar_tensor_tensor(yc, xs[:, k:k + T], wk, yc, op0=ALU.mult, op1=ALU.add),
                  reads=[bxs, byc, self.Bpc], writes=[byc])
        kb.op("dve", lambda e: e.tensor_copy(tail[:, ci, 0:nt], xs[:, T:T + nt]), reads=[bxs], writes=[btail])
        return yc, byc

    def emit_ffn(self, l):
        kb = self.kb
        self.alloc_ffn()
        tail, btail = self.ftail[l]
        wkeys = [f"fcw{l}_{k}" for k in range(3)]
        ringc = [0]

        def down_gen(grp):
            aT, baT = self.aT[grp % 2]
            slot, bs = self.wget(hold_prev=True)
            sv = slot.rearrange("p (c n) -> p c n", n=D)
            for m in range(KC):
                bank, bb = self.next_bank()
                self.mm(bb, [(bank, sv[:, c, m * 128:(m + 1) * 128], aT[:, c, :]) for c in range(4)], reads=[bs, baT])
                kb.op("dve", lambda e, m=m, bank=bank: e.tensor_tensor(self.hT[:, m, :], self.hT[:, m, :], bank, ALU.add), reads=[bb, self.Bh[m]], writes=[self.Bh[m]])
                if m % 4 == 3:
                    yield

        def up_slab(grp, s2, dg):
            aT, baT = self.aT[grp % 2]
            s = 2 * grp + s2
            slot, bs = self.wget()
            sv = slot.rearrange("p (h kc n) -> p h kc n", h=2, n=256)
            ys = []
            for c in range(4):
                bank, bb = self.next_bank()
                self.mm(bb, [(bank, sv[:, c // 2, kc, (c % 2) * 128:(c % 2 + 1) * 128], self.uT[:, kc, :]) for kc in range(KC)], reads=[bs, self.Bu])
                ci = (2 * s + c) if c < 2 else (44 + 2 * s + (c - 2))
                ys.append(self.conv_chunk(bank, bb, tail, btail, ci, wkeys, f"fcb{l}", 3, ringc[0]))
                ringc[0] += 1
                if dg is not None:
                    next(dg, None)
            for c in range(2):
                gsb, bgs = self.gs[c]
                (yg, byg), (yu, byu) = ys[c], ys[2 + c]
                kb.op("act", lambda e, gsb=gsb, yg=yg: e.activation(gsb, yg, AF.Silu), reads=[byg], writes=[bgs])
                kb.op("dve", lambda e, gsb=gsb, yu=yu, c=c: e.tensor_tensor(aT[:, 2 * s2 + c, :], gsb, yu, ALU.mult), reads=[bgs, byu], writes=[baT])

        for grp in range(11):
            up_slab(grp, 0, None)
            dg = down_gen(grp - 1) if grp >= 1 else None
            up_slab(grp, 1, dg)
            if dg is not None:
                for _ in dg:
                    pass
        for _ in down_gen(10):
            pass

    def alloc_hgrn(self):
        self.carve_reset()
        c = self.carve
        self.hS = c("hS", [128, 16, 128], F32)
        self.h_in = []
        for i in range(4):
            self.h_in.append({"q": c(f"h_q{i}", [128, T], F32), "sg": c(f"h_sg{i}", [128, T], F32),
                              "gs": c(f"h_gs{i}", [128, T], BF16), "vT": c(f"h_vT{i}", [128, T], BF16)})
        self.h_ones = c("h_ones", [128, T], F32)
        self.h_pb = []
        for i in range(2):
            d = {}
            d["k"] = c(f"h_k{i}", [128, T], F32)
            d["bp"] = c(f"h_bp{i}", [128, T + 8], F32)
            d["d1"] = c(f"h_d1{i}", [128, T], F32)
            d["d2"] = c(f"h_d2{i}", [128, T], F32)
            d["d3"] = c(f"h_d3{i}", [128, T], F32)
            for nm in ("qm", "km", "qa", "kl"):
                d[nm] = c(f"h_{nm}{i}", [128, T], BF16)
            d["vtm"] = c(f"h_vtm{i}", [64, 8, 128], BF16)
            d["kltm"] = c(f"h_kltm{i}", [64, 8, 128], BF16)
            d["A"] = c(f"h_A{i}", [64, 8, 64], BF16)
            d["Sbf"] = c(f"h_Sbf{i}", [128, 8, 128], BF16)
            d["dd"] = c(f"h_dd{i}", [128, 8], F32)
            d["dec"] = c(f"h_dec{i}", [128, 8], F32)
            self.h_pb.append(d)
        self.h_oT = [c(f"h_oT{i}", [128, 4, T], BF16) for i in range(2)]

    @staticmethod
    def hgrn_seq():
        seq = [("A", 0), ("A", 1)]
        for p in range(8):
            if 2 * p + 2 < 16:
                seq.append(("A", 2 * p + 2))
                seq.append(("A", 2 * p + 3))
            if p % 2 == 0 and p >= 2:
                seq.append(("O", (p - 2) // 2))
        seq.append(("O", 3))
        return seq

    def plan_hgrn(self, l):
        j = l // 2
        wi = self.W["hgrn_w_in"][j]
        wo = self.W["hgrn_w_out"][j]
        for (kind, i) in self.hgrn_seq():
            if kind == "A":
                h = i
                self.padd(("hi", l, h), 8192, [(self.wdst(2048 * k, 16, 128), self.wsrc(wi, 2048 * k + 128 * h, 128)) for k in range(4)])
            elif kind == "O":
                grp = i
                self.padd(("ho", l, grp), 8192, [(lambda slot: slot.rearrange("p (c n) -> p c n", n=D),
                                                  wo[512 * grp:512 * grp + 512, :].rearrange("(c p) n -> p c n", p=128))])

    def emit_hgrn(self, l):
        kb = self.kb
        j = l // 2
        self.alloc_hgrn()
        lb, blb, oml, boml, noml, bnoml = self.lb[j]
        S, _bS = self.hS
        bSh = [Buf(f"hS{i}") for i in range(16)]
        ones_f, bones = self.h_ones
        identb = self.cst("ident", bf=True)
        onesb = self.cst("ones", bf=True)
        U64 = self.cst("U")[0:64, 0:64].rearrange("p (o t) -> p o t", o=1).to_broadcast([64, 8, 64])
        gn = self.pcol(f"gn{j}")

        if self.first_tile:
            kb.op("dve", lambda e: e.memset(S, 0.0), writes=bSh)
        else:
            kb.dma("pool", S.rearrange("p a b -> p (a b)"), self.st_d[l], f"stin{l}", writes=bSh)
        kb.op("dve", lambda e: e.memset(ones_f, 1.0), writes=[bones])
        for i in range(2):
            bp_, bbp_ = self.h_pb[i]["bp"]
            kb.op("dve", lambda e, bp_=bp_: e.memset(bp_[:, 0:1], 0.0), writes=[bbp_])

        v3 = lambda ap: ap.rearrange("p (c t) -> p c t", t=64)
        bc = lambda ap: ap.to_broadcast([128, 8, 64])
        pbanks = {}

        def gen_A(h):
            slot, bs = self.wget()
            sv = slot.rearrange("p (c kc n) -> p c kc n", c=4, n=128)
            pb = []
            for c in range(4):
                bank, bb = self.next_bank("P")
                self.mm(bb, [(bank, sv[:, c, kc, :], self.uT[:, kc, :]) for kc in range(KC)], reads=[bs, self.Bu])
                pb.append((bank, bb))
                if c < 3:
                    yield
            pbanks[h] = pb
            stage_B1(h)
            yield

        def stage_B1(h):
            (q_ps, bq_ps), (f_ps, bf_ps), (v_ps, bv_ps), (g_ps, bg_ps) = pbanks.pop(h)
            I = self.h_in[h % 4]
            q, bq = I["q"]; sg, bsg = I["sg"]; gs, bgs = I["gs"]; vT, bvT = I["vT"]
            kb.op("act", lambda e: e.activation(q, q_ps, AF.Silu), reads=[bq_ps], writes=[bq])
            kb.op("act", lambda e: e.activation(gs, g_ps, AF.Silu), reads=[bg_ps], writes=[bgs])
            kb.op("act", lambda e: e.activation(sg, f_ps, AF.Sigmoid), reads=[bf_ps], writes=[bsg])
            kb.op("act", lambda e: e.copy(vT, v_ps), reads=[bv_ps], writes=[bvT])

        def stage_B2(h):
            par = h % 2
            hh = h % 4
            oT, boT = self.h_oT[(h // 4) % 2]
            I = self.h_in[h % 4]
            q, bq = I["q"]; sg, bsg = I["sg"]; gs, bgs = I["gs"]; vT, bvT = I["vT"]
            Pb = self.h_pb[par]
            k_, bk = Pb["k"]; bp, bbp = Pb["bp"]; d1, bd1 = Pb["d1"]; d2, bd2 = Pb["d2"]; d3, bd3 = Pb["d3"]
            qm, bqm = Pb["qm"]; km, bkm = Pb["km"]; qa, bqa = Pb["qa"]; kl, bkl = Pb["kl"]
            vtm, bvtm = Pb["vtm"]; kltm, bkltm = Pb["kltm"]; A, bA = Pb["A"]; Sbf, bSbf = Pb["Sbf"]
            dd, bdd = Pb["dd"]; dec, bdec = Pb["dec"]
            bS = bSh[h]
            lf, blf = d3, bd3
            sbk = [4 + 2 * par, 5 + 2 * par]
            nb = [0]

            def nbank():
                i = sbk[nb[0] % 2]
                nb[0] += 1
                return self.banks[i], self.bbufs[i]
            b3 = bp[:, 1:T + 1].rearrange("p (c t) -> p c t", t=64)
            bst = bp[:, 0:T].rearrange("p (c t) -> p c t", t=64)[:, :, 0:1]
            bmid = b3[:, :, 31:32]
            blast = b3[:, :, 63:64]
            kb.op("act", lambda e: e.activation(lf, sg, AF.Ln, bias=lb[:, h:h + 1], scale=oml[:, h:h + 1]), reads=[bsg, blb, boml], writes=[blf])
            yield
            kb.op("dve", lambda e: e.tensor_scalar(k_, sg, noml[:, h:h + 1], oml[:, h:h + 1], op0=ALU.mult, op1=ALU.add), reads=[bsg, bnoml, boml], writes=[bk])
            kb.op("dve", lambda e: e.tensor_tensor_scan(bp[:, 1:T + 1], ones_f, lf, 0.0, ALU.mult, ALU.add), reads=[bones, blf], writes=[bbp])
            kb.op("dve", lambda e: e.tensor_tensor(v3(d1), b3, bc(bmid), ALU.subtract), reads=[bbp], writes=[bd1])
            yield
            kb.op("act", lambda e: e.activation(d2, d1, AF.Exp), reads=[bd1], writes=[bd2])
            kb.op("act", lambda e: e.activation(d3, d1, AF.Exp, scale=-1.0), reads=[bd1], writes=[bd3])
            yield
            kb.op("dve", lambda e: e.tensor_tensor(qm, q, d2, ALU.mult), reads=[bq, bd2], writes=[bqm])
            kb.op("dve", lambda e: e.tensor_tensor(km, k_, d3, ALU.mult), reads=[bk, bd3], writes=[bkm])
            kb.op("dve", lambda e: e.tensor_tensor(v3(d1), b3, bc(bst), ALU.subtract), reads=[bbp], writes=[bd1])
            yield
            kb.op("act", lambda e: e.activation(d2, d1, AF.Exp), reads=[bd1], writes=[bd2])
            yield
            kb.op("dve", lambda e: e.tensor_tensor(qa, q, d2, ALU.mult), reads=[bq, bd2], writes=[bqa])
            kb.op("dve", lambda e: e.tensor_tensor(v3(d1), b3, bc(blast), ALU.subtract), reads=[bbp], writes=[bd1])
            kb.op("dve", lambda e: e.tensor_tensor(dd.rearrange("p (c o) -> p c o", o=1), blast, bst, ALU.subtract), reads=[bbp], writes=[bdd])
            yield
            kb.op("act", lambda e: e.activation(d3, d1, AF.Exp, scale=-1.0), reads=[bd1], writes=[bd3])
            kb.op("act", lambda e: e.activation(dec, dd, AF.Exp), reads=[bdd], writes=[bdec])
            yield
            kb.op("dve", lambda e: e.tensor_tensor(kl, k_, d3, ALU.mult), reads=[bk, bd3], writes=[bkl])
            yield
            bank, bb = nbank()
            bkb = bank.bitcast(BF16)
            self.tr_multi(bb, [(bkb[0:64, c * 128:(c + 1) * 128], vT[:, c * 64:(c + 1) * 64]) for c in range(8)], identb, reads=[bvT, self.Bcb])
            bank2, bb2 = nbank()
            bkb2 = bank2.bitcast(BF16)
            self.tr_multi(bb2, [(bkb2[0:64, c * 128:(c + 1) * 128], kl[:, c * 64:(c + 1) * 64]) for c in range(8)], identb, reads=[bkl, self.Bcb])
            yield
            kb.op("act", lambda e: e.copy(vtm.rearrange("p c v -> p (c v)"), bkb[0:64, :]), reads=[bb], writes=[bvtm])
            kb.op("dve", lambda e: e.tensor_copy(kltm.rearrange("p c v -> p (c v)"), bkb2[0:64, :]), reads=[bb2], writes=[bkltm])
            yield
            bankA, bbA = nbank()
            self.mm_multi(bbA, [[(bankA[0:64, c * 64:(c + 1) * 64], km[:, c * 64:(c + 1) * 64], qm[:, c * 64:(c + 1) * 64])] for c in range(8)], reads=[bkm, bqm])
            dS = []
            bankd, bbd = nbank()
            self.mm_multi(bbd, [[(bankd[:, cc * 128:(cc + 1) * 128], kltm[:, cc, :], vtm[:, cc, :])] for cc in range(4)], reads=[bkltm, bvtm])
            dS.append((bankd, bbd))
            yield
            kb.op("dve", lambda e: e.tensor_tensor(A, bankA[0:64, :].rearrange("p (c t) -> p c t", t=64), U64, ALU.mult), reads=[bbA, self.Bcf], writes=[bA])
            yield
            bankd, bbd = nbank()
            self.mm_multi(bbd, [[(bankd[:, cc * 128:(cc + 1) * 128], kltm[:, 4 + cc, :], vtm[:, 4 + cc, :])] for cc in range(4)], reads=[bkltm, bvtm])
            dS.append((bankd, bbd))
            yield
            for c in range(8):
                bank, bb = dS[c // 4]
                kb.op("act", lambda e, c=c: e.copy(Sbf[:, c, :], S[:, h, :]), reads=[bS], writes=[bSbf])
                kb.op("dve", lambda e, c=c, bank=bank: e.scalar_tensor_tensor(S[:, h, :], S[:, h, :], dec[:, c:c + 1], bank[:, (c % 4) * 128:(c % 4 + 1) * 128], op0=ALU.mult, op1=ALU.add),
                      reads=[bS, bdec, bb], writes=[bS])
                yield
            o_ps, bo_ps = nbank()
            self.mm_multi(bo_ps, [[(o_ps[:, c * 64:(c + 1) * 64], vtm[:, c, :], A[:, c, :]),
                                   (o_ps[:, c * 64:(c + 1) * 64], Sbf[:, c, :], qa[:, c * 64:(c + 1) * 64])] for c in range(8)],
                          reads=[bvtm, bA, bSbf, bqa])
            yield
            sqt, bsq = self.sqring[par]
            kb.op("act", lambda e: e.activation(sqt, o_ps, AF.Square), reads=[bo_ps], writes=[bsq])
            yield
            bankn, bbn = nbank()
            self.mm(bbn, [(bankn, onesb, sqt)], reads=[bsq, self.Bcb])
            yield
            kb.op("act", lambda e: e.activation(d1, bankn, AF.Ln, bias=EPS, scale=1.0 / 128), reads=[bbn], writes=[bd1])
            kb.op("act", lambda e: e.activation(d2, d1, AF.Exp, scale=-0.5), reads=[bd1], writes=[bd2])
            yield
            kb.op("dve", lambda e: e.scalar_tensor_tensor(d3, o_ps, gn[:, 0:1], d2, op0=ALU.mult, op1=ALU.mult), reads=[bo_ps, bd2, self.Bpc], writes=[bd3])
            kb.op("dve", lambda e: e.tensor_tensor(oT[:, hh, :], d3, gs, ALU.mult), reads=[bd3, bgs], writes=[boT])
            yield

        def gen_O(grp):
            oT, boT = self.h_oT[grp % 2]
            slot, bs = self.wget()
            sv = slot.rearrange("p (c n) -> p c n", n=D)
            for m in range(KC):
                bank, bb = self.next_bank("P")
                self.mm(bb, [(bank, sv[:, c, m * 128:(m + 1) * 128], oT[:, c, :]) for c in range(4)], reads=[bs, boT])
                kb.op("dve", lambda e, m=m, bank=bank: e.tensor_tensor(self.hT[:, m, :], self.hT[:, m, :], bank, ALU.add), reads=[bb, self.Bh[m]], writes=[self.Bh[m]])
                if m % 4 == 3:
                    yield

        def chain(gens):
            for g_ in gens:
                yield from g_

        def drain(gen):
            for _ in gen:
                pass

        acc = [0.0]

        def rr2(mains, bg, rate=1.0):
            mains = list(mains)
            while mains:
                for gen in list(mains):
                    try:
                        next(gen)
                    except StopIteration:
                        mains.remove(gen)
                acc[0] += rate
                while acc[0] >= 1.0:
                    acc[0] -= 1.0
                    try:
                        next(bg)
                    except StopIteration:
                        pass

        drain(gen_A(0))
        drain(gen_A(1))
        for p in range(8):
            items = []
            kinds = []
            if 2 * p + 2 < 16:
                items += [gen_A(2 * p + 2), gen_A(2 * p + 3)]
                kinds += ["A", "A"]
            if p % 2 == 0 and p >= 2:
                items.append(gen_O((p - 2) // 2))
                kinds.append("O")
            npieces = sum(5 if it_[0] == "A" else 4 for it_ in kinds)
            bg = chain(items)
            rr2([stage_B2(2 * p), stage_B2(2 * p + 1)], bg, rate=npieces / 25.0)
            drain(bg)
        drain(gen_O(3))
        self._dma_toks.append(kb.dma("pool", self.st_d[l], S.rearrange("p a b -> p (a b)"), f"stout{l}", reads=bSh))

    def alloc_mamba(self):
        self.carve_reset()
        c = self.carve
        self.m_h2 = [c(f"m_h{i}", [128, T], F32) for i in range(2)]
        self.m_dt = c("m_dt", [128, 4, 64], F32)
        self.m_e = c("m_e", [128, 4, 64], F32)
        self.m_dtA = c("m_dtA", [128, 4, 64], F32)
        self.m_acs = c("m_acs", [128, 4, 64], F32)
        self.m_eacs = c("m_eacs", [128, 4, 64], F32)
        self.m_dst = c("m_dst", [128, 4, 64], F32)
        self.m_cdec = c("m_cdec", [128, 4, 64], F32)
        self.m_xc2 = [c(f"m_xc{i}", [128, 4, T], BF16) for i in range(2)]
        self.m_BT2 = [c(f"m_BT{i}", [128, T], BF16) for i in range(2)]
        self.m_CT2 = [c(f"m_CT{i}", [128, T], BF16) for i in range(2)]
        self.m_zs2 = [c(f"m_zs{i}", [128, 4, T], BF16) for i in range(2)]
        self.xs = [c(f"mxs{i}", [128, T + 4], F32) for i in range(2)]
        self.yc = [c(f"myc{i}", [128, T], F32) for i in range(2)]
        self.m_cb = []
        for i in range(2):
            d = {}
            d["xtm"] = c(f"m_xtm{i}", [128, T], F32)
            d["X"] = c(f"m_X{i}", [128, 8, 64], BF16)
            d["Xh"] = c(f"m_Xh{i}", [128, 8, 64], BF16)
            d["Btm"] = c(f"m_Btm{i}", [128, 128], BF16)
            d["CBm"] = c(f"m_CBm{i}", [128, 128], BF16)
            d["Lh"] = c(f"m_Lh{i}", [128, 8, 128], F32)
            d["E"] = c(f"m_E{i}", [128, 8, 128], BF16)
            d["M"] = c(f"m_M{i}", [128, 8, 128], BF16)
            d["t1"] = c(f"m_t1{i}", [128, T], F32)
            d["t2"] = c(f"m_t2{i}", [128, T], F32)
            d["ygn"] = c(f"m_ygn{i}", [128, T], BF16)
            d["ss"] = c(f"m_ss{i}", [128, 4], F32)
            self.m_cb.append(d)
        self.m_hbf = c("m_hbf", [128, T], BF16)
        self.m_yT = [c(f"m_yT{i}", [128, 4, T], BF16) for i in range(2)]

    @staticmethod
    def mamba_seq():
        seq = [("Z", 0), ("X", 0), ("BC", 0)]
        for g in range(8):
            if g < 7:
                seq += [("Z", g + 1), ("X", g + 1), ("BC", g + 1)]
            if g >= 1:
                seq.append(("O", g - 1))
        seq.append(("O", 7))
        return seq

    def plan_mamba(self, l):
        j = l // 2
        wi = self.W["m_w_in"][j]
        wo = self.W["m_w_out"][j]
        self.padd(("mdt", l), 1024, [(self.wdst(0, 16, 64), self.wsrc(wi, 10240, 64))])
        for (kind, g) in self.mamba_seq():
            if kind == "Z":
                self.padd(("mz", l, g), 8192, [(self.wdst(0, 16, 512), self.wsrc(wi, 512 * g, 512))])
            elif kind == "X":
                self.padd(("mx", l, g), 8192, [(self.wdst(0, 16, 512), self.wsrc(wi, 4096 + 512 * g, 512))])
            elif kind == "BC":
                self.padd(("mbc", l, g), 4096, [(self.wdst(0, 16, 128), self.wsrc(wi, 8192 + 128 * g, 128)),
                                                (self.wdst(2048, 16, 128), self.wsrc(wi, 9216 + 128 * g, 128))])
            elif kind == "O":
                self.padd(("mo", l, g), 8192, [(lambda slot: slot.rearrange("p (c n) -> p c n", n=D),
                                                wo[512 * g:512 * g + 512, :].rearrange("(c p) n -> p c n", p=128))])

    def emit_mamba(self, l):
        kb = self.kb
        j = l // 2
        self.alloc_mamba()
        dt, bdt = self.m_dt
        ee, bee = self.m_e
        dtA, bdtA = self.m_dtA
        acs, bacs = self.m_acs
        eacs, beacs = self.m_eacs
        dst, bdst = self.m_dst
        cdec, bcdec = self.m_cdec
        hbf, bhbf = self.m_hbf
        tail, btail = self.mtail[j]
        identb = self.cst("ident", bf=True)
        Uf = self.cst("U")
        onesf = self.cst("ones")
        MS = self.cst("MS")
        Arow, bArow = self.Arow[j]
        dtb = self.pcol(f"dtb{j}")
        dsk = self.pcol(f"dsk{j}")
        wkeys = [f"mcw{j}_{k}" for k in range(4)]

        first_tile = self.first_tile

        slot, bs = self.wget()
        sv = slot[:, 0:16 * 64].rearrange("p (kc n) -> p kc n", n=64)
        bank, bb = self.next_bank()
        self.mm_multi(bb, [[(bank[:, sub * 64:(sub + 1) * 64], self.uT[:, kc, sub * 128:(sub + 1) * 128], sv[:, kc, :]) for kc in range(KC)] for sub in range(4)], reads=[bs, self.Bu])
        b4 = lambda ap: ap.rearrange("p (s h) -> p s h", h=64)
        kb.op("dve", lambda e, bank=bank: e.tensor_tensor(dt, b4(bank[:, 0:256]), dtb.rearrange("p (o h) -> p o h", o=1).to_broadcast([128, 4, 64]), ALU.add), reads=[bb, self.Bpc], writes=[bdt])
        kb.op("act", lambda e: e.activation(ee, dt, AF.Exp), reads=[bdt], writes=[bee])
        kb.op("act", lambda e: e.activation(dt, ee, AF.Ln, bias=1.0), reads=[bee], writes=[bdt])
        kb.op("dve", lambda e: e.tensor_tensor(dtA, dt, Arow.rearrange("p (o h) -> p o h", o=1).to_broadcast([128, 4, 64]), ALU.mult), reads=[bdt, bArow], writes=[bdtA])
        bank, bb = self.next_bank()
        self.mm_multi(bb, [[(bank[:, sub * 64:(sub + 1) * 64], Uf, dtA[:, sub, :])] for sub in range(4)], reads=[bdtA, self.Bcf])
        kb.op("act", lambda e, bank=bank: e.copy(acs, b4(bank[:, 0:256])), reads=[bb], writes=[bacs])
        kb.op("act", lambda e, bank=bank: e.activation(eacs, b4(bank[:, 0:256]), AF.Exp), reads=[bb], writes=[beacs])
        bank, bb = self.next_bank()
        self.mm_multi(bb, [[(bank[:, sub * 64:(sub + 1) * 64], onesf, dtA[:, sub, :])] for sub in range(4)], reads=[bdtA, self.Bcf])
        kb.op("dve", lambda e, bank=bank: e.tensor_tensor(dst, b4(bank[:, 0:256]), acs, ALU.subtract), reads=[bb, bacs], writes=[bdst])
        kb.op("act", lambda e: e.activation(dst, dst, AF.Exp), reads=[bdst], writes=[bdst])
        kb.op("act", lambda e, bank=bank: e.activation(cdec, b4(bank[:, 0:256]), AF.Exp), reads=[bb], writes=[bcdec])

        ringc = [0]
        hb8 = lambda ap, g: ap[:, 8 * g:8 * g + 8].rearrange("p (h o) -> p h o", o=1).to_broadcast([128, 8, 64])

        def gen_Z(g):
            zs, bzs = self.m_zs2[g % 2]
            hS, bhS = self.m_h2[g % 2]
            if first_tile:
                kb.op("dve", lambda e: e.memset(hS, 0.0), writes=[bhS])
            else:
                kb.dma("pool", hS, self.st_d[l][:, 512 * g:512 * g + 512], f"stin{l}_{g % 2}", writes=[bhS])
            slot, bs = self.wget()
            sv = slot.rearrange("p (kc n) -> p kc n", n=512)
            for sub in range(4):
                bank, bb = self.next_bank("P")
                self.mm(bb, [(bank, self.uT[:, kc, sub * 128:(sub + 1) * 128], sv[:, kc, :]) for kc in range(KC)], reads=[bs, self.Bu])
                kb.op("act", lambda e, bank=bank, sub=sub: e.activation(zs[:, sub, :], bank, AF.Silu), reads=[bb], writes=[bzs])
                yield

        def gen_X(g):
            xc, bxc = self.m_xc2[g % 2]
            ring = ringc[0]
            slot, bs = self.wget()
            sv = slot.rearrange("p (kc n) -> p kc n", n=512)
            for c in range(4):
                ring = ringc[0]
                bank, bb = self.next_bank("P")
                self.mm(bb, [(bank, sv[:, kc, c * 128:(c + 1) * 128], self.uT[:, kc, :]) for kc in range(KC)], reads=[bs, self.Bu])
                y_, by_ = self.conv_chunk(bank, bb, tail, btail, 4 * g + c, wkeys, f"mcb{j}", 4, ring)
                ring += 1
                kb.op("act", lambda e, y_=y_, c=c: e.activation(xc[:, c, :], y_, AF.Silu), reads=[by_], writes=[bxc])
                ringc[0] = ring
                yield
            ringc[0] = ring

        def gen_BC(g):
            BT, bBT = self.m_BT2[g % 2]
            CT, bCT = self.m_CT2[g % 2]
            ring = ringc[0]
            slot, bs = self.wget()
            sv = slot[:, 0:4096].rearrange("p (c kc n) -> p c kc n", c=2, n=128)
            for c, (dstT, bdstT, ci) in enumerate([(BT, bBT, 32 + g), (CT, bCT, 40 + g)]):
                ring = ringc[0]
                bank, bb = self.next_bank("P")
                self.mm(bb, [(bank, sv[:, c, kc, :], self.uT[:, kc, :]) for kc in range(KC)], reads=[bs, self.Bu])
                y_, by_ = self.conv_chunk(bank, bb, tail, btail, ci, wkeys, f"mcb{j}", 4, ring)
                ring += 1
                kb.op("act", lambda e, y_=y_, dstT=dstT: e.activation(dstT, y_, AF.Silu), reads=[by_], writes=[bdstT])
                ringc[0] = ring
                yield
            ringc[0] = ring

        def c_front(g, s_, B):
            xc, bxc = self.m_xc2[g % 2]
            BT, bBT = self.m_BT2[g % 2]
            CT, bCT = self.m_CT2[g % 2]
            xtm, bxtm = B["xtm"]; X, bX = B["X"]; Xh, bXh = B["Xh"]; Btm, bBtm = B["Btm"]
            CBm, bCBm = B["CBm"]; Lh, bLh = B["Lh"]; E, bE = B["E"]; M, bM = B["M"]
            tc = slice(128 * s_, 128 * s_ + 128)
            bank, bb = self.next_bank("S")
            bkb = bank.bitcast(BF16)
            self.tr_multi(bb, [(bkb[:, c * 128:(c + 1) * 128], xc[:, c, tc]) for c in range(4)] + [(bkb[:, 512:640], BT[:, tc])], identb, reads=[bxc, bBT, self.Bcb])
            yield
            kb.op("act", lambda e: e.copy(xtm, bkb[:, 0:512]), reads=[bb], writes=[bxtm])
            kb.op("act", lambda e: e.copy(Btm, bkb[:, 512:640]), reads=[bb], writes=[bBtm])
            yield
            x3 = xtm.rearrange("p (h q) -> p h q", q=64)
            kb.op("dve", lambda e: e.tensor_tensor(X, x3, hb8(dt[:, s_, :], g), ALU.mult), reads=[bxtm, bdt], writes=[bX])
            kb.op("dve", lambda e: e.tensor_tensor(Xh, X, hb8(dst[:, s_, :], g), ALU.mult), reads=[bX, bdst], writes=[bXh])
            kb.op("dve", lambda e: e.tensor_tensor(Lh, MS.rearrange("p (o t) -> p o t", o=1).to_broadcast([128, 8, 128]),
                                                     dtA[:, s_, 8 * g:8 * g + 8].rearrange("p (h o) -> p h o", o=1).to_broadcast([128, 8, 128]), ALU.mult),
                  reads=[bdtA, self.Bcf], writes=[bLh])
            yield
            bankc, bbc = self.next_bank("S")
            self.mm(bbc, [(bankc[:, 0:128], BT[:, tc], CT[:, tc])], reads=[bBT, bCT])
            yield
            kb.op("dve", lambda e: e.tensor_tensor(CBm, bankc[:, 0:128], Uf, ALU.mult), reads=[bbc, self.Bcf], writes=[bCBm])
            yield
            dbanks = []
            for half in range(2):
                bank2, bb2 = self.next_bank("S")
                self.mm_multi(bb2, [[(bank2[:, q * 128:(q + 1) * 128], Lh[:, 4 * half + q, :], Uf)] for q in range(4)], reads=[bLh, self.Bcf])
                dbanks.append((bank2, bb2))
            yield
            for half, (bank2, bb2) in enumerate(dbanks):
                kb.op("act", lambda e, bank2=bank2, half=half: e.activation(E[:, 4 * half:4 * half + 4, :], bank2.rearrange("p (h t) -> p h t", t=128), AF.Exp), reads=[bb2], writes=[bE])
            yield
            kb.op("dve", lambda e: e.tensor_tensor(M, E, CBm.rearrange("p (o t) -> p o t", o=1).to_broadcast([128, 8, 128]), ALU.mult), reads=[bE, bCBm], writes=[bM])
            yield

        def c_mid(g, s_, B):
            CT, bCT = self.m_CT2[g % 2]
            hS, bhS = self.m_h2[g % 2]
            X, bX = B["X"]; Xh, bXh = B["Xh"]; Btm, bBtm = B["Btm"]; M, bM = B["M"]; t1, bt1 = B["t1"]
            tc = slice(128 * s_, 128 * s_ + 128)
            bankd, bbd = self.next_bank("S")
            self.mm_multi(bbd, [[(bankd[:, q * 64:(q + 1) * 64], M[:, q, :], X[:, q, :])] for q in range(8)], reads=[bM, bX])
            banko, bbo = self.next_bank("S")
            self.mm(bbo, [(banko, CT[:, tc], hbf)], reads=[bCT, bhbf])
            yield
            t13 = t1.rearrange("p (h q) -> p h q", q=64)
            kb.op("dve", lambda e: e.tensor_tensor(t13, banko.rearrange("p (h q) -> p h q", q=64), hb8(eacs[:, s_, :], g), ALU.mult), reads=[bbo, beacs], writes=[bt1])
            kb.op("dve", lambda e: e.tensor_tensor(t1, t1, bankd, ALU.add), reads=[bt1, bbd], writes=[bt1])
            yield
            banks_, bbs = self.next_bank("S")
            self.mm(bbs, [(banks_, Btm, Xh.rearrange("p h q -> p (h q)"))], reads=[bBtm, bXh])
            yield
            h3 = hS.rearrange("p (h q) -> p h q", q=64)
            kb.op("dve", lambda e: e.tensor_tensor(h3, h3, hb8(cdec[:, s_, :], g), ALU.mult), reads=[bhS, bcdec], writes=[bhS])
            kb.op("dve", lambda e: e.tensor_tensor(hS, hS, banks_, ALU.add), reads=[bhS, bbs], writes=[bhS])
            if s_ < 3:
                kb.op("act", lambda e: e.copy(hbf, hS), reads=[bhS], writes=[bhbf])
            yield

        def c_tail(g, s_, B):
            yT, byT = self.m_yT[g % 2]
            zs, bzs = self.m_zs2[g % 2]
            xtm, bxtm = B["xtm"]; t1, bt1 = B["t1"]; t2, bt2 = B["t2"]; ygn, bygn = B["ygn"]; ss, bss = B["ss"]
            tc = slice(128 * s_, 128 * s_ + 128)
            x3 = xtm.rearrange("p (h q) -> p h q", q=64)
            t23 = t2.rearrange("p (h q) -> p h q", q=64)
            kb.op("dve", lambda e: e.tensor_tensor(t23, x3, hb8(dsk, g), ALU.mult), reads=[bxtm, self.Bpc], writes=[bt2])
            kb.op("dve", lambda e: e.tensor_tensor(t1, t1, t2, ALU.add), reads=[bt1, bt2], writes=[bt1])
            kb.op("dve", lambda e: e.tensor_tensor(t1, t1, zs[:, s_, :], ALU.mult), reads=[bt1, bzs], writes=[bt1])
            yield
            kb.op("act", lambda e: e.activation(t2, t1, AF.Square, accum_out=ss[:, 0:1]), reads=[bt1], writes=[bt2, bss])
            kb.op("act", lambda e: e.activation(ss[:, 1:2], ss[:, 0:1], AF.Ln, bias=EPS, scale=1.0 / 512), reads=[bss], writes=[bss])
            kb.op("act", lambda e: e.activation(ss[:, 2:3], ss[:, 1:2], AF.Exp, scale=-0.5), reads=[bss], writes=[bss])
            yield
            kb.op("dve", lambda e: e.tensor_scalar(ygn, t1, ss[:, 2:3], None, op0=ALU.mult), reads=[bt1, bss], writes=[bygn])
            yield
            bank, bb = self.next_bank("S")
            bkb = bank.bitcast(BF16)
            self.tr_multi(bb, [(bkb[:, c * 128:(c + 1) * 128], ygn[:, c * 128:(c + 1) * 128]) for c in range(4)], identb, reads=[bygn, self.Bcb])
            yield
            o_, _n = self.pco[f"mnw{j}"]
            kb.op("dve", lambda e: e.tensor_tensor(yT[:, :, tc], bkb[:, 0:512].rearrange("p (c t) -> p c t", t=128),
                                                     self.pc[:, o_ + 4 * g:o_ + 4 * g + 4].rearrange("p (c o) -> p c o", o=1).to_broadcast([128, 4, 128]), ALU.mult),
                  reads=[bb, self.Bpc], writes=[byT])
            yield

        def chain(gens):
            for g_ in gens:
                yield from g_

        def drain(gen):
            for _ in gen:
                pass

        acc = [0.0]

        def rr2(mains, bg, rate=1.0):
            mains = list(mains)
            while mains:
                for gen in list(mains):
                    try:
                        next(gen)
                    except StopIteration:
                        mains.remove(gen)
                acc[0] += rate
                while acc[0] >= 1.0:
                    acc[0] -= 1.0
                    try:
                        next(bg)
                    except StopIteration:
                        pass

        def do_pair(g, pair, bg, rate):
            hS, bhS = self.m_h2[g % 2]
            sa, sb = 2 * pair, 2 * pair + 1
            Ba, Bb = self.m_cb[0], self.m_cb[1]
            if pair == 0:
                kb.op("act", lambda e: e.copy(hbf, hS), reads=[bhS], writes=[bhbf])
            rr2([c_front(g, sa, Ba), c_front(g, sb, Bb)], bg, rate)
            rr2([chain([c_mid(g, sa, Ba), c_mid(g, sb, Bb)])], bg, rate)
            rr2([c_tail(g, sa, Ba), c_tail(g, sb, Bb)], bg, rate)
            if pair == 1:
                self._dma_toks.append(kb.dma("pool", self.st_d[l][:, 512 * g:512 * g + 512], hS, f"stout{l}_{g % 2}", reads=[bhS]))

        def gen_O(g):
            yT, byT = self.m_yT[g % 2]
            slot, bs = self.wget()
            sv = slot.rearrange("p (c n) -> p c n", n=D)
            for m in range(KC):
                bank, bb = self.next_bank("P")
                self.mm(bb, [(bank, sv[:, c, m * 128:(m + 1) * 128], yT[:, c, :]) for c in range(4)], reads=[bs, byT])
                kb.op("dve", lambda e, m=m, bank=bank: e.tensor_tensor(self.hT[:, m, :], self.hT[:, m, :], bank, ALU.add), reads=[bb, self.Bh[m]], writes=[self.Bh[m]])
                if m % 4 == 3:
                    yield

        drain(gen_Z(0))
        drain(gen_X(0))
        drain(gen_BC(0))
        for g in range(8):
            items = ([gen_Z(g + 1), gen_X(g + 1), gen_BC(g + 1)] if g < 7 else []) + ([gen_O(g - 1)] if g >= 1 else [])
            npieces = (10 if g < 7 else 0) + (4 if g >= 1 else 0)
            bg = chain(items)
            rate = npieces / 40.0
            do_pair(g, 0, bg, rate)
            do_pair(g, 1, bg, rate)
            drain(bg)
        drain(gen_O(7))


_CACHE = {}


def build_nc(inputs_offs, **kw):
    pc_offs, cc_offs = inputs_offs
    p = Prog(**kw)
    nc = p.build(pc_offs, cc_offs)
    return nc


def kernel(**inputs):
    inp = {k: np.asarray(v) for k, v in inputs.items()}
    pcols, pc_offs = pack_params(inp)
    consts, cc_offs = make_consts()
    nc = build_nc((pc_offs, cc_offs))
    x = np.ascontiguousarray(inp["x"], dtype=np.float32)
    in_maps = []
    for c in range(8):
        m = {"x": x[2 * c:2 * c + 2], "pcols": pcols, "consts": consts}
        for k in ("hgrn_w_in", "hgrn_w_out", "m_w_in", "m_w_out", "f_w_up", "f_w_down"):
            m[k] = np.ascontiguousarray(inp[k], dtype=np.float32)
        in_maps.append(m)
    res = run_bass_kernel_spmd(nc, in_maps, core_ids=list(range(8)))
    return np.concatenate([np.asarray(r["out"]) for r in res.results], axis=0).astype(np.float32)
```

```python
import numpy as np
import concourse.bass as bass
import concourse.mybir as mybir
from concourse.bass_utils import run_bass_kernel_spmd

F32 = mybir.dt.float32
BF16 = mybir.dt.bfloat16
ALU = mybir.AluOpType
AF = mybir.ActivationFunctionType

D = 2048
KC = 16
T = 512
NSUB = T // 128
SEQ = 2048
EPS = 1e-5
DFF = 5632
M_DI = 4096
M_CONVD = 6144
M_IN = 10304


class Buf:
    __slots__ = ("name", "w", "r")

    def __init__(self, name):
        self.name = name
        self.w = None
        self.r = {}


class KB:
    SEM_ROLL = 30000

    def __init__(self, nc):
        self.nc = nc
        self.engs = {"pe": nc.tensor, "dve": nc.vector, "act": nc.scalar,
                     "pool": nc.gpsimd, "sp": nc.sync}
        self.sem = {}
        self.cnt = {}
        self.nsem = 0
        self.seen = {e: {} for e in self.engs}
        self.dsem = {}
        self.nwait = 0
        self.nins = {e: 0 for e in self.engs}

    def _newsem(self, name):
        self.nsem += 1
        return self.nc.alloc_semaphore(f"{name}_{self.nsem}")

    def _bump(self, eng):
        if eng not in self.sem or self.cnt[eng] >= self.SEM_ROLL:
            self.sem[eng] = self._newsem("s_" + eng)
            self.cnt[eng] = 0
        self.cnt[eng] += 1
        return (id(self.sem[eng]), self.sem[eng], self.cnt[eng], eng)

    def _wait(self, eng, deps):
        h = self.engs[eng]
        seen = self.seen[eng]
        for tok in deps:
            k, sem, val, src = tok
            if eng == "pe" and src == "pe":
                continue
            if seen.get(k, 0) >= val:
                continue
            h.wait_ge(sem, val)
            self.nwait += 1
            seen[k] = val

    @staticmethod
    def _deps(reads, writes):
        deps = []
        for b in reads:
            if b.w is not None:
                deps.append(b.w)
        for b in writes:
            if b.w is not None:
                deps.append(b.w)
            deps.extend(b.r.values())
        return deps

    def op(self, eng, fn, reads=(), writes=()):
        self._wait(eng, self._deps(reads, writes))
        ins = fn(self.engs[eng])
        self.nins[eng] += 1
        tok = self._bump(eng)
        ins.then_inc(tok[1], 1)
        for b in reads:
            b.r[eng] = tok
        for b in writes:
            b.w = tok
            b.r = {}
        return ins

    def dma(self, q, out, in_, key, reads=(), writes=()):
        self._wait(q, self._deps(reads, writes))
        if key not in self.dsem:
            self.dsem[key] = [self._newsem("d_" + key), 0]
        ent = self.dsem[key]
        ent[1] += 16
        ins = self.engs[q].dma_start(out=out, in_=in_)
        ins.then_inc(ent[0], 16)
        self.nins[q] += 1
        tok = (id(ent[0]), ent[0], ent[1], "dma:" + key)
        for b in reads:
            b.r["dma:" + key] = tok
        for b in writes:
            b.w = tok
            b.r = {}
        return tok

    def wait_all(self, eng, bufs):
        deps = []
        for b in bufs:
            if b.w is not None:
                deps.append(b.w)
            deps.extend(b.r.values())
        self._wait(eng, deps)


def _cols(v):
    v = np.asarray(v, np.float32)
    return np.ascontiguousarray(v.reshape(-1, 128).T)


def _pack(named):
    offs = {}
    parts = []
    o = 0
    for k, a in named:
        a = np.asarray(a, np.float32).reshape(128, -1)
        offs[k] = (o, a.shape[1])
        parts.append(a)
        o += a.shape[1]
    return np.ascontiguousarray(np.concatenate(parts, axis=1)), offs


def pack_params(inp):
    named = []
    for l in range(4):
        named.append((f"mixn{l}", _cols(inp["mix_norm"][l])))
        named.append((f"ffnn{l}", _cols(inp["ffn_norm"][l])))
        fcw = inp["f_conv_w"][l]
        for k in range(3):
            named.append((f"fcw{l}_{k}", _cols(fcw[k])))
        named.append((f"fcb{l}", _cols(inp["f_conv_b"][l])))
    named.append(("finn", _cols(inp["final_norm"])))
    for j in range(2):
        named.append((f"lbl{j}", _cols(inp["hgrn_lb_logits"][j])))
        named.append((f"gn{j}", _cols(inp["hgrn_gnorm"][j])))
        for k in range(4):
            named.append((f"mcw{j}_{k}", _cols(inp["m_conv_w"][j][k])))
        named.append((f"mcb{j}", _cols(inp["m_conv_b"][j])))
        named.append((f"mnw{j}", _cols(inp["m_norm"][j])))
        named.append((f"dtb{j}", np.broadcast_to(np.asarray(inp["m_dt_bias"][j], np.float32)[None, :], (128, 64))))
        named.append((f"alog{j}", np.broadcast_to(np.asarray(inp["m_A_log"][j], np.float32)[None, :], (128, 64))))
        named.append((f"dsk{j}", np.broadcast_to(np.asarray(inp["m_D"][j], np.float32)[None, :], (128, 64))))
    return _pack(named)


def make_consts():
    i = np.arange(128)
    ident = np.eye(128, dtype=np.float32)
    U = (i[:, None] <= i[None, :]).astype(np.float32)
    MS = (i[:, None] > i[None, :]).astype(np.float32)
    ones = np.ones((128, 128), np.float32)
    return _pack([("ident", ident), ("U", U), ("MS", MS), ("ones", ones)])


class Prog:
    def __init__(self, n_seq=2, tiles_per_seq=4, stages=None, final=True, nslots=3):
        self.n_seq = n_seq
        self.tiles_per_seq = tiles_per_seq
        self.final = final
        if stages is None:
            stages = []
            for l in range(4):
                stages.append(("hgrn" if l % 2 == 0 else "mamba", l))
                stages.append(("ffn", l))
        self.stages = stages
        self.nslots = nslots
        nc = bass.Bass("TRN2", target_bir_lowering=False)
        self.nc = nc
        self.kb = KB(nc)
        self._n = 0
        self._dma_toks = []
        self._scr_off = 0
        self.bank_i = 0
        self.bank_p = 0
        self.bank_s = 0

    def T_(self, name, shape, dt):
        t = self.nc.alloc_sbuf_tensor(name, list(shape), dt).ap()
        return t, Buf(name)

    def carve_reset(self):
        self._scr_off = 0

    def carve(self, name, shape, dt):
        esz = 4 if dt == F32 else 2
        n = int(np.prod(shape[1:]))
        nbytes = (n * esz + 31) // 32 * 32
        off = self._scr_off
        self._scr_off += nbytes
        assert self._scr_off <= self.SCR, (name, self._scr_off)
        v = self.scr[0:shape[0], off // 2: off // 2 + n * esz // 2]
        if dt == F32:
            v = v.bitcast(F32)
        if len(shape) == 3:
            v = v.rearrange("p (a b) -> p a b", b=shape[2])
        return v, Buf(name)

    def barrier(self):
        kb = self.kb
        engs = ["pe", "act", "dve", "pool"]
        toks = []
        for e in engs:
            if e in kb.sem:
                toks.append((id(kb.sem[e]), kb.sem[e], kb.cnt[e], e))
        toks.extend(self._dma_toks)
        self._dma_toks = []
        for e in engs:
            h = kb.engs[e]
            for (k, sem, val, src) in toks:
                if src == e or kb.seen[e].get(k, 0) >= val:
                    continue
                h.wait_ge(sem, val)
                kb.nwait += 1
                kb.seen[e][k] = val

    def next_bank(self, pool="A"):
        if pool == "A":
            i = self.bank_i % 8
            self.bank_i += 1
        elif pool == "P":
            i = self.bank_p % 4
            self.bank_p += 1
        else:
            i = 4 + self.bank_s % 4
            self.bank_s += 1
        return self.banks[i], self.bbufs[i]

    def mm(self, bank_buf, pairs, reads, first_start=True, last_stop=True):
        n = len(pairs)

        def fn(e):
            ins = None
            for i, (o, l, r) in enumerate(pairs):
                ins = e.matmul(o, l, r, start=(i == 0 and first_start), stop=(i == n - 1 and last_stop))
            return ins
        self.kb.nins["pe"] += n - 1
        assert bank_buf.w is None or bank_buf.r, bank_buf.name
        return self.kb.op("pe", fn, reads=reads, writes=[bank_buf])

    def mm_multi(self, bank_buf, groups, reads):
        def fn(e):
            ins = None
            for g in groups:
                n = len(g)
                for i, (o, l, r) in enumerate(g):
                    ins = e.matmul(o, l, r, start=(i == 0), stop=(i == n - 1))
            return ins
        self.kb.nins["pe"] += sum(len(g) for g in groups) - 1
        assert bank_buf.w is None or bank_buf.r, bank_buf.name
        return self.kb.op("pe", fn, reads=reads, writes=[bank_buf])

    def tr_multi(self, bank_buf, items, ident, reads):
        def fn(e):
            ins = None
            for (o, i_) in items:
                if i_.dtype == F32:
                    ins = e.matmul(o, i_, ident, start=True, stop=True)
                else:
                    ins = e.transpose(o, i_, ident)
            return ins
        self.kb.nins["pe"] += len(items) - 1
        assert bank_buf.w is None or bank_buf.r, bank_buf.name
        return self.kb.op("pe", fn, reads=reads, writes=[bank_buf])

    def build(self, pc_offs, cc_offs):
        nc, kb = self.nc, self.kb
        self.pco = pc_offs
        self.cco = cc_offs
        npc = max(o + n for o, n in pc_offs.values())
        ncc = max(o + n for o, n in cc_offs.values())
        ntok = self.tiles_per_seq * T
        dr = lambda name, shape: nc.dram_tensor(name, list(shape), F32, kind="ExternalInput").ap()
        self.x = dr("x", [self.n_seq, SEQ, D])
        self.pcols_d = dr("pcols", [128, npc])
        self.consts_d = dr("consts", [128, ncc])
        kinds = {k for k, _ in self.stages}
        wshapes = {"hgrn": {"hgrn_w_in": [2, D, 8192], "hgrn_w_out": [2, D, D]},
                   "mamba": {"m_w_in": [2, D, M_IN], "m_w_out": [2, M_DI, D]},
                   "ffn": {"f_w_up": [4, D, 2 * DFF], "f_w_down": [4, DFF, D]}}
        self.W = {}
        for k in ("hgrn", "mamba", "ffn"):
            if k in kinds:
                for nm, shp in wshapes[k].items():
                    self.W[nm] = dr(nm, shp)
        self.out = nc.dram_tensor("out", [self.n_seq, SEQ, D], F32, kind="ExternalOutput").ap()

        self.pc, self.Bpc = self.T_("pc", [128, npc], F32)
        self.cf, self.Bcf = self.T_("cf", [128, ncc], F32)
        self.cb, self.Bcb = self.T_("cb", [128, ncc], BF16)
        kb.dma("sp", self.pc, self.pcols_d, "pc", writes=[self.Bpc])
        kb.dma("sp", self.cf, self.consts_d, "cf", writes=[self.Bcf])
        kb.dma("pool", self.cb, self.consts_d, "cb", writes=[self.Bcb])

        self.banks = [nc.alloc_psum_tensor(f"bank{i}", [128, 512], F32).ap() for i in range(8)]
        self.bbufs = [Buf(f"bank{i}") for i in range(8)]

        self.hT, _ = self.T_("hT", [128, KC, T], F32)
        self.Bh = [Buf(f"h{m}") for m in range(KC)]
        self.uT, self.Bu = self.T_("uT", [128, KC, T], BF16)
        self.slots = [self.T_(f"wslot{i}", [128, 8192], BF16) for i in range(self.nslots)]
        self.plan = []
        self.issued = 0
        self.consumed = 0
        self.sqring = [self.T_(f"sqr{i}", [128, T], BF16) for i in range(4)]
        self.lnv, self.Blnv = self.T_("lnv", [128, T], F32)
        self.rstd, self.Brstd = self.T_("rstd", [128, T], F32)
        self.state_bufs = []
        self.ftail = []
        for l in range(4):
            t_, b_ = self.T_(f"ftail{l}", [128, 88, 2], F32)
            self.ftail.append((t_, b_))
            self.state_bufs.append((t_, b_))
        self.mtail = []
        for j in range(2):
            t_, b_ = self.T_(f"mtail{j}", [128, 48, 3], F32)
            self.mtail.append((t_, b_))
            self.state_bufs.append((t_, b_))
        self.SCR = 86 * 1024
        self.scr = nc.alloc_sbuf_tensor("scr", [128, self.SCR // 2], BF16).ap()
        self.st_d = {}
        for (kind, l) in self.stages:
            if kind == "hgrn":
                self.st_d[l] = nc.dram_tensor(f"st{l}", [128, 16 * 128], F32).ap()
            if kind == "mamba":
                self.st_d[l] = nc.dram_tensor(f"st{l}", [128, 64 * 64], F32).ap()
        print("sbuf bytes remaining/partition:", nc.sbuf_bytes_remaining)

        self.uniq = {}
        for s in range(self.n_seq):
            for t in range(self.tiles_per_seq):
                for si, (kind, l) in enumerate(self.stages):
                    self.cur_stage = si
                    getattr(self, "plan_" + kind)(l)
        print("weight slabs:", len(self.plan), "unique:", len(self.uniq))
        self.prologue_init()

        self.prep_params()
        ti = 0
        nst = len(self.stages)
        for s in range(self.n_seq):
            self.reset_state()
            for t in range(self.tiles_per_seq):
                self.first_tile = (t == 0)
                self.barrier()
                self.load_tile(s, t, ti)
                if ti == 0:
                    self.prologue_stage(0)
                for si, (kind, l) in enumerate(self.stages):
                    self.rmsnorm(("ffnn%d" if kind == "ffn" else "mixn%d") % l)
                    self.barrier()
                    if ti == 0 and si + 1 < nst:
                        self.prologue_stage(si + 1)
                    getattr(self, "emit_" + kind)(l)
                self.barrier()
                self.store_tile(s, t, ti)
                ti += 1
        assert self.consumed == len(self.plan), (self.consumed, len(self.plan))
        kb._wait("pool", self._dma_toks)
        print("instructions:", kb.nins, "waits:", kb.nwait, "sems:", kb.nsem)
        return nc

    def pcol(self, key, i=None):
        o, n = self.pco[key]
        if i is None:
            return self.pc[:, o:o + n]
        return self.pc[:, o + i:o + i + 1]

    def cst(self, key, bf=False):
        o, n = self.cco[key]
        return (self.cb if bf else self.cf)[:, o:o + n]

    def padd(self, key, nel, pieces):
        if key not in self.uniq:
            self.uniq[key] = (len(self.uniq), nel, pieces, self.cur_stage)
        self.plan.append(key)

    def prologue_init(self):
        nu = len(self.uniq)
        NPT = 96
        wts = [self.nc.dram_tensor(f"wbf{i}", [min(NPT, nu - i * NPT), 128, 8192], BF16).ap() for i in range((nu + NPT - 1) // NPT)]

        class _W:
            def __getitem__(_s, u):
                return wts[u // NPT][u % NPT]
        self.wbf = _W()
        self.wbuf = {}

    def prologue_stage(self, st):
        kb = self.kb
        items = [(key, v) for key, v in self.uniq.items() if v[3] == st]
        bounds = [0, 2, 6, 12, len(items)] if len(items) > 12 else [0, 2, 6, len(items)]
        for gi in range(len(bounds) - 1):
            g0, g1 = bounds[gi], bounds[gi + 1]
            if g0 >= g1:
                continue
            skey = f"pro{st}_{gi}"
            bufs = []
            for key, (u, nel, pieces, st_) in items[g0:g1]:
                for (dst_fn, src) in pieces:
                    kb.dma("pool", dst_fn(self.wbf[u]), src, skey)
                b = Buf(f"wbf{u}")
                self.wbuf[key] = b
                bufs.append(b)
            ent = kb.dsem[skey]
            tok = (id(ent[0]), ent[0], ent[1], "dma:" + skey)
            for b in bufs:
                b.w = tok

    def wget(self, hold_prev=False):
        kb = self.kb
        lim = self.consumed + self.nslots - (1 if hold_prev else 0)
        while self.issued < len(self.plan) and self.issued < lim:
            i = self.issued
            key = self.plan[i]
            u, nel, _, _ = self.uniq[key]
            slot, b = self.slots[i % self.nslots]
            kb.dma("sp", slot[:, 0:nel], self.wbf[u][:, 0:nel], f"ws{i % self.nslots}", reads=[self.wbuf[key]], writes=[b])
            self.issued += 1
        slot, b = self.slots[self.consumed % self.nslots]
        self.consumed += 1
        return slot, b

    @staticmethod
    def wsrc(w2d, c0, n):
        return w2d.rearrange("(kc p) n -> p kc n", p=128)[:, :, c0:c0 + n]

    @staticmethod
    def wdst(off, kc, n):
        return lambda slot: slot[:, off:off + kc * n].rearrange("p (kc n) -> p kc n", n=n)

    def load_tile(self, s, t, ti):
        kb = self.kb
        ident = self.cst("ident")
        self.carve_reset()
        self.xin = [self.carve(f"xin{i}", [128, D], F32) for i in range(2)]
        for sub in range(NSUB):
            xin, bx = self.xin[sub % 2]
            r0 = t * T + sub * 128
            kb.dma("pool", xin, self.x[s, r0:r0 + 128, :], f"xin{sub % 2}", writes=[bx])
            for q in range(4):
                bank, bb = self.next_bank()
                items = [(bank[:, j * 128:(j + 1) * 128], xin[:, (4 * q + j) * 128:(4 * q + j + 1) * 128]) for j in range(4)]
                self.tr_multi(bb, items, ident, reads=[bx, self.Bcf])
                dst = self.hT[:, 4 * q:4 * q + 4, sub * 128:(sub + 1) * 128]
                src = bank.rearrange("p (j t) -> p j t", t=128)
                eng = "act" if q % 2 == 0 else "dve"
                if eng == "act":
                    kb.op("act", lambda e, d=dst, s_=src: e.copy(d, s_), reads=[bb], writes=self.Bh[4 * q:4 * q + 4])
                else:
                    kb.op("dve", lambda e, d=dst, s_=src: e.tensor_copy(d, s_), reads=[bb], writes=self.Bh[4 * q:4 * q + 4])

    def store_tile(self, s, t, ti):
        kb = self.kb
        ident = self.cst("ident")
        self.carve_reset()
        self.xin = [self.carve(f"xout{i}", [128, D], F32) for i in range(2)]
        if self.final:
            self.rms_stats(self.Bh)
            o, _ = self.pco["finn"]
            for m in range(KC):
                kb.op("dve", lambda e, m=m: e.scalar_tensor_tensor(self.hT[:, m, :], self.hT[:, m, :], self.pc[:, o + m:o + m + 1], self.rstd, op0=ALU.mult, op1=ALU.mult),
                      reads=[self.Bh[m], self.Brstd, self.Bpc], writes=[self.Bh[m]])
        for sub in range(NSUB):
            xo, bx = self.xin[sub % 2]
            for q in range(4):
                bank, bb = self.next_bank()
                items = [(bank[:, j * 128:(j + 1) * 128], self.hT[:, 4 * q + j, sub * 128:(sub + 1) * 128]) for j in range(4)]
                self.tr_multi(bb, items, ident, reads=self.Bh[4 * q:4 * q + 4] + [self.Bcf])
                if q % 2 == 0:
                    kb.op("act", lambda e, q=q, bank=bank, xo=xo: e.copy(xo[:, q * 512:(q + 1) * 512], bank), reads=[bb], writes=[bx])
                else:
                    kb.op("dve", lambda e, q=q, bank=bank, xo=xo: e.tensor_copy(xo[:, q * 512:(q + 1) * 512], bank), reads=[bb], writes=[bx])
            r0 = t * T + sub * 128
            self._dma_toks.append(kb.dma("pool", self.out[s, r0:r0 + 128, :], xo, f"xout{sub % 2}", reads=[bx]))

    def rms_stats(self, hbufs):
        kb = self.kb
        ones = self.cst("ones", bf=True)
        bank, bb = self.next_bank()
        for m in range(KC):
            sqt, bsq = self.sqring[m % len(self.sqring)]
            if m % 3 == 0:
                kb.op("act", lambda e, m=m, sqt=sqt: e.activation(sqt, self.hT[:, m, :], AF.Square), reads=[hbufs[m]], writes=[bsq])
            else:
                kb.op("pool" if m % 3 == 1 else "dve", lambda e, m=m, sqt=sqt: e.tensor_tensor(sqt, self.hT[:, m, :], self.hT[:, m, :], ALU.mult), reads=[hbufs[m]], writes=[bsq])
            ins_first = (m == 0)
            self.kb._wait("pe", kb._deps([bsq, self.Bcb], [bb] if m == 0 else []))
            ins = self.nc.tensor.matmul(bank, ones, sqt, start=(m == 0), stop=(m == KC - 1))
            kb.nins["pe"] += 1
            tok = kb._bump("pe")
            ins.then_inc(tok[1], 1)
            bsq.r["pe"] = tok
            if m == KC - 1:
                bb.w = tok
                bb.r = {}
        kb.op("act", lambda e: e.activation(self.lnv, bank, AF.Ln, bias=EPS, scale=1.0 / D), reads=[bb], writes=[self.Blnv])
        kb.op("act", lambda e: e.activation(self.rstd, self.lnv, AF.Exp, scale=-0.5), reads=[self.Blnv], writes=[self.Brstd])

    def rmsnorm(self, wkey):
        kb = self.kb
        self.rms_stats(self.Bh)
        o, _ = self.pco[wkey]
        for m in range(KC):
            kb.op("dve", lambda e, m=m: e.scalar_tensor_tensor(self.uT[:, m, :], self.hT[:, m, :], self.pc[:, o + m:o + m + 1], self.rstd, op0=ALU.mult, op1=ALU.mult),
                  reads=[self.Bh[m], self.Brstd, self.Bpc], writes=[self.Bu])

    def prep_params(self):
        kb = self.kb
        self.lb = []
        l0 = self.pcol("lbl0")
        l1 = self.pcol("lbl1")
        d01, bd01 = self.T_("d01", [128, 16], F32)
        p0, bp0 = self.T_("p0", [128, 16], F32)
        p1, bp1 = self.T_("p1", [128, 16], F32)
        kb.op("dve", lambda e: e.tensor_tensor(d01, l0, l1, ALU.subtract), reads=[self.Bpc], writes=[bd01])
        kb.op("act", lambda e: e.activation(p0, d01, AF.Sigmoid), reads=[bd01], writes=[bp0])
        kb.op("act", lambda e: e.activation(p1, d01, AF.Sigmoid, scale=-1.0), reads=[bd01], writes=[bp1])
        for j in range(2):
            lb, blb = self.T_(f"lb{j}", [128, 16], F32)
            oml, boml = self.T_(f"oml{j}", [128, 16], F32)
            noml, bnoml = self.T_(f"noml{j}", [128, 16], F32)
            if j == 0:
                kb.op("dve", lambda e, lb=lb: e.tensor_tensor(lb, p0, p0, ALU.subtract), reads=[bp0], writes=[blb])
            else:
                kb.op("dve", lambda e, lb=lb: e.tensor_tensor(lb, p0, p1, ALU.add), reads=[bp0, bp1], writes=[blb])
                kb.op("dve", lambda e, lb=lb: e.tensor_tensor(lb, lb, p0, ALU.subtract), reads=[bp0, blb], writes=[blb])
            kb.op("dve", lambda e, lb=lb, noml=noml: e.tensor_scalar(noml, lb, 1.0, None, op0=ALU.subtract), reads=[blb], writes=[bnoml])
            kb.op("dve", lambda e, oml=oml, noml=noml: e.tensor_scalar(oml, noml, -1.0, None, op0=ALU.mult), reads=[bnoml], writes=[boml])
            self.lb.append((lb, blb, oml, boml, noml, bnoml))
        self.Arow = []
        for j in range(2):
            a, ba = self.T_(f"Arow{j}", [128, 64], F32)
            kb.op("act", lambda e, a=a, j=j: e.activation(a, self.pcol(f"alog{j}"), AF.Exp), reads=[self.Bpc], writes=[ba])
            kb.op("dve", lambda e, a=a: e.tensor_scalar(a, a, -1.0, None, op0=ALU.mult), reads=[ba], writes=[ba])
            self.Arow.append((a, ba))

    def reset_state(self):
        kb = self.kb
        for (t, b) in self.state_bufs:
            kb.op("dve", lambda e, t=t: e.memset(t, 0.0), writes=[b])

    def alloc_ffn(self):
        self.carve_reset()
        self.xs = [self.carve(f"xs{i}", [128, T + 4], F32) for i in range(3)]
        self.yc = [self.carve(f"yc{i}", [128, T], F32) for i in range(4)]
        self.gs = [self.carve(f"gs{i}", [128, T], F32) for i in range(2)]
        self.aT = [self.carve(f"aT{i}", [128, 4, T], BF16) for i in range(2)]

    def plan_ffn(self, l):
        wu = self.W["f_w_up"][l]
        wd = self.W["f_w_down"][l]

        def fd(grp):
            self.padd(("fd", l, grp), 8192, [(lambda slot: slot.rearrange("p (c n) -> p c n", n=D),
                                              wd[512 * grp:512 * grp + 512, :].rearrange("(c p) n -> p c n", p=128))])
        for grp in range(11):
            for s2 in range(2):
                s = 2 * grp + s2
                self.padd(("fu", l, s), 8192, [(self.wdst(0, 16, 256), self.wsrc(wu, 256 * s, 256)),
                                               (self.wdst(4096, 16, 256), self.wsrc(wu, DFF + 256 * s, 256))])
            if grp >= 1:
                fd(grp - 1)
        fd(10)

    def conv_chunk(self, bank, bb, tail, btail, ci, wkeys, bkey, ntap, ring_i):
        kb = self.kb
        nt = ntap - 1
        xs, bxs = self.xs[ring_i % len(self.xs)]
        yc, byc = self.yc[ring_i % len(self.yc)]
        kb.op("act", lambda e: e.copy(xs[:, nt:nt + T], bank), reads=[bb], writes=[bxs])
        kb.op("dve", lambda e: e.tensor_copy(xs[:, 0:nt], tail[:, ci, 0:nt]), reads=[btail], writes=[bxs])
        wl = self.pcol(wkeys[ntap - 1], ci)
        kb.op("act", lambda e: e.activation(yc, bank, AF.Identity, bias=self.pcol(bkey, ci), scale=wl), reads=[bb, self.Bpc], writes=[byc])
        for k in range(ntap - 1):
            wk = self.pcol(wkeys[k], ci)
            kb.op("dve", lambda e, k=k, wk=wk: e.scalar_tensor_tensor(yc, xs[:, k:k + T], wk, yc, op0=ALU.mult, op1=ALU.add),
                  reads=[bxs, byc, self.Bpc], writes=[byc])
        kb.op("dve", lambda e: e.tensor_copy(tail[:, ci, 0:nt], xs[:, T:T + nt]), reads=[bxs], writes=[btail])
        return yc, byc

    def emit_ffn(self, l):
        kb = self.kb
        self.alloc_ffn()
        tail, btail = self.ftail[l]
        wkeys = [f"fcw{l}_{k}" for k in range(3)]
        ringc = [0]

        def down_gen(grp):
            aT, baT = self.aT[grp % 2]
            slot, bs = self.wget(hold_prev=True)
            sv = slot.rearrange("p (c n) -> p c n", n=D)
            for m in range(KC):
                bank, bb = self.next_bank()
                self.mm(bb, [(bank, sv[:, c, m * 128:(m + 1) * 128], aT[:, c, :]) for c in range(4)], reads=[bs, baT])
                kb.op("dve", lambda e, m=m, bank=bank: e.tensor_tensor(self.hT[:, m, :], self.hT[:, m, :], bank, ALU.add), reads=[bb, self.Bh[m]], writes=[self.Bh[m]])
                if m % 4 == 3:
                    yield

        def up_slab(grp, s2, dg):
            aT, baT = self.aT[grp % 2]
            s = 2 * grp + s2
            slot, bs = self.wget()
            sv = slot.rearrange("p (h kc n) -> p h kc n", h=2, n=256)
            ys = []
            for c in range(4):
                bank, bb = self.next_bank()
                self.mm(bb, [(bank, sv[:, c // 2, kc, (c % 2) * 128:(c % 2 + 1) * 128], self.uT[:, kc, :]) for kc in range(KC)], reads=[bs, self.Bu])
                ci = (2 * s + c) if c < 2 else (44 + 2 * s + (c - 2))
                ys.append(self.conv_chunk(bank, bb, tail, btail, ci, wkeys, f"fcb{l}", 3, ringc[0]))
                ringc[0] += 1
                if dg is not None:
                    next(dg, None)
            for c in range(2):
                gsb, bgs = self.gs[c]
                (yg, byg), (yu, byu) = ys[c], ys[2 + c]
                kb.op("act", lambda e, gsb=gsb, yg=yg: e.activation(gsb, yg, AF.Silu), reads=[byg], writes=[bgs])
                kb.op("dve", lambda e, gsb=gsb, yu=yu, c=c: e.tensor_tensor(aT[:, 2 * s2 + c, :], gsb, yu, ALU.mult), reads=[bgs, byu], writes=[baT])

        for grp in range(11):
            up_slab(grp, 0, None)
            dg = down_gen(grp - 1) if grp >= 1 else None
            up_slab(grp, 1, dg)
            if dg is not None:
                for _ in dg:
                    pass
        for _ in down_gen(10):
            pass

    def alloc_hgrn(self):
        self.carve_reset()
        c = self.carve
        self.hS = c("hS", [128, 16, 128], F32)
        self.h_in = []
        for i in range(4):
            self.h_in.append({"q": c(f"h_q{i}", [128, T], F32), "sg": c(f"h_sg{i}", [128, T], F32),
                              "gs": c(f"h_gs{i}", [128, T], BF16), "vT": c(f"h_vT{i}", [128, T], BF16)})
        self.h_ones = c("h_ones", [128, T], F32)
        self.h_pb = []
        for i in range(2):
            d = {}
            d["k"] = c(f"h_k{i}", [128, T], F32)
            d["bp"] = c(f"h_bp{i}", [128, T + 8], F32)
            d["d1"] = c(f"h_d1{i}", [128, T], F32)
            d["d2"] = c(f"h_d2{i}", [128, T], F32)
            d["d3"] = c(f"h_d3{i}", [128, T], F32)
            for nm in ("qm", "km", "qa", "kl"):
                d[nm] = c(f"h_{nm}{i}", [128, T], BF16)
            d["vtm"] = c(f"h_vtm{i}", [64, 8, 128], BF16)
            d["kltm"] = c(f"h_kltm{i}", [64, 8, 128], BF16)
            d["A"] = c(f"h_A{i}", [64, 8, 64], BF16)
            d["Sbf"] = c(f"h_Sbf{i}", [128, 8, 128], BF16)
            d["dd"] = c(f"h_dd{i}", [128, 8], F32)
            d["dec"] = c(f"h_dec{i}", [128, 8], F32)
            self.h_pb.append(d)
        self.h_oT = [c(f"h_oT{i}", [128, 4, T], BF16) for i in range(2)]

    @staticmethod
    def hgrn_seq():
        seq = [("A", 0), ("A", 1)]
        for p in range(8):
            if 2 * p + 2 < 16:
                seq.append(("A", 2 * p + 2))
                seq.append(("A", 2 * p + 3))
            if p % 2 == 0 and p >= 2:
                seq.append(("O", (p - 2) // 2))
        seq.append(("O", 3))
        return seq

    def plan_hgrn(self, l):
        j = l // 2
        wi = self.W["hgrn_w_in"][j]
        wo = self.W["hgrn_w_out"][j]
        for (kind, i) in self.hgrn_seq():
            if kind == "A":
                h = i
                self.padd(("hi", l, h), 8192, [(self.wdst(2048 * k, 16, 128), self.wsrc(wi, 2048 * k + 128 * h, 128)) for k in range(4)])
            elif kind == "O":
                grp = i
                self.padd(("ho", l, grp), 8192, [(lambda slot: slot.rearrange("p (c n) -> p c n", n=D),
                                                  wo[512 * grp:512 * grp + 512, :].rearrange("(c p) n -> p c n", p=128))])

    def emit_hgrn(self, l):
        kb = self.kb
        j = l // 2
        self.alloc_hgrn()
        lb, blb, oml, boml, noml, bnoml = self.lb[j]
        S, _bS = self.hS
        bSh = [Buf(f"hS{i}") for i in range(16)]
        ones_f, bones = self.h_ones
        identb = self.cst("ident", bf=True)
        onesb = self.cst("ones", bf=True)
        U64 = self.cst("U")[0:64, 0:64].rearrange("p (o t) -> p o t", o=1).to_broadcast([64, 8, 64])
        gn = self.pcol(f"gn{j}")

        if self.first_tile:
            kb.op("dve", lambda e: e.memset(S, 0.0), writes=bSh)
        else:
            kb.dma("pool", S.rearrange("p a b -> p (a b)"), self.st_d[l], f"stin{l}", writes=bSh)
        kb.op("dve", lambda e: e.memset(ones_f, 1.0), writes=[bones])
        for i in range(2):
            bp_, bbp_ = self.h_pb[i]["bp"]
            kb.op("dve", lambda e, bp_=bp_: e.memset(bp_[:, 0:1], 0.0), writes=[bbp_])

        v3 = lambda ap: ap.rearrange("p (c t) -> p c t", t=64)
        bc = lambda ap: ap.to_broadcast([128, 8, 64])
        pbanks = {}

        def gen_A(h):
            slot, bs = self.wget()
            sv = slot.rearrange("p (c kc n) -> p c kc n", c=4, n=128)
            pb = []
            for c in range(4):
                bank, bb = self.next_bank("P")
                self.mm(bb, [(bank, sv[:, c, kc, :], self.uT[:, kc, :]) for kc in range(KC)], reads=[bs, self.Bu])
                pb.append((bank, bb))
                if c < 3:
                    yield
            pbanks[h] = pb
            stage_B1(h)
            yield

        def stage_B1(h):
            (q_ps, bq_ps), (f_ps, bf_ps), (v_ps, bv_ps), (g_ps, bg_ps) = pbanks.pop(h)
            I = self.h_in[h % 4]
            q, bq = I["q"]; sg, bsg = I["sg"]; gs, bgs = I["gs"]; vT, bvT = I["vT"]
            kb.op("act", lambda e: e.activation(q, q_ps, AF.Silu), reads=[bq_ps], writes=[bq])
            kb.op("act", lambda e: e.activation(gs, g_ps, AF.Silu), reads=[bg_ps], writes=[bgs])
            kb.op("act", lambda e: e.activation(sg, f_ps, AF.Sigmoid), reads=[bf_ps], writes=[bsg])
            kb.op("act", lambda e: e.copy(vT, v_ps), reads=[bv_ps], writes=[bvT])

        def stage_B2(h):
            par = h % 2
            hh = h % 4
            oT, boT = self.h_oT[(h // 4) % 2]
            I = self.h_in[h % 4]
            q, bq = I["q"]; sg, bsg = I["sg"]; gs, bgs = I["gs"]; vT, bvT = I["vT"]
            Pb = self.h_pb[par]
            k_, bk = Pb["k"]; bp, bbp = Pb["bp"]; d1, bd1 = Pb["d1"]; d2, bd2 = Pb["d2"]; d3, bd3 = Pb["d3"]
            qm, bqm = Pb["qm"]; km, bkm = Pb["km"]; qa, bqa = Pb["qa"]; kl, bkl = Pb["kl"]
            vtm, bvtm = Pb["vtm"]; kltm, bkltm = Pb["kltm"]; A, bA = Pb["A"]; Sbf, bSbf = Pb["Sbf"]
            dd, bdd = Pb["dd"]; dec, bdec = Pb["dec"]
            bS = bSh[h]
            lf, blf = d3, bd3
            sbk = [4 + 2 * par, 5 + 2 * par]
            nb = [0]

            def nbank():
                i = sbk[nb[0] % 2]
                nb[0] += 1
                return self.banks[i], self.bbufs[i]
            b3 = bp[:, 1:T + 1].rearrange("p (c t) -> p c t", t=64)
            bst = bp[:, 0:T].rearrange("p (c t) -> p c t", t=64)[:, :, 0:1]
            bmid = b3[:, :, 31:32]
            blast = b3[:, :, 63:64]
            kb.op("act", lambda e: e.activation(lf, sg, AF.Ln, bias=lb[:, h:h + 1], scale=oml[:, h:h + 1]), reads=[bsg, blb, boml], writes=[blf])
            yield
            kb.op("dve", lambda e: e.tensor_scalar(k_, sg, noml[:, h:h + 1], oml[:, h:h + 1], op0=ALU.mult, op1=ALU.add), reads=[bsg, bnoml, boml], writes=[bk])
            kb.op("dve", lambda e: e.tensor_tensor_scan(bp[:, 1:T + 1], ones_f, lf, 0.0, ALU.mult, ALU.add), reads=[bones, blf], writes=[bbp])
            kb.op("dve", lambda e: e.tensor_tensor(v3(d1), b3, bc(bmid), ALU.subtract), reads=[bbp], writes=[bd1])
            yield
            kb.op("act", lambda e: e.activation(d2, d1, AF.Exp), reads=[bd1], writes=[bd2])
            kb.op("act", lambda e: e.activation(d3, d1, AF.Exp, scale=-1.0), reads=[bd1], writes=[bd3])
            yield
            kb.op("dve", lambda e: e.tensor_tensor(qm, q, d2, ALU.mult), reads=[bq, bd2], writes=[bqm])
            kb.op("dve", lambda e: e.tensor_tensor(km, k_, d3, ALU.mult), reads=[bk, bd3], writes=[bkm])
            kb.op("dve", lambda e: e.tensor_tensor(v3(d1), b3, bc(bst), ALU.subtract), reads=[bbp], writes=[bd1])
            yield
            kb.op("act", lambda e: e.activation(d2, d1, AF.Exp), reads=[bd1], writes=[bd2])
            yield
            kb.op("dve", lambda e: e.tensor_tensor(qa, q, d2, ALU.mult), reads=[bq, bd2], writes=[bqa])
            kb.op("dve", lambda e: e.tensor_tensor(v3(d1), b3, bc(blast), ALU.subtract), reads=[bbp], writes=[bd1])
            kb.op("dve", lambda e: e.tensor_tensor(dd.rearrange("p (c o) -> p c o", o=1), blast, bst, ALU.subtract), reads=[bbp], writes=[bdd])
            yield
            kb.op("act", lambda e: e.activation(d3, d1, AF.Exp, scale=-1.0), reads=[bd1], writes=[bd3])
            kb.op("act", lambda e: e.activation(dec, dd, AF.Exp), reads=[bdd], writes=[bdec])
            yield
            kb.op("dve", lambda e: e.tensor_tensor(kl, k_, d3, ALU.mult), reads=[bk, bd3], writes=[bkl])
            yield
            bank, bb = nbank()
            bkb = bank.bitcast(BF16)
            self.tr_multi(bb, [(bkb[0:64, c * 128:(c + 1) * 128], vT[:, c * 64:(c + 1) * 64]) for c in range(8)], identb, reads=[bvT, self.Bcb])
            bank2, bb2 = nbank()
            bkb2 = bank2.bitcast(BF16)
            self.tr_multi(bb2, [(bkb2[0:64, c * 128:(c + 1) * 128], kl[:, c * 64:(c + 1) * 64]) for c in range(8)], identb, reads=[bkl, self.Bcb])
            yield
            kb.op("act", lambda e: e.copy(vtm.rearrange("p c v -> p (c v)"), bkb[0:64, :]), reads=[bb], writes=[bvtm])
            kb.op("dve", lambda e: e.tensor_copy(kltm.rearrange("p c v -> p (c v)"), bkb2[0:64, :]), reads=[bb2], writes=[bkltm])
            yield
            bankA, bbA = nbank()
            self.mm_multi(bbA, [[(bankA[0:64, c * 64:(c + 1) * 64], km[:, c * 64:(c + 1) * 64], qm[:, c * 64:(c + 1) * 64])] for c in range(8)], reads=[bkm, bqm])
            dS = []
            bankd, bbd = nbank()
            self.mm_multi(bbd, [[(bankd[:, cc * 128:(cc + 1) * 128], kltm[:, cc, :], vtm[:, cc, :])] for cc in range(4)], reads=[bkltm, bvtm])
            dS.append((bankd, bbd))
            yield
            kb.op("dve", lambda e: e.tensor_tensor(A, bankA[0:64, :].rearrange("p (c t) -> p c t", t=64), U64, ALU.mult), reads=[bbA, self.Bcf], writes=[bA])
            yield
            bankd, bbd = nbank()
            self.mm_multi(bbd, [[(bankd[:, cc * 128:(cc + 1) * 128], kltm[:, 4 + cc, :], vtm[:, 4 + cc, :])] for cc in range(4)], reads=[bkltm, bvtm])
            dS.append((bankd, bbd))
            yield
            for c in range(8):
                bank, bb = dS[c // 4]
                kb.op("act", lambda e, c=c: e.copy(Sbf[:, c, :], S[:, h, :]), reads=[bS], writes=[bSbf])
                kb.op("dve", lambda e, c=c, bank=bank: e.scalar_tensor_tensor(S[:, h, :], S[:, h, :], dec[:, c:c + 1], bank[:, (c % 4) * 128:(c % 4 + 1) * 128], op0=ALU.mult, op1=ALU.add),
                      reads=[bS, bdec, bb], writes=[bS])
                yield
            o_ps, bo_ps = nbank()
            self.mm_multi(bo_ps, [[(o_ps[:, c * 64:(c + 1) * 64], vtm[:, c, :], A[:, c, :]),
                                   (o_ps[:, c * 64:(c + 1) * 64], Sbf[:, c, :], qa[:, c * 64:(c + 1) * 64])] for c in range(8)],
                          reads=[bvtm, bA, bSbf, bqa])
            yield
            sqt, bsq = self.sqring[par]
            kb.op("act", lambda e: e.activation(sqt, o_ps, AF.Square), reads=[bo_ps], writes=[bsq])
            yield
            bankn, bbn = nbank()
            self.mm(bbn, [(bankn, onesb, sqt)], reads=[bsq, self.Bcb])
            yield
            kb.op("act", lambda e: e.activation(d1, bankn, AF.Ln, bias=EPS, scale=1.0 / 128), reads=[bbn], writes=[bd1])
            kb.op("act", lambda e: e.activation(d2, d1, AF.Exp, scale=-0.5), reads=[bd1], writes=[bd2])
            yield
            kb.op("dve", lambda e: e.scalar_tensor_tensor(d3, o_ps, gn[:, 0:1], d2, op0=ALU.mult, op1=ALU.mult), reads=[bo_ps, bd2, self.Bpc], writes=[bd3])
            kb.op("dve", lambda e: e.tensor_tensor(oT[:, hh, :], d3, gs, ALU.mult), reads=[bd3, bgs], writes=[boT])
            yield

        def gen_O(grp):
            oT, boT = self.h_oT[grp % 2]
            slot, bs = self.wget()
            sv = slot.rearrange("p (c n) -> p c n", n=D)
            for m in range(KC):
                bank, bb = self.next_bank("P")
                self.mm(bb, [(bank, sv[:, c, m * 128:(m + 1) * 128], oT[:, c, :]) for c in range(4)], reads=[bs, boT])
                kb.op("dve", lambda e, m=m, bank=bank: e.tensor_tensor(self.hT[:, m, :], self.hT[:, m, :], bank, ALU.add), reads=[bb, self.Bh[m]], writes=[self.Bh[m]])
                if m % 4 == 3:
                    yield

        def chain(gens):
            for g_ in gens:
                yield from g_

        def drain(gen):
            for _ in gen:
                pass

        acc = [0.0]

        def rr2(mains, bg, rate=1.0):
            mains = list(mains)
            while mains:
                for gen in list(mains):
                    try:
                        next(gen)
                    except StopIteration:
                        mains.remove(gen)
                acc[0] += rate
                while acc[0] >= 1.0:
                    acc[0] -= 1.0
                    try:
                        next(bg)
                    except StopIteration:
                        pass

        drain(gen_A(0))
        drain(gen_A(1))
        for p in range(8):
            items = []
            kinds = []
            if 2 * p + 2 < 16:
                items += [gen_A(2 * p + 2), gen_A(2 * p + 3)]
                kinds += ["A", "A"]
            if p % 2 == 0 and p >= 2:
                items.append(gen_O((p - 2) // 2))
                kinds.append("O")
            npieces = sum(5 if it_[0] == "A" else 4 for it_ in kinds)
            bg = chain(items)
            rr2([stage_B2(2 * p), stage_B2(2 * p + 1)], bg, rate=npieces / 25.0)
            drain(bg)
        drain(gen_O(3))
        self._dma_toks.append(kb.dma("pool", self.st_d[l], S.rearrange("p a b -> p (a b)"), f"stout{l}", reads=bSh))

    def alloc_mamba(self):
        self.carve_reset()
        c = self.carve
        self.m_h2 = [c(f"m_h{i}", [128, T], F32) for i in range(2)]
        self.m_dt = c("m_dt", [128, 4, 64], F32)
        self.m_e = c("m_e", [128, 4, 64], F32)
        self.m_dtA = c("m_dtA", [128, 4, 64], F32)
        self.m_acs = c("m_acs", [128, 4, 64], F32)
        self.m_eacs = c("m_eacs", [128, 4, 64], F32)
        self.m_dst = c("m_dst", [128, 4, 64], F32)
        self.m_cdec = c("m_cdec", [128, 4, 64], F32)
        self.m_xc2 = [c(f"m_xc{i}", [128, 4, T], BF16) for i in range(2)]
        self.m_BT2 = [c(f"m_BT{i}", [128, T], BF16) for i in range(2)]
        self.m_CT2 = [c(f"m_CT{i}", [128, T], BF16) for i in range(2)]
        self.m_zs2 = [c(f"m_zs{i}", [128, 4, T], BF16) for i in range(2)]
        self.xs = [c(f"mxs{i}", [128, T + 4], F32) for i in range(2)]
        self.yc = [c(f"myc{i}", [128, T], F32) for i in range(2)]
        self.m_cb = []
        for i in range(2):
            d = {}
            d["xtm"] = c(f"m_xtm{i}", [128, T], F32)
            d["X"] = c(f"m_X{i}", [128, 8, 64], BF16)
            d["Xh"] = c(f"m_Xh{i}", [128, 8, 64], BF16)
            d["Btm"] = c(f"m_Btm{i}", [128, 128], BF16)
            d["CBm"] = c(f"m_CBm{i}", [128, 128], BF16)
            d["Lh"] = c(f"m_Lh{i}", [128, 8, 128], F32)
            d["E"] = c(f"m_E{i}", [128, 8, 128], BF16)
            d["M"] = c(f"m_M{i}", [128, 8, 128], BF16)
            d["t1"] = c(f"m_t1{i}", [128, T], F32)
            d["t2"] = c(f"m_t2{i}", [128, T], F32)
            d["ygn"] = c(f"m_ygn{i}", [128, T], BF16)
            d["ss"] = c(f"m_ss{i}", [128, 4], F32)
            self.m_cb.append(d)
        self.m_hbf = c("m_hbf", [128, T], BF16)
        self.m_yT = [c(f"m_yT{i}", [128, 4, T], BF16) for i in range(2)]

    @staticmethod
    def mamba_seq():
        seq = [("Z", 0), ("X", 0), ("BC", 0)]
        for g in range(8):
            if g < 7:
                seq += [("Z", g + 1), ("X", g + 1), ("BC", g + 1)]
            if g >= 1:
                seq.append(("O", g - 1))
        seq.append(("O", 7))
        return seq

    def plan_mamba(self, l):
        j = l // 2
        wi = self.W["m_w_in"][j]
        wo = self.W["m_w_out"][j]
        self.padd(("mdt", l), 1024, [(self.wdst(0, 16, 64), self.wsrc(wi, 10240, 64))])
        for (kind, g) in self.mamba_seq():
            if kind == "Z":
                self.padd(("mz", l, g), 8192, [(self.wdst(0, 16, 512), self.wsrc(wi, 512 * g, 512))])
            elif kind == "X":
                self.padd(("mx", l, g), 8192, [(self.wdst(0, 16, 512), self.wsrc(wi, 4096 + 512 * g, 512))])
            elif kind == "BC":
                self.padd(("mbc", l, g), 4096, [(self.wdst(0, 16, 128), self.wsrc(wi, 8192 + 128 * g, 128)),
                                                (self.wdst(2048, 16, 128), self.wsrc(wi, 9216 + 128 * g, 128))])
            elif kind == "O":
                self.padd(("mo", l, g), 8192, [(lambda slot: slot.rearrange("p (c n) -> p c n", n=D),
                                                wo[512 * g:512 * g + 512, :].rearrange("(c p) n -> p c n", p=128))])

    def emit_mamba(self, l):
        kb = self.kb
        j = l // 2
        self.alloc_mamba()
        dt, bdt = self.m_dt
        ee, bee = self.m_e
        dtA, bdtA = self.m_dtA
        acs, bacs = self.m_acs
        eacs, beacs = self.m_eacs
        dst, bdst = self.m_dst
        cdec, bcdec = self.m_cdec
        hbf, bhbf = self.m_hbf
        tail, btail = self.mtail[j]
        identb = self.cst("ident", bf=True)
        Uf = self.cst("U")
        onesf = self.cst("ones")
        MS = self.cst("MS")
        Arow, bArow = self.Arow[j]
        dtb = self.pcol(f"dtb{j}")
        dsk = self.pcol(f"dsk{j}")
        wkeys = [f"mcw{j}_{k}" for k in range(4)]

        first_tile = self.first_tile

        slot, bs = self.wget()
        sv = slot[:, 0:16 * 64].rearrange("p (kc n) -> p kc n", n=64)
        bank, bb = self.next_bank()
        self.mm_multi(bb, [[(bank[:, sub * 64:(sub + 1) * 64], self.uT[:, kc, sub * 128:(sub + 1) * 128], sv[:, kc, :]) for kc in range(KC)] for sub in range(4)], reads=[bs, self.Bu])
        b4 = lambda ap: ap.rearrange("p (s h) -> p s h", h=64)
        kb.op("dve", lambda e, bank=bank: e.tensor_tensor(dt, b4(bank[:, 0:256]), dtb.rearrange("p (o h) -> p o h", o=1).to_broadcast([128, 4, 64]), ALU.add), reads=[bb, self.Bpc], writes=[bdt])
        kb.op("act", lambda e: e.activation(ee, dt, AF.Exp), reads=[bdt], writes=[bee])
        kb.op("act", lambda e: e.activation(dt, ee, AF.Ln, bias=1.0), reads=[bee], writes=[bdt])
        kb.op("dve", lambda e: e.tensor_tensor(dtA, dt, Arow.rearrange("p (o h) -> p o h", o=1).to_broadcast([128, 4, 64]), ALU.mult), reads=[bdt, bArow], writes=[bdtA])
        bank, bb = self.next_bank()
        self.mm_multi(bb, [[(bank[:, sub * 64:(sub + 1) * 64], Uf, dtA[:, sub, :])] for sub in range(4)], reads=[bdtA, self.Bcf])
        kb.op("act", lambda e, bank=bank: e.copy(acs, b4(bank[:, 0:256])), reads=[bb], writes=[bacs])
        kb.op("act", lambda e, bank=bank: e.activation(eacs, b4(bank[:, 0:256]), AF.Exp), reads=[bb], writes=[beacs])
        bank, bb = self.next_bank()
        self.mm_multi(bb, [[(bank[:, sub * 64:(sub + 1) * 64], onesf, dtA[:, sub, :])] for sub in range(4)], reads=[bdtA, self.Bcf])
        kb.op("dve", lambda e, bank=bank: e.tensor_tensor(dst, b4(bank[:, 0:256]), acs, ALU.subtract), reads=[bb, bacs], writes=[bdst])
        kb.op("act", lambda e: e.activation(dst, dst, AF.Exp), reads=[bdst], writes=[bdst])
        kb.op("act", lambda e, bank=bank: e.activation(cdec, b4(bank[:, 0:256]), AF.Exp), reads=[bb], writes=[bcdec])

        ringc = [0]
        hb8 = lambda ap, g: ap[:, 8 * g:8 * g + 8].rearrange("p (h o) -> p h o", o=1).to_broadcast([128, 8, 64])

        def gen_Z(g):
            zs, bzs = self.m_zs2[g % 2]
            hS, bhS = self.m_h2[g % 2]
            if first_tile:
                kb.op("dve", lambda e: e.memset(hS, 0.0), writes=[bhS])
            else:
                kb.dma("pool", hS, self.st_d[l][:, 512 * g:512 * g + 512], f"stin{l}_{g % 2}", writes=[bhS])
            slot, bs = self.wget()
            sv = slot.rearrange("p (kc n) -> p kc n", n=512)
            for sub in range(4):
                bank, bb = self.next_bank("P")
                self.mm(bb, [(bank, self.uT[:, kc, sub * 128:(sub + 1) * 128], sv[:, kc, :]) for kc in range(KC)], reads=[bs, self.Bu])
                kb.op("act", lambda e, bank=bank, sub=sub: e.activation(zs[:, sub, :], bank, AF.Silu), reads=[bb], writes=[bzs])
                yield

        def gen_X(g):
            xc, bxc = self.m_xc2[g % 2]
            ring = ringc[0]
            slot, bs = self.wget()
            sv = slot.rearrange("p (kc n) -> p kc n", n=512)
            for c in range(4):
                ring = ringc[0]
                bank, bb = self.next_bank("P")
                self.mm(bb, [(bank, sv[:, kc, c * 128:(c + 1) * 128], self.uT[:, kc, :]) for kc in range(KC)], reads=[bs, self.Bu])
                y_, by_ = self.conv_chunk(bank, bb, tail, btail, 4 * g + c, wkeys, f"mcb{j}", 4, ring)
                ring += 1
                kb.op("act", lambda e, y_=y_, c=c: e.activation(xc[:, c, :], y_, AF.Silu), reads=[by_], writes=[bxc])
                ringc[0] = ring
                yield
            ringc[0] = ring

        def gen_BC(g):
            BT, bBT = self.m_BT2[g % 2]
            CT, bCT = self.m_CT2[g % 2]
            ring = ringc[0]
            slot, bs = self.wget()
            sv = slot[:, 0:4096].rearrange("p (c kc n) -> p c kc n", c=2, n=128)
            for c, (dstT, bdstT, ci) in enumerate([(BT, bBT, 32 + g), (CT, bCT, 40 + g)]):
                ring = ringc[0]
                bank, bb = self.next_bank("P")
                self.mm(bb, [(bank, sv[:, c, kc, :], self.uT[:, kc, :]) for kc in range(KC)], reads=[bs, self.Bu])
                y_, by_ = self.conv_chunk(bank, bb, tail, btail, ci, wkeys, f"mcb{j}", 4, ring)
                ring += 1
                kb.op("act", lambda e, y_=y_, dstT=dstT: e.activation(dstT, y_, AF.Silu), reads=[by_], writes=[bdstT])
                ringc[0] = ring
                yield
            ringc[0] = ring

        def c_front(g, s_, B):
            xc, bxc = self.m_xc2[g % 2]
            BT, bBT = self.m_BT2[g % 2]
            CT, bCT = self.m_CT2[g % 2]
            xtm, bxtm = B["xtm"]; X, bX = B["X"]; Xh, bXh = B["Xh"]; Btm, bBtm = B["Btm"]
            CBm, bCBm = B["CBm"]; Lh, bLh = B["Lh"]; E, bE = B["E"]; M, bM = B["M"]
            tc = slice(128 * s_, 128 * s_ + 128)
            bank, bb = self.next_bank("S")
            bkb = bank.bitcast(BF16)
            self.tr_multi(bb, [(bkb[:, c * 128:(c + 1) * 128], xc[:, c, tc]) for c in range(4)] + [(bkb[:, 512:640], BT[:, tc])], identb, reads=[bxc, bBT, self.Bcb])
            yield
            kb.op("act", lambda e: e.copy(xtm, bkb[:, 0:512]), reads=[bb], writes=[bxtm])
            kb.op("act", lambda e: e.copy(Btm, bkb[:, 512:640]), reads=[bb], writes=[bBtm])
            yield
            x3 = xtm.rearrange("p (h q) -> p h q", q=64)
            kb.op("dve", lambda e: e.tensor_tensor(X, x3, hb8(dt[:, s_, :], g), ALU.mult), reads=[bxtm, bdt], writes=[bX])
            kb.op("dve", lambda e: e.tensor_tensor(Xh, X, hb8(dst[:, s_, :], g), ALU.mult), reads=[bX, bdst], writes=[bXh])
            kb.op("dve", lambda e: e.tensor_tensor(Lh, MS.rearrange("p (o t) -> p o t", o=1).to_broadcast([128, 8, 128]),
                                                     dtA[:, s_, 8 * g:8 * g + 8].rearrange("p (h o) -> p h o", o=1).to_broadcast([128, 8, 128]), ALU.mult),
                  reads=[bdtA, self.Bcf], writes=[bLh])
            yield
            bankc, bbc = self.next_bank("S")
            self.mm(bbc, [(bankc[:, 0:128], BT[:, tc], CT[:, tc])], reads=[bBT, bCT])
            yield
            kb.op("dve", lambda e: e.tensor_tensor(CBm, bankc[:, 0:128], Uf, ALU.mult), reads=[bbc, self.Bcf], writes=[bCBm])
            yield
            dbanks = []
            for half in range(2):
                bank2, bb2 = self.next_bank("S")
                self.mm_multi(bb2, [[(bank2[:, q * 128:(q + 1) * 128], Lh[:, 4 * half + q, :], Uf)] for q in range(4)], reads=[bLh, self.Bcf])
                dbanks.append((bank2, bb2))
            yield
            for half, (bank2, bb2) in enumerate(dbanks):
                kb.op("act", lambda e, bank2=bank2, half=half: e.activation(E[:, 4 * half:4 * half + 4, :], bank2.rearrange("p (h t) -> p h t", t=128), AF.Exp), reads=[bb2], writes=[bE])
            yield
            kb.op("dve", lambda e: e.tensor_tensor(M, E, CBm.rearrange("p (o t) -> p o t", o=1).to_broadcast([128, 8, 128]), ALU.mult), reads=[bE, bCBm], writes=[bM])
            yield

        def c_mid(g, s_, B):
            CT, bCT = self.m_CT2[g % 2]
            hS, bhS = self.m_h2[g % 2]
            X, bX = B["X"]; Xh, bXh = B["Xh"]; Btm, bBtm = B["Btm"]; M, bM = B["M"]; t1, bt1 = B["t1"]
            tc = slice(128 * s_, 128 * s_ + 128)
            bankd, bbd = self.next_bank("S")
            self.mm_multi(bbd, [[(bankd[:, q * 64:(q + 1) * 64], M[:, q, :], X[:, q, :])] for q in range(8)], reads=[bM, bX])
            banko, bbo = self.next_bank("S")
            self.mm(bbo, [(banko, CT[:, tc], hbf)], reads=[bCT, bhbf])
            yield
            t13 = t1.rearrange("p (h q) -> p h q", q=64)
            kb.op("dve", lambda e: e.tensor_tensor(t13, banko.rearrange("p (h q) -> p h q", q=64), hb8(eacs[:, s_, :], g), ALU.mult), reads=[bbo, beacs], writes=[bt1])
            kb.op("dve", lambda e: e.tensor_tensor(t1, t1, bankd, ALU.add), reads=[bt1, bbd], writes=[bt1])
            yield
            banks_, bbs = self.next_bank("S")
            self.mm(bbs, [(banks_, Btm, Xh.rearrange("p h q -> p (h q)"))], reads=[bBtm, bXh])
            yield
            h3 = hS.rearrange("p (h q) -> p h q", q=64)
            kb.op("dve", lambda e: e.tensor_tensor(h3, h3, hb8(cdec[:, s_, :], g), ALU.mult), reads=[bhS, bcdec], writes=[bhS])
            kb.op("dve", lambda e: e.tensor_tensor(hS, hS, banks_, ALU.add), reads=[bhS, bbs], writes=[bhS])
            if s_ < 3:
                kb.op("act", lambda e: e.copy(hbf, hS), reads=[bhS], writes=[bhbf])
            yield

        def c_tail(g, s_, B):
            yT, byT = self.m_yT[g % 2]
            zs, bzs = self.m_zs2[g % 2]
            xtm, bxtm = B["xtm"]; t1, bt1 = B["t1"]; t2, bt2 = B["t2"]; ygn, bygn = B["ygn"]; ss, bss = B["ss"]
            tc = slice(128 * s_, 128 * s_ + 128)
            x3 = xtm.rearrange("p (h q) -> p h q", q=64)
            t23 = t2.rearrange("p (h q) -> p h q", q=64)
            kb.op("dve", lambda e: e.tensor_tensor(t23, x3, hb8(dsk, g), ALU.mult), reads=[bxtm, self.Bpc], writes=[bt2])
            kb.op("dve", lambda e: e.tensor_tensor(t1, t1, t2, ALU.add), reads=[bt1, bt2], writes=[bt1])
            kb.op("dve", lambda e: e.tensor_tensor(t1, t1, zs[:, s_, :], ALU.mult), reads=[bt1, bzs], writes=[bt1])
            yield
            kb.op("act", lambda e: e.activation(t2, t1, AF.Square, accum_out=ss[:, 0:1]), reads=[bt1], writes=[bt2, bss])
            kb.op("act", lambda e: e.activation(ss[:, 1:2], ss[:, 0:1], AF.Ln, bias=EPS, scale=1.0 / 512), reads=[bss], writes=[bss])
            kb.op("act", lambda e: e.activation(ss[:, 2:3], ss[:, 1:2], AF.Exp, scale=-0.5), reads=[bss], writes=[bss])
            yield
            kb.op("dve", lambda e: e.tensor_scalar(ygn, t1, ss[:, 2:3], None, op0=ALU.mult), reads=[bt1, bss], writes=[bygn])
            yield
            bank, bb = self.next_bank("S")
            bkb = bank.bitcast(BF16)
            self.tr_multi(bb, [(bkb[:, c * 128:(c + 1) * 128], ygn[:, c * 128:(c + 1) * 128]) for c in range(4)], identb, reads=[bygn, self.Bcb])
            yield
            o_, _n = self.pco[f"mnw{j}"]
            kb.op("dve", lambda e: e.tensor_tensor(yT[:, :, tc], bkb[:, 0:512].rearrange("p (c t) -> p c t", t=128),
                                                     self.pc[:, o_ + 4 * g:o_ + 4 * g + 4].rearrange("p (c o) -> p c o", o=1).to_broadcast([128, 4, 128]), ALU.mult),
                  reads=[bb, self.Bpc], writes=[byT])
            yield

        def chain(gens):
            for g_ in gens:
                yield from g_

        def drain(gen):
            for _ in gen:
                pass

        acc = [0.0]

        def rr2(mains, bg, rate=1.0):
            mains = list(mains)
            while mains:
                for gen in list(mains):
                    try:
                        next(gen)
                    except StopIteration:
                        mains.remove(gen)
                acc[0] += rate
                while acc[0] >= 1.0:
                    acc[0] -= 1.0
                    try:
                        next(bg)
                    except StopIteration:
                        pass

        def do_pair(g, pair, bg, rate):
            hS, bhS = self.m_h2[g % 2]
            sa, sb = 2 * pair, 2 * pair + 1
            Ba, Bb = self.m_cb[0], self.m_cb[1]
            if pair == 0:
                kb.op("act", lambda e: e.copy(hbf, hS), reads=[bhS], writes=[bhbf])
            rr2([c_front(g, sa, Ba), c_front(g, sb, Bb)], bg, rate)
            rr2([chain([c_mid(g, sa, Ba), c_mid(g, sb, Bb)])], bg, rate)
            rr2([c_tail(g, sa, Ba), c_tail(g, sb, Bb)], bg, rate)
            if pair == 1:
                self._dma_toks.append(kb.dma("pool", self.st_d[l][:, 512 * g:512 * g + 512], hS, f"stout{l}_{g % 2}", reads=[bhS]))

        def gen_O(g):
            yT, byT = self.m_yT[g % 2]
            slot, bs = self.wget()
            sv = slot.rearrange("p (c n) -> p c n", n=D)
            for m in range(KC):
                bank, bb = self.next_bank("P")
                self.mm(bb, [(bank, sv[:, c, m * 128:(m + 1) * 128], yT[:, c, :]) for c in range(4)], reads=[bs, byT])
                kb.op("dve", lambda e, m=m, bank=bank: e.tensor_tensor(self.hT[:, m, :], self.hT[:, m, :], bank, ALU.add), reads=[bb, self.Bh[m]], writes=[self.Bh[m]])
                if m % 4 == 3:
                    yield

        drain(gen_Z(0))
        drain(gen_X(0))
        drain(gen_BC(0))
        for g in range(8):
            items = ([gen_Z(g + 1), gen_X(g + 1), gen_BC(g + 1)] if g < 7 else []) + ([gen_O(g - 1)] if g >= 1 else [])
            npieces = (10 if g < 7 else 0) + (4 if g >= 1 else 0)
            bg = chain(items)
            rate = npieces / 40.0
            do_pair(g, 0, bg, rate)
            do_pair(g, 1, bg, rate)
            drain(bg)
        drain(gen_O(7))


_CACHE = {}


def build_nc(inputs_offs, **kw):
    pc_offs, cc_offs = inputs_offs
    p = Prog(**kw)
    nc = p.build(pc_offs, cc_offs)
    return nc


def kernel(**inputs):
    inp = {k: np.asarray(v) for k, v in inputs.items()}
    pcols, pc_offs = pack_params(inp)
    consts, cc_offs = make_consts()
    nc = build_nc((pc_offs, cc_offs))
    x = np.ascontiguousarray(inp["x"], dtype=np.float32)
    in_maps = []
    for c in range(8):
        m = {"x": x[2 * c:2 * c + 2], "pcols": pcols, "consts": consts}
        for k in ("hgrn_w_in", "hgrn_w_out", "m_w_in", "m_w_out", "f_w_up", "f_w_down"):
            m[k] = np.ascontiguousarray(inp[k], dtype=np.float32)
        in_maps.append(m)
    res = run_bass_kernel_spmd(nc, in_maps, core_ids=list(range(8)))
    return np.concatenate([np.asarray(r["out"]) for r in res.results], axis=0).astype(np.float32)
```
